# Optimizing a Trainium2 kernel written in Bass

```python
import math
import jax, jax.numpy as jnp
from jax import lax
import numpy as np

D_MODEL = 2048
BATCH = 4
SEQ = 4096
DEPTH = 1

N_HEADS = 16
N_KV_HEADS = 4
HEAD_DIM = 128
ROPE_DIM = HEAD_DIM // 4
ROPE_THETA = 500000.0
IDX_HEADS = 16
IDX_DIM = 64
IDX_ROPE_DIM = IDX_DIM // 4
TOPK_MAX = 256
Q_BLOCK = 128
D_INNER = 2 * D_MODEL
SSM_HEAD_DIM = 64
SSM_HEADS = D_INNER // SSM_HEAD_DIM
SSM_GROUPS = 8
SSM_STATE = 128
CONV_WIDTH = 4
CHUNK = 128
CONV_CH = D_INNER + 2 * SSM_GROUPS * SSM_STATE
D_FF = 5504
LN_EPS = 1e-5
RMS_EPS = 1e-5
DEEPNORM_ALPHA = (2 * DEPTH) ** 0.25
DEEPNORM_BETA = (8 * DEPTH) ** -0.25
IN_WIDTHS = (N_HEADS * HEAD_DIM, N_KV_HEADS * HEAD_DIM, N_KV_HEADS * HEAD_DIM,
             IDX_HEADS * IDX_DIM, IDX_DIM, IDX_HEADS,
             D_INNER, CONV_CH, SSM_HEADS, 2 * D_MODEL)
W_IN_TOTAL = sum(IN_WIDTHS)

kernel_name = "hybrid_dsa_mamba2_macaron_deepnorm"


def _split_points(widths):
    pts, acc = [], 0
    for w in widths[:-1]:
        acc += w
        pts.append(acc)
    return pts


def layer_norm(x, g, b):
    xf = x.astype(jnp.float32)
    mu = jnp.mean(xf, -1, keepdims=True)
    var = jnp.mean(jnp.square(xf - mu), -1, keepdims=True)
    return ((xf - mu) * lax.rsqrt(var + LN_EPS) * g.astype(jnp.float32) + b.astype(jnp.float32)).astype(x.dtype)


def swiglu(x, w_gu, w_down):
    g, u = jnp.split(x @ w_gu, 2, axis=-1)
    return (jax.nn.silu(g) * u) @ w_down


def rope_tables(S, rot_dim):
    inv = 1.0 / (ROPE_THETA ** (jnp.arange(0, rot_dim, 2, dtype=jnp.float32) / rot_dim))
    ang = jnp.arange(S, dtype=jnp.float32)[:, None] * inv[None, :]
    return jnp.cos(ang), jnp.sin(ang)


def apply_partial_rope(x, cos, sin, rot_dim):
    half = rot_dim // 2
    x1, x2, xp = x[..., :half], x[..., half:rot_dim], x[..., rot_dim:]
    c, s = cos[None, :, None, :], sin[None, :, None, :]
    rot = jnp.concatenate([x1 * c - x2 * s, x2 * c + x1 * s], -1).astype(x.dtype)
    return jnp.concatenate([rot, xp], -1)


def dsa_attention(q, k, v, qi, ki, wi):
    B, S = q.shape[0], q.shape[1]
    n_top = min(TOPK_MAX, S // 4)
    n_blocks = S // Q_BLOCK
    grp = N_HEADS // N_KV_HEADS
    key_pos = jnp.arange(S)
    bidx = jnp.arange(B)[:, None, None]

    def block(j):
        t0 = j * Q_BLOCK
        qb = lax.dynamic_slice_in_dim(q, t0, Q_BLOCK, 1)
        qib = lax.dynamic_slice_in_dim(qi, t0, Q_BLOCK, 1)
        wib = lax.dynamic_slice_in_dim(wi, t0, Q_BLOCK, 1)
        qpos = t0 + jnp.arange(Q_BLOCK)
        causal = key_pos[None, :] <= qpos[:, None]
        lg = jnp.einsum('bqhd,bsd->bqhs', qib, ki, preferred_element_type=jnp.float32) * (IDX_DIM ** -0.5)
        score = jnp.einsum('bqhs,bqh->bqs', jax.nn.relu(lg), wib.astype(jnp.float32)) * (IDX_HEADS ** -0.5)
        score = jnp.where(causal[None], score, -jnp.inf)
        _, sel = lax.top_k(score, n_top)
        ksel = k[bidx, sel]
        vsel = v[bidx, sel]
        valid = sel <= qpos[None, :, None]
        qg = qb.reshape(B, Q_BLOCK, N_KV_HEADS, grp, HEAD_DIM)
        s = jnp.einsum('bqhgd,bqkhd->bqhgk', qg, ksel, preferred_element_type=jnp.float32) * (HEAD_DIM ** -0.5)
        s = jnp.where(valid[:, :, None, None, :], s, -jnp.inf)
        p = jax.nn.softmax(s, axis=-1).astype(v.dtype)
        o = jnp.einsum('bqhgk,bqkhd->bqhgd', p, vsel)
        return o.reshape(B, Q_BLOCK, N_HEADS * HEAD_DIM)

    out = lax.map(block, jnp.arange(n_blocks))
    return out.transpose(1, 0, 2, 3).reshape(B, S, N_HEADS * HEAD_DIM)


def causal_dwconv(x, w, b):
    C = x.shape[-1]
    y = lax.conv_general_dilated(x, w[:, None, :].astype(x.dtype), window_strides=(1,),
                                 padding=[(CONV_WIDTH - 1, 0)],
                                 dimension_numbers=('NWC', 'WIO', 'NWC'),
                                 feature_group_count=C)
    return y + b.astype(x.dtype)


def ssd_scan(x, dt, A, Bm, Cm):
    Bsz, S, H, P = x.shape
    G, N, L = SSM_GROUPS, SSM_STATE, CHUNK
    Hg, nc = H // G, S // L
    xc = x.reshape(Bsz, nc, L, G, Hg, P).transpose(1, 0, 2, 3, 4, 5)
    dtc = dt.reshape(Bsz, nc, L, G, Hg).transpose(1, 0, 2, 3, 4)
    Bc = Bm.reshape(Bsz, nc, L, G, N).transpose(1, 0, 2, 3, 4)
    Cc = Cm.reshape(Bsz, nc, L, G, N).transpose(1, 0, 2, 3, 4)
    Ag = A.reshape(G, Hg)
    tri = jnp.tril(jnp.ones((L, L), dtype=bool))

    def step(state, inp):
        xk, dtk, Bk, Ck = inp
        a = jnp.cumsum(dtk * Ag, axis=1).transpose(0, 2, 3, 1)
        seg = a[..., :, None] - a[..., None, :]
        decay = jnp.exp(jnp.where(tri, seg, -jnp.inf))
        cb = jnp.einsum('blgn,bsgn->bgls', Ck, Bk)
        xdt = xk * dtk[..., None]
        y_diag = jnp.einsum('bghls,bsghp->blghp', cb[:, :, None] * decay, xdt)
        y_off = jnp.einsum('blgn,bghpn,bghl->blghp', Ck, state, jnp.exp(a))
        a_last = a[..., -1:]
        w = jnp.exp(a_last - a)
        new_state = state * jnp.exp(a_last)[..., None] + jnp.einsum('bsgn,bghs,bsghp->bghpn', Bk, w, xdt)
        return new_state, y_diag + y_off

    state0 = jnp.zeros((Bsz, G, Hg, P, N), jnp.float32)
    _, ys = lax.scan(step, state0, (xc, dtc, Bc, Cc))
    return ys.transpose(1, 0, 2, 3, 4, 5).reshape(Bsz, S, H, P)


def mamba2_branch(z, xbc, dt_raw, conv_w, conv_b, dt_bias, A_log, D_skip, norm_w):
    B, S, _ = xbc.shape
    xbc = jax.nn.silu(causal_dwconv(xbc, conv_w, conv_b))
    xs, Bm, Cm = jnp.split(xbc, [D_INNER, D_INNER + SSM_GROUPS * SSM_STATE], axis=-1)
    xs = xs.reshape(B, S, SSM_HEADS, SSM_HEAD_DIM).astype(jnp.float32)
    dt = jax.nn.softplus(dt_raw.astype(jnp.float32) + dt_bias.astype(jnp.float32))
    A = -jnp.exp(A_log.astype(jnp.float32))
    y = ssd_scan(xs, dt, A,
                 Bm.reshape(B, S, SSM_GROUPS, SSM_STATE).astype(jnp.float32),
                 Cm.reshape(B, S, SSM_GROUPS, SSM_STATE).astype(jnp.float32))
    y = y + D_skip.astype(jnp.float32)[:, None] * xs
    y = y.reshape(B, S, D_INNER) * jax.nn.silu(z.astype(jnp.float32))
    yg = y.reshape(B, S, SSM_GROUPS, D_INNER // SSM_GROUPS)
    yg = yg * lax.rsqrt(jnp.mean(jnp.square(yg), -1, keepdims=True) + RMS_EPS)
    return (yg.reshape(B, S, D_INNER) * norm_w.astype(jnp.float32)).astype(z.dtype)


def hybrid_mixer(h, w_in, conv_w, conv_b, dt_bias, A_log, D_skip, ssm_norm_w, w_o_attn, w_o_ssm, w_out):
    B, S, _ = h.shape
    proj = h @ w_in
    q, k, v, qi, ki, wi, z, xbc, dt_raw, gates = jnp.split(proj, _split_points(IN_WIDTHS), axis=-1)
    cos, sin = rope_tables(S, ROPE_DIM)
    cos_i, sin_i = rope_tables(S, IDX_ROPE_DIM)
    q = apply_partial_rope(q.reshape(B, S, N_HEADS, HEAD_DIM), cos, sin, ROPE_DIM)
    k = apply_partial_rope(k.reshape(B, S, N_KV_HEADS, HEAD_DIM), cos, sin, ROPE_DIM)
    v = v.reshape(B, S, N_KV_HEADS, HEAD_DIM)
    qi = apply_partial_rope(qi.reshape(B, S, IDX_HEADS, IDX_DIM), cos_i, sin_i, IDX_ROPE_DIM)
    ki = apply_partial_rope(ki.reshape(B, S, 1, IDX_DIM), cos_i, sin_i, IDX_ROPE_DIM)[:, :, 0]
    attn = dsa_attention(q, k, v, qi, ki, wi)
    ssm = mamba2_branch(z, xbc, dt_raw, conv_w, conv_b, dt_bias, A_log, D_skip, ssm_norm_w)
    g_attn, g_ssm = jnp.split(jax.nn.sigmoid(gates.astype(jnp.float32)), 2, axis=-1)
    merged = g_attn * (attn @ w_o_attn).astype(jnp.float32) + g_ssm * (ssm @ w_o_ssm).astype(jnp.float32)
    return merged.astype(h.dtype) @ w_out


def setup_inputs(seed: int = 0) -> dict:
    key = jax.random.key(seed)
    ks = jax.random.split(key, 24)
    f32 = jnp.float32

    def nrm(k, shape, fan_in, scale=1.0):
        return jax.random.normal(k, shape, f32) * (scale * fan_in ** -0.5)

    def gain(k, shape):
        return 1.0 + 0.02 * jax.random.normal(k, shape, f32)

    def bias(k, shape):
        return 0.02 * jax.random.normal(k, shape, f32)

    x = jax.random.normal(ks[0], (BATCH, SEQ, D_MODEL), f32)
    w_in = nrm(ks[1], (DEPTH, D_MODEL, W_IN_TOTAL), D_MODEL)
    v0 = IN_WIDTHS[0] + IN_WIDTHS[1]
    w_in = w_in.at[:, :, v0:v0 + IN_WIDTHS[2]].multiply(DEEPNORM_BETA)
    dt_init = jnp.exp(jax.random.uniform(ks[2], (DEPTH, SSM_HEADS), f32, math.log(1e-3), math.log(1e-1)))
    dt_bias = dt_init + jnp.log(-jnp.expm1(-dt_init))
    A_log = jnp.log(jax.random.uniform(ks[3], (DEPTH, SSM_HEADS), f32, 1.0, 16.0))
    return {
        "x": x,
        "ln1_g": gain(ks[4], (DEPTH, D_MODEL)),
        "ln1_b": bias(ks[5], (DEPTH, D_MODEL)),
        "ffn1_w_gu": nrm(ks[6], (DEPTH, D_MODEL, 2 * D_FF), D_MODEL),
        "ffn1_w_down": nrm(ks[7], (DEPTH, D_FF, D_MODEL), D_FF, DEEPNORM_BETA),
        "w_in": w_in,
        "conv_w": nrm(ks[8], (DEPTH, CONV_WIDTH, CONV_CH), CONV_WIDTH),
        "conv_b": bias(ks[9], (DEPTH, CONV_CH)),
        "dt_bias": dt_bias,
        "A_log": A_log,
        "D_skip": gain(ks[10], (DEPTH, SSM_HEADS)),
        "ssm_norm_w": gain(ks[11], (DEPTH, D_INNER)),
        "w_o_attn": nrm(ks[12], (DEPTH, N_HEADS * HEAD_DIM, D_MODEL), N_HEADS * HEAD_DIM, DEEPNORM_BETA),
        "w_o_ssm": nrm(ks[13], (DEPTH, D_INNER, D_MODEL), D_INNER, DEEPNORM_BETA),
        "w_out": nrm(ks[14], (DEPTH, D_MODEL, D_MODEL), D_MODEL, DEEPNORM_BETA),
        "ln2_g": gain(ks[15], (DEPTH, D_MODEL)),
        "ln2_b": bias(ks[16], (DEPTH, D_MODEL)),
        "ffn2_w_gu": nrm(ks[17], (DEPTH, D_MODEL, 2 * D_FF), D_MODEL),
        "ffn2_w_down": nrm(ks[18], (DEPTH, D_FF, D_MODEL), D_FF, DEEPNORM_BETA),
        "ln3_g": gain(ks[19], (DEPTH, D_MODEL)),
        "ln3_b": bias(ks[20], (DEPTH, D_MODEL)),
    }


def reference(x, ln1_g, ln1_b, ffn1_w_gu, ffn1_w_down, w_in, conv_w, conv_b, dt_bias, A_log, D_skip,
              ssm_norm_w, w_o_attn, w_o_ssm, w_out, ln2_g, ln2_b, ffn2_w_gu, ffn2_w_down, ln3_g, ln3_b):
    h = x
    for l in range(DEPTH):
        h = layer_norm(DEEPNORM_ALPHA * h + 0.5 * swiglu(h, ffn1_w_gu[l], ffn1_w_down[l]), ln1_g[l], ln1_b[l])
        mix = hybrid_mixer(h, w_in[l], conv_w[l], conv_b[l], dt_bias[l], A_log[l], D_skip[l], ssm_norm_w[l],
                           w_o_attn[l], w_o_ssm[l], w_out[l])
        h = layer_norm(DEEPNORM_ALPHA * h + mix, ln2_g[l], ln2_b[l])
        h = layer_norm(DEEPNORM_ALPHA * h + 0.5 * swiglu(h, ffn2_w_gu[l], ffn2_w_down[l]), ln3_g[l], ln3_b[l])
    return h
```

```python
from contextlib import ExitStack
import numpy as np
import concourse.bass as bass
import concourse.mybir as mybir
from concourse.bass_utils import run_bass_kernel_spmd

F32 = mybir.dt.float32
BF16 = mybir.dt.bfloat16
AF = mybir.ActivationFunctionType
ALU = mybir.AluOpType
AX = mybir.AxisListType

D = 2048
T = 2048
NT = 16
DFF = 5504
NF = 43
WIN = 18576
ALPHA = 2.0 ** 0.25
NEG = -1.0e30
OFF_Q, OFF_K, OFF_V, OFF_QI, OFF_KI, OFF_WI, OFF_Z, OFF_XBC, OFF_DT, OFF_G = 0, 2048, 2560, 3072, 4096, 4160, 4176, 8272, 14416, 14480
N_BISECT = 18
NCORES = [8]


class Stream:
    def __init__(self, name, sem, inc):
        self.name, self.sem, self.inc, self.n = name, sem, inc, 0


class Buf:
    __slots__ = ("w", "r")

    def __init__(self):
        self.w = None
        self.r = {}


class Sched:
    QUEUES = ("pe", "act", "dve", "pool", "sp")

    def __init__(self, nc):
        self.nc = nc
        self.q = {k: [] for k in self.QUEUES}
        self.streams = {}
        for k in ("pe", "act", "dve", "pool"):
            self.streams[k] = Stream(k, nc.alloc_semaphore("s_" + k), 1)
        self.free_dma = []
        self.ndma = 0
        self.known = {k: {} for k in self.QUEUES}

    def dma_stream(self, tile):
        if tile.ds is None:
            if self.free_dma:
                tile.ds = self.free_dma.pop()
            else:
                name = "d%d" % self.ndma
                self.ndma += 1
                tile.ds = Stream(name, self.nc.alloc_semaphore("s_" + name), 16)
                self.streams[name] = tile.ds
        return tile.ds

    def release(self, tiles):
        for t in tiles:
            if t.ds is not None:
                self.free_dma.append(t.ds)
                t.ds = None

    def cc_stream(self):
        name = "cc%d" % self.ndma
        self.ndma += 1
        st = Stream(name, self.nc.alloc_semaphore("s_" + name), 1)
        self.streams[name] = st
        return st

    def _wait(self, queue, st, n):
        if n <= 0 or self.known[queue].get(st.name, 0) >= n:
            return
        self.known[queue][st.name] = n
        val, sem = n * st.inc, st.sem
        self.q[queue].append(lambda eng: eng.wait_ge(sem, val))

    def op(self, queue, fn, reads=(), writes=(), stream=None):
        st = stream if isinstance(stream, Stream) else self.streams[stream or queue]
        deps = {}
        for b in reads:
            if b.w is not None and deps.get(b.w[0].name, (None, 0))[1] < b.w[1]:
                deps[b.w[0].name] = b.w
        for b in writes:
            if b.w is not None and deps.get(b.w[0].name, (None, 0))[1] < b.w[1]:
                deps[b.w[0].name] = b.w
            for d in b.r.values():
                if deps.get(d[0].name, (None, 0))[1] < d[1]:
                    deps[d[0].name] = d
        for s, n in deps.values():
            if queue == "pe" and s.name == "pe":
                continue
            self._wait(queue, s, n)
        st.n += 1
        me = (st, st.n)
        sem, inc = st.sem, st.inc
        self.q[queue].append(lambda eng: fn(eng).then_inc(sem, inc))
        for b in reads:
            b.r[st.name] = me
        for b in writes:
            b.w = me
            b.r = {}
        return me

    def barrier(self):
        for queue in self.QUEUES:
            for st in self.streams.values():
                if queue == "pe" and st.name == "pe":
                    continue
                self._wait(queue, st, st.n)

    def emit(self):
        with self.nc.Block() as block:
            @block.tensor
            def _(e):
                for f in self.q["pe"]:
                    f(e)

            @block.scalar
            def _(e):
                for f in self.q["act"]:
                    f(e)

            @block.vector
            def _(e):
                for f in self.q["dve"]:
                    f(e)

            @block.gpsimd
            def _(e):
                for f in self.q["pool"]:
                    f(e)

            @block.sync
            def _(e):
                for f in self.q["sp"]:
                    f(e)

    def mm(self, out, lhsT, rhs, start, stop, reads, writes):
        self.op("pe", lambda e: e.matmul(out, lhsT=lhsT, rhs=rhs, start=start, stop=stop), reads, writes)

    def tr(self, out, in_, ident, reads, writes):
        self.op("pe", lambda e: e.transpose(out, in_, ident), reads, writes)

    def act(self, out, in_, func, reads, writes, scale=None, bias=None, accum_out=None):
        kw = {}
        if scale is not None:
            kw["scale"] = scale
        if bias is not None:
            kw["bias"] = bias
        if accum_out is not None:
            kw["accum_out"] = accum_out
        self.op("act", lambda e: e.activation(out=out, in_=in_, func=func, **kw), reads, writes)

    def tt(self, q, out, in0, in1, op, reads, writes):
        self.op(q, lambda e: e.tensor_tensor(out=out, in0=in0, in1=in1, op=op), reads, writes)

    def ts(self, q, out, in0, s1, s2, op0, op1, reads, writes, accum_out=None):
        if op1 is None:
            self.op(q, lambda e: e.tensor_scalar(out=out, in0=in0, scalar1=s1, scalar2=None, op0=op0), reads, writes)
        elif accum_out is None:
            self.op(q, lambda e: e.tensor_scalar(out=out, in0=in0, scalar1=s1, scalar2=s2, op0=op0, op1=op1), reads, writes)
        else:
            self.op(q, lambda e: e.tensor_scalar(out=out, in0=in0, scalar1=s1, scalar2=s2, op0=op0, op1=op1, accum_out=accum_out), reads, writes)

    def stt(self, out, in0, scalar, in1, op0, op1, reads, writes):
        self.op("dve", lambda e: e.scalar_tensor_tensor(out=out, in0=in0, scalar=scalar, in1=in1, op0=op0, op1=op1), reads, writes)

    def copy(self, q, out, in_, reads, writes):
        if q == "act":
            self.op("act", lambda e: e.copy(out=out, in_=in_), reads, writes)
        else:
            self.op(q, lambda e: e.tensor_copy(out=out, in_=in_), reads, writes)

    def dma(self, q, tile, out, in_, reads, writes, slow=False):
        if slow:
            self.op(q, lambda e: e.dma_start(out=out, in_=in_, allow_slow_non_contiguous=True), reads, writes, self.dma_stream(tile))
        else:
            self.op(q, lambda e: e.dma_start(out=out, in_=in_), reads, writes, self.dma_stream(tile))


class Tile:
    def __init__(self, t):
        self.t = t
        self.b = Buf()
        self.ds = None

    def __getitem__(self, k):
        return self.t[k]


def build_nc(debug=()):
    nc = bass.Bass("TRN2", target_bir_lowering=False)

    def din(name, shape, dt=F32):
        return nc.dram_tensor(name, list(shape), dt, kind="ExternalInput").ap()

    x_in = din("x", [T, D])
    w_gu1, w_dn1 = din("ffn1_w_gu", [D, 2 * DFF]), din("ffn1_w_down", [DFF, D])
    w_gu2, w_dn2 = din("ffn2_w_gu", [D, 2 * DFF]), din("ffn2_w_down", [DFF, D])
    w_in = din("w_in", [D, WIN])
    w_oa, w_os, w_out = din("w_o_attn", [D, D]), din("w_o_ssm", [2 * D, D]), din("w_out", [D, D])
    lnp = din("lnp", [6, D])
    conv_wb = din("conv_wb", [5, 6144])
    convT = din("convT", [128, 48, 5])
    ssmv = din("ssmv", [4, 64])
    normw = din("normw", [1, 4096])
    ropet = din("ropet", [T, 48])
    cst = din("cst", [128, 5, 128])
    flg = din("flg", [128, 2])
    out = nc.dram_tensor("out", [T, D], F32, kind="ExternalOutput").ap()

    def dscr(name, shape, dt):
        if name in debug:
            return nc.dram_tensor(name, list(shape), dt, kind="ExternalOutput").ap()
        return nc.dram_tensor(name, list(shape), dt).ap()

    h1_s = dscr("h1_s", [T, D], F32)
    q_s = dscr("q_s", [T, D], F32)
    qi_s = dscr("qi_s", [T, 1024], F32)
    wi_s = dscr("wi_s", [T, 16], F32)
    kvg_in = dscr("kvg_in", [T, 1088], F32)
    kvg = [nc.dram_tensor("kvg%d" % i, [256, 2176], F32).ap() for i in range(8)]
    z_s = dscr("z_s", [T, 4096], BF16)
    xbc_s = dscr("xbc_s", [6, 6144], F32)
    halo_in = nc.dram_tensor("halo_in", [3, 6144], F32).ap()
    halo_g = nc.dram_tensor("halo_g", [6, 6144], F32).ap()
    dt_s = dscr("dt_s", [T, 64], F32)
    gT_s = dscr("gT_s", [4096, T], BF16)
    xc_s = dscr("xc_s", [T, 6144], F32)
    ypre_s = dscr("ypre_s", [T, 4096], F32)
    ct_s = nc.dram_tensor("ct_s", [NT * 8 * 128, 128], BF16).ap()
    st_in = nc.dram_tensor("st_in", [128, 4096], F32).ap()
    st_g = nc.dram_tensor("st_g", [256, 4096], F32).ap()
    attnT_s = dscr("attnT_s", [D, T], BF16)
    ssmT_s = dscr("ssmT_s", [4096, T], BF16)

    wscr = nc.dram_tensor("wscr", [128, 128, 16 * 512], BF16).ap()
    wscr2 = nc.dram_tensor("wscr2", [112, 128, 16 * 512], BF16).ap()

    S = Sched(nc)

    ES = [None]

    PH_TILES = []

    def sb(name, shape, dt):
        if ES[0] is None:
            return Tile(nc.alloc_sbuf_tensor(name, list(shape), dt))
        t = Tile(ES[0].enter_context(nc.sbuf_tensor(name, list(shape), dt)))
        PH_TILES.append(t)
        return t

    def end_phase():
        S.barrier()
        S.release(PH_TILES)
        del PH_TILES[:]

    cst_t = sb("cst_t", [128, 5, 128], F32)
    identb = sb("identb", [128, 128], BF16)
    onesb = sb("onesb", [128, 128], BF16)
    ub = sb("ub", [128, 128], BF16)
    flg_t = sb("flg_t", [128, 2], F32)
    S.dma("sp", cst_t, cst_t[:], cst, [], [cst_t.b])
    S.dma("sp", flg_t, flg_t[:], flg, [], [flg_t.b])
    S.copy("dve", identb[:], cst_t[:, 0, :], [cst_t.b], [identb.b])
    S.copy("dve", onesb[:], cst_t[:, 2, :], [cst_t.b], [onesb.b])
    S.copy("dve", ub[:], cst_t[:, 1, :], [cst_t.b], [ub.b])
    ident_f = cst_t[:, 0, :]
    U_f = cst_t[:, 1, :]
    ones_f = cst_t[:, 2, :]
    cmask_f = cst_t[:, 3, :]

    psum = [Tile(nc.alloc_psum_tensor("ps%d" % i, [128, 512], F32)) for i in range(8)]

    def bcast_rows(ap_row, n):
        return ap_row.broadcast(0, 128) if hasattr(ap_row, "broadcast") else ap_row


    class Env:
        pass

    E = Env()
    E.p5 = False

    def alloc_dense(tag):
        E.xT = sb("xT" + tag, [128, 16, 512], BF16)
        E.actT = sb("actT" + tag, [128, NF, 512], BF16)
        E.wsl = [sb("wsl%d" % i + tag, [128, 16, 512], BF16) for i in range(3)]
        E.xt = [sb("xt%d" % i + tag, [128, D], F32) for i in range(4)]
        E.yt = E.xt
        E.lng = sb("lng" + tag, [128, D], F32)
        E.lnb = sb("lnb" + tag, [128, D], F32)
        E.sg = [sb("sg%d" % i + tag, [128, 512], F32) for i in range(2)]
        E.stg = [sb("stg%d" % i + tag, [128, 512], F32) for i in range(3)]
        E.stgb = [sb("stgb%d" % i + tag, [128, 512], BF16) for i in range(3)]
        E.small = [sb("small%d" % i + tag, [128, 64], F32) for i in range(4)]
        E.stats = sb("stats" + tag, [128, 4, 6], F32)
        E.wslot = 0
        E.stgi = 0
        E.stgbi = 0

    def next_wsl():
        E.wslot = (E.wslot + 1) % 3
        return E.wsl[E.wslot]

    def next_stg():
        E.stgi = (E.stgi + 1) % 3
        return E.stg[E.stgi]

    def next_stgb():
        E.stgbi = (E.stgbi + 1) % 3
        return E.stgb[E.stgbi]

    def load_w(slab, w_ap, r0, nk, c0, ncols, col_off=0):
        idx = E.slab_idx
        E.slab_idx += 1
        dst = slab[:, 0:nk, col_off:col_off + ncols]
        if E.p5:
            assert P5LIST[idx] == (w_ap, r0, nk, c0, ncols, col_off), (idx, r0, nk, c0, ncols, col_off)
            scr = wscr2[idx].rearrange("p (k n) -> p k n", k=16)[:, 0:nk, col_off:col_off + ncols]
            S.dma("pool", slab, dst, scr, [], [slab.b])
            return
        scr = wscr[idx].rearrange("p (k n) -> p k n", k=16)[:, 0:nk, col_off:col_off + ncols]
        if E.blk == 0:
            src = w_ap[r0:r0 + nk * 128, c0:c0 + ncols].rearrange("(k p) n -> p k n", p=128)
            S.dma("pool", slab, dst, src, [], [slab.b])
            E.pending_stores.append((slab, scr, dst))
        else:
            S.dma("pool", slab, dst, scr, [], [slab.b])

    def p5_slabs():
        for sl in range(4):
            yield (w_oa, 0, 16, sl * 512, 512, 0)
            yield (w_os, 0, 16, sl * 512, 512, 0)
            yield (w_os, 2048, 16, sl * 512, 512, 0)
        for nb in range(4):
            yield (w_out, 0, 16, nb * 512, 512, 0)
        for s0 in range(0, NF, 2):
            nf = min(2, NF - s0)
            yield (w_gu2, 0, 16, s0 * 128, nf * 128, 0)
            yield (w_gu2, 0, 16, DFF + s0 * 128, nf * 128, 256)
        for nb in range(4):
            for s0 in range(0, NF, 4):
                yield (w_dn2, s0 * 128, min(4, NF - s0), nb * 512, 512, 0)

    P5LIST = list(p5_slabs())
    PRE = Tile(None)

    def precast(lo, hi):
        for idx in range(lo, min(hi, len(P5LIST))):
            w_ap, r0, nk, c0, ncols, col_off = P5LIST[idx]
            src = w_ap[r0:r0 + nk * 128, c0:c0 + ncols].rearrange("(k p) n -> p k n", p=128)
            scr = wscr2[idx].rearrange("p (k n) -> p k n", k=16)[:, 0:nk, col_off:col_off + ncols]
            S.dma("pool", PRE, scr, src, [], [])

    def flush_w():
        for slab, scr, dst in E.pending_stores:
            S.dma("sp", slab, scr, dst, [slab.b], [])
        del E.pending_stores[:]

    def begin_block(blk):
        E.blk = blk
        E.slab_idx = 0
        E.pending_stores = []

    def transpose_to_xT(tiles, dstT, ncol_chunks=16):
        k = 0
        for tt in range(4):
            for kc0 in range(0, ncol_chunks, 4):
                ps = psum[k % 2]
                k += 1
                for j in range(4):
                    S.tr(ps[:, j * 128:(j + 1) * 128], tiles[tt][:, (kc0 + j) * 128:(kc0 + j + 1) * 128], ident_f,
                         [tiles[tt].b, cst_t.b], [ps.b])
                q = "act" if k % 2 else "dve"
                S.copy(q, dstT[:, kc0:kc0 + 4, tt * 128:(tt + 1) * 128],
                       ps[:].rearrange("p (j t) -> p j t", j=4), [ps.b], [dstT.b])

    def layer_norm_tile(y, g_t, b_t, out_t):
        st = E.stats
        for c in range(4):
            S.op("dve", lambda e, c=c: e.bn_stats(st[:, c, :], y[:, c * 512:(c + 1) * 512]), [y.b], [st.b])
        mv = E.small[0]
        S.op("dve", lambda e: e.bn_aggr(mv[:, 0:2], st[:]), [st.b], [mv.b])
        S.ts("dve", mv[:, 2:3], mv[:, 1:2], 1e-5, None, ALU.add, None, [mv.b], [mv.b])
        S.act(mv[:, 3:4], mv[:, 2:3], AF.Sqrt, [mv.b], [mv.b])
        S.op("dve", lambda e: e.reciprocal(mv[:, 4:5], mv[:, 3:4]), [mv.b], [mv.b])
        S.ts("dve", out_t[:], y[:], mv[:, 0:1], mv[:, 4:5], ALU.subtract, ALU.mult, [y.b, mv.b], [out_t.b])
        S.tt("dve", out_t[:], out_t[:], g_t[:], ALU.mult, [out_t.b, g_t.b], [out_t.b])
        S.tt("dve", out_t[:], out_t[:], b_t[:], ALU.add, [out_t.b, b_t.b], [out_t.b])

    def load_ln(idx):
        S.dma("sp", E.lng, E.lng[:], lnp[idx:idx + 1, :].broadcast_to([128, D]), [], [E.lng.b])
        S.dma("sp", E.lnb, E.lnb[:], lnp[idx + 1:idx + 2, :].broadcast_to([128, D]), [], [E.lnb.b])

    def ffn_block(xin, w_gu, w_dn, yout):
        transpose_to_xT(xin, E.xT)
        for tt in range(4):
            S.op("act", lambda e, tt=tt: e.mul(xin[tt][:], xin[tt][:], ALPHA), [xin[tt].b], [xin[tt].b])
        for s0 in range(0, NF, 2):
            nf = min(2, NF - s0)
            slab = next_wsl()
            load_w(slab, w_gu, 0, 16, s0 * 128, nf * 128, 0)
            load_w(slab, w_gu, 0, 16, DFF + s0 * 128, nf * 128, 256)
            flush_w()
            for m in range(nf):
                f = s0 + m
                pg, pu = psum[2 + f % 2], psum[4 + f % 2]
                for kc in range(16):
                    S.mm(pg[:], slab[:, kc, m * 128:(m + 1) * 128], E.xT[:, kc, :], kc == 0, kc == 15, [slab.b, E.xT.b], [pg.b])
                for kc in range(16):
                    S.mm(pu[:], slab[:, kc, 256 + m * 128:256 + (m + 1) * 128], E.xT[:, kc, :], kc == 0, kc == 15, [slab.b, E.xT.b], [pu.b])
                sg = E.sg[f % 2]
                S.act(sg[:], pg[:], AF.Silu, [pg.b], [sg.b])
                S.tt("dve", E.actT[:, f, :], sg[:], pu[:], ALU.mult, [sg.b, pu.b], [E.actT.b])
        for nb in range(4):
            banks = [psum[(nb % 2) * 4 + tt] for tt in range(4)]
            for s0 in range(0, NF, 4):
                nf = min(4, NF - s0)
                slab = next_wsl()
                load_w(slab, w_dn, s0 * 128, nf, nb * 512, 512)
                flush_w()
                for m in range(nf):
                    f = s0 + m
                    for tt in range(4):
                        S.mm(banks[tt][:], E.actT[:, f, tt * 128:(tt + 1) * 128], slab[:, m, :], f == 0, f == NF - 1,
                             [slab.b, E.actT.b], [banks[tt].b])
            for tt in range(4):
                S.stt(yout[tt][:, nb * 512:(nb + 1) * 512], banks[tt][:], 0.5, xin[tt][:, nb * 512:(nb + 1) * 512],
                      ALU.mult, ALU.add, [banks[tt].b, xin[tt].b], [yout[tt].b])
        for tt in range(4):
            layer_norm_tile(yout[tt], E.lng, E.lnb, yout[tt])

    def linear_tm(xT, nk, w_ap, r0, c0, ncols, epilogue, bank0=0):
        slab = next_wsl()
        load_w(slab, w_ap, r0, nk, c0, ncols)
        flush_w()
        for tt in range(4):
            ps = psum[bank0 + tt % 2]
            for kc in range(nk):
                S.mm(ps[:, 0:ncols], xT[:, kc, tt * 128:(tt + 1) * 128], slab[:, kc, 0:ncols], kc == 0, kc == nk - 1,
                     [slab.b, xT.b], [ps.b])
            epilogue(tt, ps)

    def linear_fm(xT, nk, w_ap, r0, c0, nchunks, epilogue, bank0=2, acc=None):
        slab = next_wsl()
        load_w(slab, w_ap, r0, nk, c0, nchunks * 128)
        flush_w()
        for m in range(nchunks):
            ps = psum[bank0 + m % 2]
            for kc in range(nk):
                S.mm(ps[:], slab[:, kc, m * 128:(m + 1) * 128], xT[:, kc, :], kc == 0, kc == nk - 1, [slab.b, xT.b], [ps.b])
            epilogue(m, ps)

    rope_t = sb("rope_t", [128, NT, 48], F32)
    S.dma("sp", rope_t, rope_t[:], ropet.rearrange("(n p) c -> p n c", p=128), [], [rope_t.b])
    dtb_t = sb("dtb_t", [128, 4, 64], F32)
    S.dma("sp", dtb_t, dtb_t[:], ssmv.unsqueeze(0).broadcast_to([128, 4, 64]), [], [dtb_t.b])
    ropetmp = sb("ropetmp", [128, 4, 8 * 16], F32)

    def rope_epi(ps, nh, hd, half, ti, cofs, dst_tile):
        n = nh * hd
        S.copy("act", dst_tile[:, 0:n], ps[:, 0:n], [ps.b], [dst_tile.b])
        dv = dst_tile[:, 0:n].rearrange("p (h d) -> p h d", h=nh)
        cos = rope_t[:, ti, cofs:cofs + half].unsqueeze(1).broadcast_to([128, nh, half])
        sin = rope_t[:, ti, cofs + half:cofs + 2 * half].unsqueeze(1).broadcast_to([128, nh, half])
        x1, x2 = dv[:, :, 0:half], dv[:, :, half:2 * half]
        tmp = [ropetmp[:, j, 0:nh * half].rearrange("p (h c) -> p h c", h=nh) for j in range(4)]
        S.tt("dve", tmp[0], x1, cos, ALU.mult, [dst_tile.b, rope_t.b], [ropetmp.b])
        S.tt("dve", tmp[1], x2, sin, ALU.mult, [dst_tile.b, rope_t.b], [ropetmp.b])
        S.tt("dve", tmp[2], x2, cos, ALU.mult, [dst_tile.b, rope_t.b], [ropetmp.b])
        S.tt("dve", tmp[3], x1, sin, ALU.mult, [dst_tile.b, rope_t.b], [ropetmp.b])
        S.tt("dve", x1, tmp[0], tmp[1], ALU.subtract, [ropetmp.b], [dst_tile.b])
        S.tt("dve", x2, tmp[2], tmp[3], ALU.add, [ropetmp.b], [dst_tile.b])

    P1 = Env()

    def phase1():
        alloc_dense("p1")
        load_ln(0)
        P1.hal = sb("hal", [128, 48, 4], F32)
        P1.cwT = sb("cwT", [128, 48, 5], F32)
        P1.xcin = [sb("xcin%d" % i, [128, 515], F32) for i in range(3)]
        P1.acc = [sb("cacc%d" % i, [128, 512], F32) for i in range(3)]
        P1.so = [sb("cso%d" % i, [128, 512], F32) for i in range(3)]
        P1.k = 0
        P1.pending = None
        S.op("pool", lambda e: e.memset(P1.hal[:], 0.0), [], [P1.hal.b])
        S.dma("sp", P1.cwT, P1.cwT[:], convT, [], [P1.cwT.b])
        for blk in range(1 if "blk1" in debug else 4):
            t0 = blk * 512
            begin_block(blk)
            for tt in range(4):
                S.dma("sp", E.xt[tt], E.xt[tt][:], x_in[t0 + tt * 128:t0 + (tt + 1) * 128, :], [], [E.xt[tt].b])
            ffn_block(E.xt, w_gu1, w_dn1, E.yt)
            for tt in range(4):
                S.dma("sp", E.yt[tt], h1_s[t0 + tt * 128:t0 + (tt + 1) * 128, :], E.yt[tt][:], [E.yt[tt].b], [])
            if "nowin" in debug:
                continue
            transpose_to_xT(E.yt, E.xT)
            h1T = E.xT

            def rows(tt):
                return slice(t0 + tt * 128, t0 + (tt + 1) * 128)

            def on(name):
                segs = [d for d in debug if d.startswith("seg_")]
                return (not segs) or ("seg_" + name in segs)

            for sl in range(4 if on("q") else 0):
                def epi(tt, ps, sl=sl):
                    st = next_stg()
                    rope_epi(ps, 4, 128, 16, blk * 4 + tt, 0, st)
                    S.dma("sp", st, q_s[rows(tt), sl * 512:(sl + 1) * 512], st[:], [st.b], [])
                linear_tm(h1T, 16, w_in, 0, OFF_Q + sl * 512, 512, epi)

            def epi_k(tt, ps):
                st = next_stg()
                rope_epi(ps, 4, 128, 16, blk * 4 + tt, 0, st)
                S.dma("sp", st, kvg_in[rows(tt), 0:512], st[:], [st.b], [])
            if on("k"):
                linear_tm(h1T, 16, w_in, 0, OFF_K, 512, epi_k)

            def epi_v(tt, ps):
                st = next_stg()
                S.copy("act", st[:], ps[:], [ps.b], [st.b])
                S.dma("sp", st, kvg_in[rows(tt), 512:1024], st[:], [st.b], [])
            if on("v"):
                linear_tm(h1T, 16, w_in, 0, OFF_V, 512, epi_v)

            for sl in range(2 if on("qi") else 0):
                def epi(tt, ps, sl=sl):
                    st = next_stg()
                    rope_epi(ps, 8, 64, 8, blk * 4 + tt, 32, st)
                    S.dma("sp", st, qi_s[rows(tt), sl * 512:(sl + 1) * 512], st[:], [st.b], [])
                linear_tm(h1T, 16, w_in, 0, OFF_QI + sl * 512, 512, epi)

            def epi_kw(tt, ps):
                st = next_stg()
                rope_epi(ps, 1, 64, 8, blk * 4 + tt, 32, st)
                S.dma("sp", st, kvg_in[rows(tt), 1024:1088], st[:, 0:64], [st.b], [])
                sf = next_stg()
                S.op("act", lambda e: e.mul(sf[:, 0:16], ps[:, 64:80], 0.125 * 0.25), [ps.b], [sf.b])
                S.dma("sp", sf, wi_s[rows(tt), :], sf[:, 0:16], [sf.b], [])
            if on("kw"):
                linear_tm(h1T, 16, w_in, 0, OFF_KI, 80, epi_kw)

            for sl in range(8 if on("z") else 0):
                def epi(tt, ps, sl=sl):
                    st = next_stgb()
                    S.copy("act" if tt % 2 else "dve", st[:], ps[:], [ps.b], [st.b])
                    S.dma("sp", st, z_s[rows(tt), sl * 512:(sl + 1) * 512], st[:], [st.b], [])
                linear_tm(h1T, 16, w_in, 0, OFF_Z + sl * 512, 512, epi)

            for sl in range(12 if on("xbc") else 0):
                def epi(m, ps, sl=sl):
                    ch = sl * 4 + m
                    xin, acc, so = P1.xcin[P1.k % 3], P1.acc[P1.k % 3], P1.so[P1.k % 3]
                    P1.k += 1
                    S.copy("act", xin[:, 3:515], ps[:], [ps.b], [xin.b])
                    S.copy("act", xin[:, 0:3], P1.hal[:, ch, 0:3], [P1.hal.b], [xin.b])
                    S.copy("act", P1.hal[:, ch, 0:3], xin[:, 512:515], [xin.b], [P1.hal.b])
                    cols = slice(ch * 128, (ch + 1) * 128)
                    if blk == 0:
                        S.dma("sp", xin, xbc_s[3:6, cols].rearrange("t c -> c t"), xin[:, 3:6], [xin.b], [], slow=True)
                    if blk == 3:
                        S.dma("sp", xin, halo_in[0:3, cols].rearrange("t c -> c t"), xin[:, 512:515], [xin.b], [], slow=True)
                    cw = P1.cwT
                    S.ts("dve", acc[:], xin[:, 0:512], cw[:, ch, 0:1], cw[:, ch, 4:5], ALU.mult, ALU.add, [xin.b, cw.b], [acc.b])
                    for i in (1, 2, 3):
                        S.stt(acc[:], xin[:, i:i + 512], cw[:, ch, i:i + 1], acc[:], ALU.mult, ALU.add, [xin.b, cw.b, acc.b], [acc.b])
                    S.act(acc[:], acc[:], AF.Silu, [acc.b], [acc.b])

                    def tail(acc=acc, so=so, cols=cols, kk=P1.k):
                        pst = psum[kk % 2]
                        for j in range(4):
                            S.tr(pst[:, j * 128:(j + 1) * 128], acc[:, j * 128:(j + 1) * 128], ident_f, [acc.b, cst_t.b], [pst.b])
                        S.copy("dve" if kk % 2 else "act", so[:], pst[:], [pst.b], [so.b])
                        S.dma("sp", so, xc_s[t0:t0 + 512, cols].rearrange("(j p) c -> p j c", p=128),
                              so[:].rearrange("p (j c) -> p j c", j=4), [so.b], [])
                    if P1.pending is not None:
                        P1.pending()
                    P1.pending = tail
                linear_fm(h1T, 16, w_in, 0, OFF_XBC + sl * 512, 4, epi)
            if P1.pending is not None:
                P1.pending()
                P1.pending = None

            def epi_dt(tt, ps):
                sf = next_stg()
                S.tt("dve", sf[:, 0:64], ps[:, 0:64], dtb_t[:, 0, :], ALU.add, [ps.b, dtb_t.b], [sf.b])
                S.act(sf[:, 64:128], sf[:, 0:64], AF.Exp, [sf.b], [sf.b])
                S.act(sf[:, 128:192], sf[:, 64:128], AF.Ln, [sf.b], [sf.b], bias=1.0)
                S.dma("sp", sf, dt_s[rows(tt), :], sf[:, 128:192], [sf.b], [])
            if on("dt"):
                linear_tm(h1T, 16, w_in, 0, OFF_DT, 64, epi_dt)

            for sl in range(8 if on("g") else 0):
                def epi(m, ps, sl=sl):
                    st = next_stgb()
                    S.act(st[:], ps[:], AF.Sigmoid, [ps.b], [st.b])
                    r0 = (sl * 4 + m) * 128
                    S.dma("sp", st, gT_s[r0:r0 + 128, t0:t0 + 512], st[:], [st.b], [])
                linear_fm(h1T, 16, w_in, 0, OFF_G + sl * 512, 4, epi)

    with ExitStack() as es:
        ES[0] = es
        phase1()
        end_phase()
    ES[0] = None

    if "stop1" in debug:
        S.emit()
        return nc

    RG = [[2 * i, 2 * i + 1] for i in range(NCORES[0] // 2)]

    def collective(in_ap, out_ap):
        st = S.cc_stream()
        bb = Buf()
        S.op("pool", lambda e: e.collective_compute("AllGather", ALU.bypass, replica_groups=RG, ins=[in_ap], outs=[out_ap]),
             [], [bb], st)

    def phase2():
        for qq in range(8):
            collective(kvg_in[qq * 256:(qq + 1) * 256, :].rearrange("(p n) c -> p (n c)", p=128), kvg[qq])
        collective(halo_in, halo_g)
        S.barrier()
        hl = sb("hl", [3, 6144], F32)
        S.dma("sp", hl, hl[:], halo_g[0:3, :], [], [hl.b])
        S.ts("dve", hl[:], hl[:], flg_t[0:3, 0:1], None, ALU.mult, None, [hl.b, flg_t.b], [hl.b])
        S.dma("sp", hl, xbc_s[0:3, :], hl[:], [hl.b], [])
        S.barrier()
        CW = 1024
        wt = [sb("cw%d" % i, [3, 5, CW], F32) for i in range(2)]
        xs = [sb("cx%d" % i, [3, 4, CW], F32) for i in range(2)]
        ys = [sb("cy%d" % i, [3, CW], F32) for i in range(2)]
        for cb in range(6144 // CW):
            w, x4, y1 = wt[cb % 2], xs[cb % 2], ys[cb % 2]
            cs = slice(cb * CW, (cb + 1) * CW)
            S.dma("sp", w, w[:], conv_wb[:, cs].unsqueeze(0).broadcast_to([3, 5, CW]), [], [w.b])
            src = bass.AP(xbc_s.tensor, xbc_s[0:1, cs].offset, [[6144, 3], [6144, 4], [1, CW]])
            S.dma("sp", x4, x4[:], src, [], [x4.b])
            S.tt("dve", x4[:], x4[:], w[:, 0:4, :], ALU.mult, [x4.b, w.b], [x4.b])
            S.tt("dve", x4[:, 0:2, :], x4[:, 0:2, :], x4[:, 2:4, :], ALU.add, [x4.b], [x4.b])
            S.tt("dve", y1[:], x4[:, 0, :], x4[:, 1, :], ALU.add, [x4.b], [y1.b])
            S.tt("dve", y1[:], y1[:], w[:, 4, :], ALU.add, [y1.b, w.b], [y1.b])
            S.act(y1[:], y1[:], AF.Silu, [y1.b], [y1.b])
            S.dma("sp", y1, xc_s[0:3, cs], y1[:], [y1.b], [])

    with ExitStack() as es:
        ES[0] = es
        phase2()
        end_phase()
    ES[0] = None
    if "stop2" in debug:
        S.emit()
        return nc

    def bc3(ap2d, n_inner):
        return ap2d.unsqueeze(2).broadcast_to([128, ap2d.shape[1], n_inner])

    def phase3():
        xc = [sb("s_xc%d" % i, [128, 4096], F32) for i in range(2)]
        bc = [sb("s_bc%d" % i, [128, 2048], F32) for i in range(2)]
        dtt = [sb("s_dt%d" % i, [128, 64], F32) for i in range(2)]
        xdt = sb("s_xdt", [128, 4096], BF16)
        xw = sb("s_xw", [128, 4096], BF16)
        btm = sb("s_btm", [128, 1024], BF16)
        st_f = sb("s_stf", [128, 4096], F32)
        st_b = sb("s_stb", [128, 4096], BF16)
        BT = sb("s_BT", [128, 8, 128], BF16)
        CT = [sb("s_CT%d" % i, [128, 8, 128], BF16) for i in range(2)]
        sm = sb("s_sm", [128, 12, 64], F32)
        etot = sb("s_etot", [128, NT, 64], F32)
        cbm = [sb("s_cbm%d" % i, [128, 128], F32) for i in range(2)]
        Zw = [sb("s_Zw%d" % i, [128, 8, 128], F32) for i in range(2)]
        Eww = [sb("s_Ew%d" % i, [128, 1024], F32) for i in range(2)]
        Mww = [sb("s_Mw%d" % i, [128, 8, 128], BF16) for i in range(2)]
        yo = [sb("s_yo%d" % i, [128, 512], F32) for i in range(2)]
        dx = [sb("s_dx%d" % i, [128, 512], F32) for i in range(2)]
        A_, DTA, A_C, NEGA, EA, W_, DEC, AOFF, TMP, TMP2 = range(10)
        S.act(sm[:, A_, :], dtb_t[:, 1, :], AF.Exp, [dtb_t.b], [sm.b])
        S.ts("dve", sm[:, A_, :], sm[:, A_, :], -1.0, None, ALU.mult, None, [sm.b], [sm.b])
        S.op("pool", lambda e: e.memset(sm[:, AOFF, :], 0.0), [], [sm.b])
        S.op("pool", lambda e: e.memset(st_f[:], 0.0), [], [st_f.b])
        S.op("pool", lambda e: e.memset(st_b[:], 0.0), [], [st_b.b])
        hk = 0
        for c in range(NT):
            X, BC, DT = xc[c % 2], bc[c % 2], dtt[c % 2]
            r = slice(c * 128, (c + 1) * 128)
            precast(c * 7, (c + 1) * 7)
            S.dma("sp", X, X[:], xc_s[r, 0:4096], [], [X.b])
            S.dma("sp", BC, BC[:], xc_s[r, 4096:6144], [], [BC.b])
            S.dma("sp", DT, DT[:], dt_s[r, :], [], [DT.b])
            S.tt("dve", sm[:, DTA, :], DT[:], sm[:, A_, :], ALU.mult, [DT.b, sm.b], [sm.b])
            pa = psum[0]
            S.mm(pa[:, 0:64], U_f, sm[:, DTA, :], True, True, [cst_t.b, sm.b], [pa.b])
            S.mm(pa[:, 64:128], ones_f, sm[:, DTA, :], True, True, [cst_t.b, sm.b], [pa.b])
            S.copy("act", sm[:, A_C, :], pa[:, 0:64], [pa.b], [sm.b])
            S.ts("dve", sm[:, NEGA, :], pa[:, 0:64], -1.0, None, ALU.mult, None, [pa.b], [sm.b])
            S.act(sm[:, EA, :], pa[:, 0:64], AF.Exp, [pa.b], [sm.b])
            S.tt("dve", sm[:, TMP, :], pa[:, 64:128], sm[:, A_C, :], ALU.subtract, [pa.b, sm.b], [sm.b])
            S.act(sm[:, W_, :], sm[:, TMP, :], AF.Exp, [sm.b], [sm.b])
            S.act(sm[:, DEC, :], pa[:, 64:128], AF.Exp, [pa.b], [sm.b])
            S.tt("dve", sm[:, TMP2, :], sm[:, A_C, :], sm[:, AOFF, :], ALU.add, [sm.b], [sm.b])
            S.act(etot[:, c, :], sm[:, TMP2, :], AF.Exp, [sm.b], [etot.b])
            S.tt("dve", sm[:, AOFF, :], sm[:, AOFF, :], pa[:, 64:128], ALU.add, [sm.b, pa.b], [sm.b])
            x3 = X[:].rearrange("p (h d) -> p h d", h=64)
            S.tt("dve", xdt[:].rearrange("p (h d) -> p h d", h=64), x3, bc3(DT[:], 64), ALU.mult, [X.b, DT.b], [xdt.b])
            S.tt("pool", xw[:].rearrange("p (h d) -> p h d", h=64), xdt[:].rearrange("p (h d) -> p h d", h=64),
                 bc3(sm[:, W_, :], 64), ALU.mult, [xdt.b, sm.b], [xw.b])
            S.copy("act", btm[:], BC[:, 0:1024], [BC.b], [btm.b])
            ct = CT[c % 2]

            def front(g, X=X, BC=BC, ct=ct):
                pt = psum[1]
                S.tr(pt[:, 0:128], BC[:, g * 128:(g + 1) * 128], ident_f, [BC.b, cst_t.b], [pt.b])
                S.tr(pt[:, 128:256], BC[:, 1024 + g * 128:1024 + (g + 1) * 128], ident_f, [BC.b, cst_t.b], [pt.b])
                S.copy("act", BT[:, g, :], pt[:, 0:128], [pt.b], [BT.b])
                S.copy("act", ct[:, g, :], pt[:, 128:256], [pt.b], [ct.b])
                pcb = psum[1]
                S.mm(pcb[:, 256:384], BT[:, g, :], ct[:, g, :], True, True, [BT.b, ct.b], [pcb.b])
                cb_ = cbm[g % 2]
                S.tt("dve", cb_[:], pcb[:, 256:384], U_f, ALU.mult, [pcb.b, cst_t.b], [cb_.b])
                pyo = psum[3]
                S.mm(pyo[:], ct[:, g, :], st_b[:, g * 512:(g + 1) * 512], True, True, [ct.b, st_b.b], [pyo.b])
                y1 = yo[g % 2]
                S.copy("act", y1[:], pyo[:], [pyo.b], [y1.b])
                Zg, Ew, Mw = Zw[g % 2], Eww[g % 2], Mww[g % 2]
                S.tt("pool", Zg[:], U_f.unsqueeze(1).broadcast_to([128, 8, 128]), bc3(sm[:, DTA, g * 8:(g + 1) * 8], 128), ALU.mult,
                     [cst_t.b, sm.b], [Zg.b])
                pab0, pab1 = psum[4], psum[5]
                S.mm(pab0[:], ones_f, Zg[:, 0:4, :], True, True, [cst_t.b, Zg.b], [pab0.b])
                S.mm(pab1[:], ones_f, Zg[:, 4:8, :], True, True, [cst_t.b, Zg.b], [pab1.b])
                S.copy("act", Ew[:, 0:512], pab0[:], [pab0.b], [Ew.b])
                S.copy("act", Ew[:, 512:1024], pab1[:], [pab1.b], [Ew.b])
                E3 = Ew[:].rearrange("p (h l) -> p h l", h=8)
                S.tt("dve", E3, E3, bc3(sm[:, NEGA, g * 8:(g + 1) * 8], 128), ALU.add, [Ew.b, sm.b], [Ew.b])
                S.act(Ew[:], Ew[:], AF.Exp, [Ew.b], [Ew.b])
                S.stt(Mw[:], E3, 1.0, cb_[:].unsqueeze(1).broadcast_to([128, 8, 128]), ALU.min, ALU.mult, [Ew.b, cb_.b], [Mw.b])
                psu = psum[7] if g % 2 else psum[2]
                S.mm(psu[:], btm[:, g * 128:(g + 1) * 128], xw[:, g * 512:(g + 1) * 512], True, True, [btm.b, xw.b], [psu.b])

            def back(g, X=X, r=r):
                pyd = psum[6]
                Mw = Mww[g % 2]
                for hh in range(8):
                    h = g * 8 + hh
                    S.mm(pyd[:, hh * 64:(hh + 1) * 64], Mw[:, hh, :], xdt[:, h * 64:(h + 1) * 64], True, True, [Mw.b, xdt.b], [pyd.b])
                y1, d1 = yo[g % 2], dx[g % 2]
                gs = slice(g * 512, (g + 1) * 512)
                S.tt("dve", y1[:].rearrange("p (h d) -> p h d", h=8), y1[:].rearrange("p (h d) -> p h d", h=8),
                     bc3(sm[:, EA, g * 8:(g + 1) * 8], 64), ALU.mult, [y1.b, sm.b], [y1.b])
                S.tt("pool", d1[:].rearrange("p (h d) -> p h d", h=8), X[:, gs].rearrange("p (h d) -> p h d", h=8),
                     bc3(dtb_t[:, 2, g * 8:(g + 1) * 8], 64), ALU.mult, [X.b, dtb_t.b], [d1.b])
                S.tt("pool", d1[:], d1[:], y1[:], ALU.add, [d1.b, y1.b], [d1.b])
                S.tt("dve", d1[:], d1[:], pyd[:], ALU.add, [d1.b, pyd.b], [d1.b])
                S.dma("sp", d1, ypre_s[r, gs], d1[:], [d1.b], [])
                psu = psum[7] if g % 2 else psum[2]
                S.tt("dve", st_f[:, gs].rearrange("p (h d) -> p h d", h=8), st_f[:, gs].rearrange("p (h d) -> p h d", h=8),
                     bc3(sm[:, DEC, g * 8:(g + 1) * 8], 64), ALU.mult, [st_f.b, sm.b], [st_f.b])
                S.tt("dve", st_f[:, gs], st_f[:, gs], psu[:], ALU.add, [st_f.b, psu.b], [st_f.b])
                S.copy("act", st_b[:, gs], st_f[:, gs], [st_f.b], [st_b.b])

            front(0)
            for g in range(8):
                if g + 1 < 8:
                    front(g + 1)
                back(g)
            S.dma("sp", ct, ct_s[c * 1024:(c + 1) * 1024, :].rearrange("(g n) l -> n g l", g=8), ct[:], [ct.b], [])
        S.dma("sp", st_f, st_in, st_f[:], [st_f.b], [])
        S.barrier()
        collective(st_in, st_g)
        S.barrier()
        S.dma("sp", st_f, st_f[:], st_g[0:128, :], [], [st_f.b])
        S.ts("dve", st_b[:], st_f[:], flg_t[:, 0:1], None, ALU.mult, None, [st_f.b, flg_t.b], [st_b.b])
        nw = xc[0]
        S.dma("sp", nw, nw[:], normw.broadcast_to([128, 4096]), [], [nw.b])
        YP = xc[1]
        zt = [xdt, xw]
        ssm_f = sb("s_ssmf", [128, 4096], F32)
        ssb = [sb("s_ssb%d" % i, [128, 4, 128], BF16) for i in range(2)]
        yg = [sb("s_yg%d" % i, [128, 512], F32) for i in range(8)]
        dg = [sb("s_dg%d" % i, [128, 512], F32) for i in range(8)]
        sqs = sb("s_sqs", [128, 8, 4], F32)
        k = 0
        for c in range(NT):
            r = slice(c * 128, (c + 1) * 128)
            ct = CT[c % 2]
            Zc = zt[c % 2]
            S.dma("sp", ct, ct[:], ct_s[c * 1024:(c + 1) * 1024, :].rearrange("(g n) l -> n g l", g=8), [], [ct.b])
            S.dma("sp", YP, YP[:], ypre_s[r, :], [], [YP.b])
            S.dma("sp", Zc, Zc[:], z_s[r, :], [], [Zc.b])
            G = range(8)
            gsl = [slice(g * 512, (g + 1) * 512) for g in G]
            for g in G:
                S.mm(psum[g][:], ct[:, g, :], st_b[:, gsl[g]], True, True, [ct.b, st_b.b], [psum[g].b])
            for g in G:
                S.copy("act", yg[g][:], psum[g][:], [psum[g].b], [yg[g].b])
            for g in G:
                y1 = yg[g]
                S.tt("dve", y1[:].rearrange("p (h d) -> p h d", h=8), y1[:].rearrange("p (h d) -> p h d", h=8),
                     bc3(etot[:, c, g * 8:(g + 1) * 8], 64), ALU.mult, [y1.b, etot.b], [y1.b])
                S.tt("dve", y1[:], y1[:], YP[:, gsl[g]], ALU.add, [y1.b, YP.b], [y1.b])
            for g in G:
                S.act(dg[g][:], Zc[:, gsl[g]], AF.Silu, [Zc.b], [dg[g].b])
            for g in G:
                S.tt("pool", yg[g][:], yg[g][:], dg[g][:], ALU.mult, [yg[g].b, dg[g].b], [yg[g].b])
            for g in G:
                S.act(dg[g][:], yg[g][:], AF.Square, [yg[g].b], [dg[g].b, sqs.b], accum_out=sqs[:, g, 0:1])
            S.ts("dve", sqs[:, :, 1], sqs[:, :, 0], 1.0 / 512.0, 1e-5, ALU.mult, ALU.add, [sqs.b], [sqs.b])
            S.act(sqs[:, :, 2], sqs[:, :, 1], AF.Sqrt, [sqs.b], [sqs.b])
            S.op("dve", lambda e: e.reciprocal(sqs[:, :, 3], sqs[:, :, 2]), [sqs.b], [sqs.b])
            for g in G:
                S.stt(ssm_f[:, gsl[g]], yg[g][:], sqs[:, g, 3:4], nw[:, gsl[g]], ALU.mult, ALU.mult, [yg[g].b, sqs.b, nw.b], [ssm_f.b])
            for kc0 in range(0, 32, 4):
                ps = psum[2 + k % 2]
                sbt = ssb[k % 2]
                k += 1
                for j in range(4):
                    S.tr(ps[:, j * 128:(j + 1) * 128], ssm_f[:, (kc0 + j) * 128:(kc0 + j + 1) * 128], ident_f, [ssm_f.b, cst_t.b], [ps.b])
                S.copy("act" if k % 2 else "dve", sbt[:], ps[:].rearrange("p (j t) -> p j t", j=4), [ps.b], [sbt.b])
                S.dma("sp", sbt, ssmT_s[kc0 * 128:(kc0 + 4) * 128, r].rearrange("(j p) t -> p j t", p=128), sbt[:], [sbt.b], [])

    E_small = [sb("e_small%d" % i, [128, 8], F32) for i in range(2)]
    with ExitStack() as es:
        ES[0] = es
        phase3()
        end_phase()
    ES[0] = None
    if "stop3" in debug:
        S.emit()
        return nc

    def phase4():
        kT = sb("a_kT", [128, 4, 4096], BF16)
        V = sb("a_V", [128, 32, 512], BF16)
        kiT2 = sb("a_kiT", [128, 4096], BF16)
        kst = [sb("a_kst%d" % i, [128, 1088], F32) for i in range(2)]
        kid = [sb("a_kid%d" % i, [128, 128], F32) for i in range(2)]
        wi_t = sb("a_wi", [128, NT, 16], F32)
        S.dma("sp", wi_t, wi_t[:], wi_s.rearrange("(n p) h -> p n h", p=128), [], [wi_t.b])
        k = 0
        for b in range(32):
            ks, kd = kst[b % 2], kid[b % 2]
            src = (kvg[b // 2].rearrange("p (n c) -> (p n) c", n=2)[(b % 2) * 128:(b % 2) * 128 + 128, :] if b < 16
                   else kvg_in[(b - 16) * 128:(b - 15) * 128, :])
            S.dma("sp", ks, ks[:], src, [], [ks.b])
            S.copy("dve", kd[:, 0:64], ks[:, 1024:1088], [ks.b], [kd.b])
            S.copy("dve", kd[:, 64:128], ks[:, 1024:1088], [ks.b], [kd.b])
            S.copy("act", V[:, b, :], ks[:, 512:1024], [ks.b], [V.b])
            ps = psum[k % 2]
            k += 1
            for j in range(4):
                S.tr(ps[:, j * 128:(j + 1) * 128], ks[:, j * 128:(j + 1) * 128], ident_f, [ks.b, cst_t.b], [ps.b])
            S.copy("act", kT[:, :, b * 128:(b + 1) * 128], ps[:].rearrange("p (j t) -> p j t", j=4), [ps.b], [kT.b])
            ps2 = psum[2 + k % 2]
            S.tr(ps2[:, 0:128], kd[:], ident_f, [kd.b, cst_t.b], [ps2.b])
            S.copy("dve", kiT2[:, b * 128:(b + 1) * 128], ps2[:, 0:128], [ps2.b], [kiT2.b])

        qf = sb("a_qf", [128, 2048], F32)
        qif = sb("a_qif", [128, 1024], F32)
        qTs = [sb("a_qT%d" % i, [128, 16, 128], BF16) for i in range(2)]
        qiT = sb("a_qiT", [128, 8, 128], BF16)
        score = sb("a_score", [128, 4096], F32)
        maskf = sb("a_maskf", [128, 4096], F32)
        maskTs = [sb("a_maskT%d" % i, [128, 32, 128], BF16) for i in range(2)]
        rl = [sb("a_rl%d" % i, [128, 512], F32) for i in range(2)]
        pt = [sb("a_p%d" % i, [128, 512], BF16) for i in range(3)]
        pm = [sb("a_pm%d" % i, [128, 512], BF16) for i in range(3)]
        bs = sb("a_bs", [128, 8], F32)
        rinv = sb("a_rinv", [128, 512], F32)
        ost = [sb("a_ost%d" % i, [128, 512], BF16) for i in range(2)]
        LO, HI, MID, CNT, GC, W0 = range(6)
        SCALE = 128.0 ** -0.5
        kk = [k]

        def prep_index_bisect(i):
            r = slice(i * 128, (i + 1) * 128)
            qT = qTs[i % 2]
            S.dma("sp", qf, qf[:], q_s[r, :], [], [qf.b])
            S.dma("sp", qif, qif[:], qi_s[r, :], [], [qif.b])
            for kc0 in range(0, 16, 4):
                ps = psum[kk[0] % 2]
                kk[0] += 1
                for j in range(4):
                    S.tr(ps[:, j * 128:(j + 1) * 128], qf[:, (kc0 + j) * 128:(kc0 + j + 1) * 128], ident_f, [qf.b, cst_t.b], [ps.b])
                S.copy("act", qT[:, kc0:kc0 + 4, :], ps[:].rearrange("p (j t) -> p j t", j=4), [ps.b], [qT.b])
            for kc0 in range(0, 8, 4):
                ps = psum[kk[0] % 2]
                kk[0] += 1
                for j in range(4):
                    S.tr(ps[:, j * 128:(j + 1) * 128], qif[:, (kc0 + j) * 128:(kc0 + j + 1) * 128], ident_f, [qif.b, cst_t.b], [ps.b])
                S.copy("act", qiT[:, kc0:kc0 + 4, :], ps[:].rearrange("p (j t) -> p j t", j=4), [ps.b], [qiT.b])
            NB = 16 + i + 1
            ncols = NB * 128
            for c0 in range(0, ncols, 512):
                wd = min(512, ncols - c0)
                for h in range(16):
                    pl = psum[2 + h % 2]
                    p0 = (h % 2) * 64
                    S.mm(pl[:, 0:wd], qiT[p0:p0 + 64, h // 2, :], kiT2[p0:p0 + 64, c0:c0 + wd], True, True, [qiT.b, kiT2.b], [pl.b])
                    rr = rl[h % 2]
                    S.act(rr[:, 0:wd], pl[:, 0:wd], AF.Relu, [pl.b], [rr.b])
                    if h == 0:
                        S.ts("dve", score[:, c0:c0 + wd], rr[:, 0:wd], wi_t[:, i, 0:1], None, ALU.mult, None, [rr.b, wi_t.b], [score.b])
                    else:
                        S.stt(score[:, c0:c0 + wd], rr[:, 0:wd], wi_t[:, i, h:h + 1], score[:, c0:c0 + wd], ALU.mult, ALU.add,
                              [rr.b, wi_t.b, score.b], [score.b])
            S.op("dve", lambda e, n=ncols: e.tensor_reduce(out=bs[:, HI:HI + 1], in_=score[:, 0:n], axis=AX.X, op=ALU.max), [score.b], [bs.b])
            S.op("dve", lambda e, n=ncols: e.tensor_reduce(out=bs[:, LO:LO + 1], in_=score[:, 0:n], axis=AX.X, op=ALU.min), [score.b], [bs.b])
            S.stt(bs[:, W0:W0 + 1], bs[:, HI:HI + 1], 1.0, bs[:, LO:LO + 1], ALU.add, ALU.subtract, [bs.b], [bs.b])
            S.ts("dve", score[:, 0:2048], score[:, 0:2048], flg_t[:, 0:1], flg_t[:, 1:2], ALU.mult, ALU.add, [score.b, flg_t.b], [score.b])
            dc = 2048 + i * 128
            S.tt("dve", score[:, dc:dc + 128], score[:, dc:dc + 128], cmask_f, ALU.add, [score.b, cst_t.b], [score.b])
            for it in range(N_BISECT):
                if it in (0, 5, 10, 14):
                    yield
                ck = 2.0 ** -(it + 1)
                S.stt(bs[:, MID:MID + 1], bs[:, W0:W0 + 1], ck, bs[:, LO:LO + 1], ALU.mult, ALU.add, [bs.b], [bs.b])
                S.ts("dve", maskf[:, 0:ncols], score[:, 0:ncols], bs[:, MID:MID + 1], 0.0, ALU.is_ge, ALU.add, [score.b, bs.b], [maskf.b, bs.b],
                     accum_out=bs[:, CNT:CNT + 1])
                S.ts("dve", bs[:, GC:GC + 1], bs[:, CNT:CNT + 1], 256.0, ck, ALU.is_ge, ALU.mult, [bs.b], [bs.b])
                S.stt(bs[:, LO:LO + 1], bs[:, GC:GC + 1], bs[:, W0:W0 + 1], bs[:, LO:LO + 1], ALU.mult, ALU.add, [bs.b], [bs.b])
            S.ts("dve", maskf[:, 0:ncols], score[:, 0:ncols], bs[:, LO:LO + 1], None, ALU.is_ge, None, [score.b, bs.b], [maskf.b])

        def mask_transpose(i):
            NB = 16 + i + 1
            maskT = maskTs[i % 2]
            for b0 in range(0, NB, 4):
                nb4 = min(4, NB - b0)
                ps = psum[kk[0] % 2]
                kk[0] += 1
                for j in range(nb4):
                    S.tr(ps[:, j * 128:(j + 1) * 128], maskf[:, (b0 + j) * 128:(b0 + j + 1) * 128], ident_f, [maskf.b, cst_t.b], [ps.b])
                S.copy("act", maskT[:, b0:b0 + nb4, :], ps[:, 0:nb4 * 128].rearrange("p (j t) -> p j t", j=nb4), [ps.b], [maskT.b])

        def attend(i):
            r = slice(i * 128, (i + 1) * 128)
            NB = 16 + i + 1
            qT, maskT = qTs[i % 2], maskTs[i % 2]
            for kv in range(4):
                pot, prs = psum[4 + kv % 2], psum[6 + kv % 2]

                def qk(b, kv=kv):
                    psc = psum[2 + b % 2]
                    S.mm(psc[:], kT[:, kv, b * 128:(b + 1) * 128], qT[:, 4 * kv:4 * kv + 4, :], True, True, [kT.b, qT.b], [psc.b])
                    p_, pm_ = pt[b % 3], pm[b % 3]
                    S.act(p_[:], psc[:], AF.Exp, [psc.b], [p_.b], scale=SCALE)
                    S.tt("pool", pm_[:].rearrange("p (h t) -> p h t", h=4), p_[:].rearrange("p (h t) -> p h t", h=4),
                         maskT[:, b, :].unsqueeze(1).broadcast_to([128, 4, 128]), ALU.mult, [p_.b, maskT.b], [pm_.b])

                def pv(b, kv=kv, pot=pot, prs=prs):
                    pm_ = pm[b % 3]
                    S.mm(pot[:], V[:, b, kv * 128:(kv + 1) * 128], pm_[:], b == 0, b == NB - 1, [V.b, pm_.b], [pot.b])
                    S.mm(prs[:], onesb[:], pm_[:], b == 0, b == NB - 1, [onesb.b, pm_.b], [prs.b])

                LA = 2
                for b in range(min(LA, NB)):
                    qk(b)
                for b in range(NB):
                    if b + LA < NB:
                        qk(b + LA)
                    pv(b)
                S.op("dve", lambda e, prs=prs: e.reciprocal(rinv[:], prs[:]), [prs.b], [rinv.b])
                o_ = ost[kv % 2]
                S.tt("dve", o_[:], pot[:], rinv[:], ALU.mult, [pot.b, rinv.b], [o_.b])
                S.dma("sp", o_, attnT_s[kv * 512:(kv + 1) * 512, r].rearrange("(h d) t -> d h t", h=4),
                      o_[:].rearrange("p (h t) -> p h t", h=4), [o_.b], [])
                yield

        for _ in prep_index_bisect(0):
            pass
        mask_transpose(0)
        for i in range(NT):
            gen_b = prep_index_bisect(i + 1) if i + 1 < NT else iter(())
            gen_a = attend(i)
            next(gen_b, None)
            for _ in range(4):
                next(gen_b, None)
                next(gen_a, None)
            for _ in gen_b:
                pass
            for _ in gen_a:
                pass
            if i + 1 < NT:
                mask_transpose(i + 1)

    with ExitStack() as es:
        ES[0] = es
        phase4()
        end_phase()
    ES[0] = None
    if "stop4" in debug:
        S.emit()
        return nc

    def phase5():
        alloc_dense("p5")
        E.p5 = True
        mT = sb("mT", [128, 16, 512], BF16)
        gt = [sb("gt%d" % i, [128, 2, 512], BF16) for i in range(2)]
        for blk in range(4):
            t0 = blk * 512
            begin_block(blk)
            ts_ = slice(t0, t0 + 512)
            S.dma("sp", E.xT, E.xT[:], attnT_s[:, ts_].rearrange("(k p) t -> p k t", p=128), [], [E.xT.b])
            S.dma("sp", E.actT, E.actT[:, 0:32, :], ssmT_s[:, ts_].rearrange("(k p) t -> p k t", p=128), [], [E.actT.b])
            for tt in range(4):
                S.dma("sp", E.xt[tt], E.xt[tt][:], h1_s[t0 + tt * 128:t0 + (tt + 1) * 128, :], [], [E.xt[tt].b])
            for sl in range(4):
                sa, s1, s2 = next_wsl(), next_wsl(), next_wsl()
                load_w(sa, w_oa, 0, 16, sl * 512, 512)
                load_w(s1, w_os, 0, 16, sl * 512, 512)
                load_w(s2, w_os, 2048, 16, sl * 512, 512)
                flush_w()
                for m in range(4):
                    dch = sl * 4 + m
                    g_ = gt[dch % 2]
                    S.dma("sp", g_, g_[:, 0, :], gT_s[dch * 128:(dch + 1) * 128, ts_], [], [g_.b])
                    S.dma("sp", g_, g_[:, 1, :], gT_s[2048 + dch * 128:2048 + (dch + 1) * 128, ts_], [], [g_.b])
                    pa, pss = psum[2 + m % 2], psum[4 + m % 2]
                    ms = slice(m * 128, (m + 1) * 128)
                    for kc in range(16):
                        S.mm(pa[:], sa[:, kc, ms], E.xT[:, kc, :], kc == 0, kc == 15, [sa.b, E.xT.b], [pa.b])
                    for kc in range(32):
                        w_ = s1 if kc < 16 else s2
                        S.mm(pss[:], w_[:, kc % 16, ms], E.actT[:, kc, :], kc == 0, kc == 31, [w_.b, E.actT.b], [pss.b])
                    a1, a2 = E.sg[0], E.sg[1]
                    S.tt("dve", a1[:], pa[:], g_[:, 0, :], ALU.mult, [pa.b, g_.b], [a1.b])
                    S.tt("dve", a2[:], pss[:], g_[:, 1, :], ALU.mult, [pss.b, g_.b], [a2.b])
                    S.tt("pool", mT[:, dch, :], a1[:], a2[:], ALU.add, [a1.b, a2.b], [mT.b])
            for nb in range(4):
                def epi(tt, ps, nb=nb):
                    cs = slice(nb * 512, (nb + 1) * 512)
                    S.stt(E.xt[tt][:, cs], E.xt[tt][:, cs], ALPHA, ps[:], ALU.mult, ALU.add, [E.xt[tt].b, ps.b], [E.xt[tt].b])
                linear_tm(mT, 16, w_out, 0, nb * 512, 512, epi, bank0=6)
            load_ln(2)
            for tt in range(4):
                layer_norm_tile(E.xt[tt], E.lng, E.lnb, E.xt[tt])
            load_ln(4)
            ffn_block(E.xt, w_gu2, w_dn2, E.xt)
            for tt in range(4):
                S.dma("sp", E.xt[tt], out[t0 + tt * 128:t0 + (tt + 1) * 128, :], E.xt[tt][:], [E.xt[tt].b], [])

    with ExitStack() as es:
        ES[0] = es
        phase5()
        end_phase()
    ES[0] = None

    S.emit()
    return nc


def _rope_tables(pos):
    def tab(rot):
        inv = (1.0 / (np.float32(500000.0) ** (np.arange(0, rot, 2, dtype=np.float32) / np.float32(rot)))).astype(np.float32)
        ang = pos.astype(np.float32)[:, None] * inv[None, :]
        return np.cos(ang).astype(np.float32), np.sin(ang).astype(np.float32)
    c, s = tab(32)
    ci, si = tab(16)
    return np.concatenate([c, s, ci, si], axis=1).astype(np.float32)


def make_in_maps(inp):
    f = lambda a: np.ascontiguousarray(np.asarray(a, dtype=np.float32))
    lnp = np.stack([f(inp[k])[0] for k in ("ln1_g", "ln1_b", "ln2_g", "ln2_b", "ln3_g", "ln3_b")], 0)
    conv_wb = np.concatenate([f(inp["conv_w"])[0], f(inp["conv_b"])], 0)
    ssmv = np.concatenate([f(inp["dt_bias"]), f(inp["A_log"]), f(inp["D_skip"]), np.zeros((1, 64), np.float32)], 0)
    ii = np.arange(128)
    cst = np.zeros((128, 5, 128), np.float32)
    cst[:, 0, :] = np.eye(128)
    cst[:, 1, :] = (ii[:, None] <= ii[None, :])
    cst[:, 2, :] = 1.0
    cst[:, 3, :] = np.where(ii[None, :] <= ii[:, None], 0.0, NEG)
    shared = {
        "ffn1_w_gu": f(inp["ffn1_w_gu"])[0], "ffn1_w_down": f(inp["ffn1_w_down"])[0],
        "ffn2_w_gu": f(inp["ffn2_w_gu"])[0], "ffn2_w_down": f(inp["ffn2_w_down"])[0],
        "w_in": f(inp["w_in"])[0], "w_o_attn": f(inp["w_o_attn"])[0], "w_o_ssm": f(inp["w_o_ssm"])[0],
        "w_out": f(inp["w_out"])[0], "lnp": lnp, "conv_wb": conv_wb, "ssmv": ssmv,
        "normw": f(inp["ssm_norm_w"]), "cst": cst,
        "convT": np.ascontiguousarray(conv_wb.reshape(5, 48, 128).transpose(2, 1, 0)),
    }
    x = f(inp["x"])
    maps = []
    for c in range(8):
        b, j = c // 2, c % 2
        m = dict(shared)
        m["x"] = np.ascontiguousarray(x[b, j * T:(j + 1) * T, :])
        m["ropet"] = _rope_tables(np.arange(j * T, (j + 1) * T))
        fl = np.zeros((128, 2), np.float32)
        fl[:, 0] = float(j)
        fl[:, 1] = (float(j) - 1.0) * 1.0e30
        m["flg"] = fl
        maps.append(m)
    return maps


def kernel(**inputs):
    import os
    dbg = tuple(d for d in os.environ.get("KDEBUG", "").split(",") if d)
    nc = build_nc(debug=dbg)
    in_maps = make_in_maps(inputs)
    res = run_bass_kernel_spmd(nc, in_maps, core_ids=list(range(8)))
    outp = np.zeros((4, 4096, D), np.float32)
    for c in range(8):
        b, j = c // 2, c % 2
        outp[b, j * T:(j + 1) * T, :] = res.results[c]["out"]
    return outp
```

```python
from contextlib import ExitStack
import numpy as np
import concourse.bass as bass
import concourse.mybir as mybir
from concourse.bass_utils import run_bass_kernel_spmd

F32 = mybir.dt.float32
BF16 = mybir.dt.bfloat16
AF = mybir.ActivationFunctionType
ALU = mybir.AluOpType
AX = mybir.AxisListType

D = 2048
T = 2048
NT = 16
DFF = 5504
NF = 43
WIN = 18576
ALPHA = 2.0 ** 0.25
NEG = -1.0e30
OFF_Q, OFF_K, OFF_V, OFF_QI, OFF_KI, OFF_WI, OFF_Z, OFF_XBC, OFF_DT, OFF_G = 0, 2048, 2560, 3072, 4096, 4160, 4176, 8272, 14416, 14480
N_BISECT = 18
NCORES = [8]


class Stream:
    def __init__(self, name, sem, inc):
        self.name, self.sem, self.inc, self.n = name, sem, inc, 0


class Buf:
    __slots__ = ("w", "r")

    def __init__(self):
        self.w = None
        self.r = {}


class Sched:
    QUEUES = ("pe", "act", "dve", "pool", "sp")

    def __init__(self, nc):
        self.nc = nc
        self.q = {k: [] for k in self.QUEUES}
        self.streams = {}
        for k in ("pe", "act", "dve", "pool"):
            self.streams[k] = Stream(k, nc.alloc_semaphore("s_" + k), 1)
        self.free_dma = []
        self.ndma = 0
        self.known = {k: {} for k in self.QUEUES}

    def dma_stream(self, tile):
        if tile.ds is None:
            if self.free_dma:
                tile.ds = self.free_dma.pop()
            else:
                name = "d%d" % self.ndma
                self.ndma += 1
                tile.ds = Stream(name, self.nc.alloc_semaphore("s_" + name), 16)
                self.streams[name] = tile.ds
        return tile.ds

    def release(self, tiles):
        for t in tiles:
            if t.ds is not None:
                self.free_dma.append(t.ds)
                t.ds = None

    def cc_stream(self):
        name = "cc%d" % self.ndma
        self.ndma += 1
        st = Stream(name, self.nc.alloc_semaphore("s_" + name), 1)
        self.streams[name] = st
        return st

    def _wait(self, queue, st, n):
        if n <= 0 or self.known[queue].get(st.name, 0) >= n:
            return
        self.known[queue][st.name] = n
        val, sem = n * st.inc, st.sem
        self.q[queue].append(lambda eng: eng.wait_ge(sem, val))

    def op(self, queue, fn, reads=(), writes=(), stream=None):
        st = stream if isinstance(stream, Stream) else self.streams[stream or queue]
        deps = {}
        for b in reads:
            if b.w is not None and deps.get(b.w[0].name, (None, 0))[1] < b.w[1]:
                deps[b.w[0].name] = b.w
        for b in writes:
            if b.w is not None and deps.get(b.w[0].name, (None, 0))[1] < b.w[1]:
                deps[b.w[0].name] = b.w
            for d in b.r.values():
                if deps.get(d[0].name, (None, 0))[1] < d[1]:
                    deps[d[0].name] = d
        for s, n in deps.values():
            if queue == "pe" and s.name == "pe":
                continue
            self._wait(queue, s, n)
        st.n += 1
        me = (st, st.n)
        sem, inc = st.sem, st.inc
        self.q[queue].append(lambda eng: fn(eng).then_inc(sem, inc))
        for b in reads:
            b.r[st.name] = me
        for b in writes:
            b.w = me
            b.r = {}
        return me

    def barrier(self):
        for queue in self.QUEUES:
            for st in self.streams.values():
                if queue == "pe" and st.name == "pe":
                    continue
                self._wait(queue, st, st.n)

    def emit(self):
        with self.nc.Block() as block:
            @block.tensor
            def _(e):
                for f in self.q["pe"]:
                    f(e)

            @block.scalar
            def _(e):
                for f in self.q["act"]:
                    f(e)

            @block.vector
            def _(e):
                for f in self.q["dve"]:
                    f(e)

            @block.gpsimd
            def _(e):
                for f in self.q["pool"]:
                    f(e)

            @block.sync
            def _(e):
                for f in self.q["sp"]:
                    f(e)

    def mm(self, out, lhsT, rhs, start, stop, reads, writes):
        self.op("pe", lambda e: e.matmul(out, lhsT=lhsT, rhs=rhs, start=start, stop=stop), reads, writes)

    def tr(self, out, in_, ident, reads, writes):
        self.op("pe", lambda e: e.transpose(out, in_, ident), reads, writes)

    def act(self, out, in_, func, reads, writes, scale=None, bias=None, accum_out=None):
        kw = {}
        if scale is not None:
            kw["scale"] = scale
        if bias is not None:
            kw["bias"] = bias
        if accum_out is not None:
            kw["accum_out"] = accum_out
        self.op("act", lambda e: e.activation(out=out, in_=in_, func=func, **kw), reads, writes)

    def tt(self, q, out, in0, in1, op, reads, writes):
        self.op(q, lambda e: e.tensor_tensor(out=out, in0=in0, in1=in1, op=op), reads, writes)

    def ts(self, q, out, in0, s1, s2, op0, op1, reads, writes, accum_out=None):
        if op1 is None:
            self.op(q, lambda e: e.tensor_scalar(out=out, in0=in0, scalar1=s1, scalar2=None, op0=op0), reads, writes)
        elif accum_out is None:
            self.op(q, lambda e: e.tensor_scalar(out=out, in0=in0, scalar1=s1, scalar2=s2, op0=op0, op1=op1), reads, writes)
        else:
            self.op(q, lambda e: e.tensor_scalar(out=out, in0=in0, scalar1=s1, scalar2=s2, op0=op0, op1=op1, accum_out=accum_out), reads, writes)

    def stt(self, out, in0, scalar, in1, op0, op1, reads, writes):
        self.op("dve", lambda e: e.scalar_tensor_tensor(out=out, in0=in0, scalar=scalar, in1=in1, op0=op0, op1=op1), reads, writes)

    def copy(self, q, out, in_, reads, writes):
        if q == "act":
            self.op("act", lambda e: e.copy(out=out, in_=in_), reads, writes)
        else:
            self.op(q, lambda e: e.tensor_copy(out=out, in_=in_), reads, writes)

    def dma(self, q, tile, out, in_, reads, writes, slow=False):
        if slow:
            self.op(q, lambda e: e.dma_start(out=out, in_=in_, allow_slow_non_contiguous=True), reads, writes, self.dma_stream(tile))
        else:
            self.op(q, lambda e: e.dma_start(out=out, in_=in_), reads, writes, self.dma_stream(tile))


class Tile:
    def __init__(self, t):
        self.t = t
        self.b = Buf()
        self.ds = None

    def __getitem__(self, k):
        return self.t[k]


def build_nc(debug=()):
    nc = bass.Bass("TRN2", target_bir_lowering=False)

    def din(name, shape, dt=F32):
        return nc.dram_tensor(name, list(shape), dt, kind="ExternalInput").ap()

    x_in = din("x", [T, D])
    w_gu1, w_dn1 = din("ffn1_w_gu", [D, 2 * DFF]), din("ffn1_w_down", [DFF, D])
    w_gu2, w_dn2 = din("ffn2_w_gu", [D, 2 * DFF]), din("ffn2_w_down", [DFF, D])
    w_in = din("w_in", [D, WIN])
    w_oa, w_os, w_out = din("w_o_attn", [D, D]), din("w_o_ssm", [2 * D, D]), din("w_out", [D, D])
    lnp = din("lnp", [6, D])
    conv_wb = din("conv_wb", [5, 6144])
    convT = din("convT", [128, 48, 5])
    ssmv = din("ssmv", [4, 64])
    normw = din("normw", [1, 4096])
    ropet = din("ropet", [T, 48])
    cst = din("cst", [128, 5, 128])
    flg = din("flg", [128, 2])
    out = nc.dram_tensor("out", [T, D], F32, kind="ExternalOutput").ap()

    def dscr(name, shape, dt):
        if name in debug:
            return nc.dram_tensor(name, list(shape), dt, kind="ExternalOutput").ap()
        return nc.dram_tensor(name, list(shape), dt).ap()

    h1_s = dscr("h1_s", [T, D], F32)
    q_s = dscr("q_s", [T, D], F32)
    qi_s = dscr("qi_s", [T, 1024], F32)
    wi_s = dscr("wi_s", [T, 16], F32)
    kvg_in = dscr("kvg_in", [T, 1088], F32)
    kvg = [nc.dram_tensor("kvg%d" % i, [256, 2176], F32).ap() for i in range(8)]
    z_s = dscr("z_s", [T, 4096], BF16)
    xbc_s = dscr("xbc_s", [6, 6144], F32)
    halo_in = nc.dram_tensor("halo_in", [3, 6144], F32).ap()
    halo_g = nc.dram_tensor("halo_g", [6, 6144], F32).ap()
    dt_s = dscr("dt_s", [T, 64], F32)
    gT_s = dscr("gT_s", [4096, T], BF16)
    xc_s = dscr("xc_s", [T, 6144], F32)
    ypre_s = dscr("ypre_s", [T, 4096], F32)
    ct_s = nc.dram_tensor("ct_s", [NT * 8 * 128, 128], BF16).ap()
    st_in = nc.dram_tensor("st_in", [128, 4096], F32).ap()
    st_g = nc.dram_tensor("st_g", [256, 4096], F32).ap()
    attnT_s = dscr("attnT_s", [D, T], BF16)
    ssmT_s = dscr("ssmT_s", [4096, T], BF16)

    wscr = nc.dram_tensor("wscr", [128, 128, 16 * 512], BF16).ap()
    wscr2 = nc.dram_tensor("wscr2", [112, 128, 16 * 512], BF16).ap()

    S = Sched(nc)

    ES = [None]

    PH_TILES = []

    def sb(name, shape, dt):
        if ES[0] is None:
            return Tile(nc.alloc_sbuf_tensor(name, list(shape), dt))
        t = Tile(ES[0].enter_context(nc.sbuf_tensor(name, list(shape), dt)))
        PH_TILES.append(t)
        return t

    def end_phase():
        S.barrier()
        S.release(PH_TILES)
        del PH_TILES[:]

    cst_t = sb("cst_t", [128, 5, 128], F32)
    identb = sb("identb", [128, 128], BF16)
    onesb = sb("onesb", [128, 128], BF16)
    ub = sb("ub", [128, 128], BF16)
    flg_t = sb("flg_t", [128, 2], F32)
    S.dma("sp", cst_t, cst_t[:], cst, [], [cst_t.b])
    S.dma("sp", flg_t, flg_t[:], flg, [], [flg_t.b])
    S.copy("dve", identb[:], cst_t[:, 0, :], [cst_t.b], [identb.b])
    S.copy("dve", onesb[:], cst_t[:, 2, :], [cst_t.b], [onesb.b])
    S.copy("dve", ub[:], cst_t[:, 1, :], [cst_t.b], [ub.b])
    ident_f = cst_t[:, 0, :]
    U_f = cst_t[:, 1, :]
    ones_f = cst_t[:, 2, :]
    cmask_f = cst_t[:, 3, :]

    psum = [Tile(nc.alloc_psum_tensor("ps%d" % i, [128, 512], F32)) for i in range(8)]

    def bcast_rows(ap_row, n):
        return ap_row.broadcast(0, 128) if hasattr(ap_row, "broadcast") else ap_row


    class Env:
        pass

    E = Env()
    E.p5 = False

    def alloc_dense(tag):
        E.xT = sb("xT" + tag, [128, 16, 512], BF16)
        E.actT = sb("actT" + tag, [128, NF, 512], BF16)
        E.wsl = [sb("wsl%d" % i + tag, [128, 16, 512], BF16) for i in range(3)]
        E.xt = [sb("xt%d" % i + tag, [128, D], F32) for i in range(4)]
        E.yt = E.xt
        E.lng = sb("lng" + tag, [128, D], F32)
        E.lnb = sb("lnb" + tag, [128, D], F32)
        E.sg = [sb("sg%d" % i + tag, [128, 512], F32) for i in range(2)]
        E.stg = [sb("stg%d" % i + tag, [128, 512], F32) for i in range(3)]
        E.stgb = [sb("stgb%d" % i + tag, [128, 512], BF16) for i in range(3)]
        E.small = [sb("small%d" % i + tag, [128, 64], F32) for i in range(4)]
        E.stats = sb("stats" + tag, [128, 4, 6], F32)
        E.wslot = 0
        E.stgi = 0
        E.stgbi = 0

    def next_wsl():
        E.wslot = (E.wslot + 1) % 3
        return E.wsl[E.wslot]

    def next_stg():
        E.stgi = (E.stgi + 1) % 3
        return E.stg[E.stgi]

    def next_stgb():
        E.stgbi = (E.stgbi + 1) % 3
        return E.stgb[E.stgbi]

    def load_w(slab, w_ap, r0, nk, c0, ncols, col_off=0):
        idx = E.slab_idx
        E.slab_idx += 1
        dst = slab[:, 0:nk, col_off:col_off + ncols]
        if E.p5:
            assert P5LIST[idx] == (w_ap, r0, nk, c0, ncols, col_off), (idx, r0, nk, c0, ncols, col_off)
            scr = wscr2[idx].rearrange("p (k n) -> p k n", k=16)[:, 0:nk, col_off:col_off + ncols]
            S.dma("pool", slab, dst, scr, [], [slab.b])
            return
        scr = wscr[idx].rearrange("p (k n) -> p k n", k=16)[:, 0:nk, col_off:col_off + ncols]
        if E.blk == 0:
            src = w_ap[r0:r0 + nk * 128, c0:c0 + ncols].rearrange("(k p) n -> p k n", p=128)
            S.dma("pool", slab, dst, src, [], [slab.b])
            E.pending_stores.append((slab, scr, dst))
        else:
            S.dma("pool", slab, dst, scr, [], [slab.b])

    def p5_slabs():
        for sl in range(4):
            yield (w_oa, 0, 16, sl * 512, 512, 0)
            yield (w_os, 0, 16, sl * 512, 512, 0)
            yield (w_os, 2048, 16, sl * 512, 512, 0)
        for nb in range(4):
            yield (w_out, 0, 16, nb * 512, 512, 0)
        for s0 in range(0, NF, 2):
            nf = min(2, NF - s0)
            yield (w_gu2, 0, 16, s0 * 128, nf * 128, 0)
            yield (w_gu2, 0, 16, DFF + s0 * 128, nf * 128, 256)
        for nb in range(4):
            for s0 in range(0, NF, 4):
                yield (w_dn2, s0 * 128, min(4, NF - s0), nb * 512, 512, 0)

    P5LIST = list(p5_slabs())
    PRE = Tile(None)

    def precast(lo, hi):
        for idx in range(lo, min(hi, len(P5LIST))):
            w_ap, r0, nk, c0, ncols, col_off = P5LIST[idx]
            src = w_ap[r0:r0 + nk * 128, c0:c0 + ncols].rearrange("(k p) n -> p k n", p=128)
            scr = wscr2[idx].rearrange("p (k n) -> p k n", k=16)[:, 0:nk, col_off:col_off + ncols]
            S.dma("pool", PRE, scr, src, [], [])

    def flush_w():
        for slab, scr, dst in E.pending_stores:
            S.dma("sp", slab, scr, dst, [slab.b], [])
        del E.pending_stores[:]

    def begin_block(blk):
        E.blk = blk
        E.slab_idx = 0
        E.pending_stores = []

    def transpose_to_xT(tiles, dstT, ncol_chunks=16):
        k = 0
        for tt in range(4):
            for kc0 in range(0, ncol_chunks, 4):
                ps = psum[k % 2]
                k += 1
                for j in range(4):
                    S.tr(ps[:, j * 128:(j + 1) * 128], tiles[tt][:, (kc0 + j) * 128:(kc0 + j + 1) * 128], ident_f,
                         [tiles[tt].b, cst_t.b], [ps.b])
                q = "act" if k % 2 else "dve"
                S.copy(q, dstT[:, kc0:kc0 + 4, tt * 128:(tt + 1) * 128],
                       ps[:].rearrange("p (j t) -> p j t", j=4), [ps.b], [dstT.b])

    def layer_norm_tile(y, g_t, b_t, out_t):
        st = E.stats
        for c in range(4):
            S.op("dve", lambda e, c=c: e.bn_stats(st[:, c, :], y[:, c * 512:(c + 1) * 512]), [y.b], [st.b])
        mv = E.small[0]
        S.op("dve", lambda e: e.bn_aggr(mv[:, 0:2], st[:]), [st.b], [mv.b])
        S.ts("dve", mv[:, 2:3], mv[:, 1:2], 1e-5, None, ALU.add, None, [mv.b], [mv.b])
        S.act(mv[:, 3:4], mv[:, 2:3], AF.Sqrt, [mv.b], [mv.b])
        S.op("dve", lambda e: e.reciprocal(mv[:, 4:5], mv[:, 3:4]), [mv.b], [mv.b])
        S.ts("dve", out_t[:], y[:], mv[:, 0:1], mv[:, 4:5], ALU.subtract, ALU.mult, [y.b, mv.b], [out_t.b])
        S.tt("dve", out_t[:], out_t[:], g_t[:], ALU.mult, [out_t.b, g_t.b], [out_t.b])
        S.tt("dve", out_t[:], out_t[:], b_t[:], ALU.add, [out_t.b, b_t.b], [out_t.b])

    def load_ln(idx):
        S.dma("sp", E.lng, E.lng[:], lnp[idx:idx + 1, :].broadcast_to([128, D]), [], [E.lng.b])
        S.dma("sp", E.lnb, E.lnb[:], lnp[idx + 1:idx + 2, :].broadcast_to([128, D]), [], [E.lnb.b])

    def ffn_block(xin, w_gu, w_dn, yout):
        transpose_to_xT(xin, E.xT)
        for tt in range(4):
            S.op("act", lambda e, tt=tt: e.mul(xin[tt][:], xin[tt][:], ALPHA), [xin[tt].b], [xin[tt].b])
        for s0 in range(0, NF, 2):
            nf = min(2, NF - s0)
            slab = next_wsl()
            load_w(slab, w_gu, 0, 16, s0 * 128, nf * 128, 0)
            load_w(slab, w_gu, 0, 16, DFF + s0 * 128, nf * 128, 256)
            flush_w()
            for m in range(nf):
                f = s0 + m
                pg, pu = psum[2 + f % 2], psum[4 + f % 2]
                for kc in range(16):
                    S.mm(pg[:], slab[:, kc, m * 128:(m + 1) * 128], E.xT[:, kc, :], kc == 0, kc == 15, [slab.b, E.xT.b], [pg.b])
                for kc in range(16):
                    S.mm(pu[:], slab[:, kc, 256 + m * 128:256 + (m + 1) * 128], E.xT[:, kc, :], kc == 0, kc == 15, [slab.b, E.xT.b], [pu.b])
                sg = E.sg[f % 2]
                S.act(sg[:], pg[:], AF.Silu, [pg.b], [sg.b])
                S.tt("dve", E.actT[:, f, :], sg[:], pu[:], ALU.mult, [sg.b, pu.b], [E.actT.b])
        for nb in range(4):
            banks = [psum[(nb % 2) * 4 + tt] for tt in range(4)]
            for s0 in range(0, NF, 4):
                nf = min(4, NF - s0)
                slab = next_wsl()
                load_w(slab, w_dn, s0 * 128, nf, nb * 512, 512)
                flush_w()
                for m in range(nf):
                    f = s0 + m
                    for tt in range(4):
                        S.mm(banks[tt][:], E.actT[:, f, tt * 128:(tt + 1) * 128], slab[:, m, :], f == 0, f == NF - 1,
                             [slab.b, E.actT.b], [banks[tt].b])
            for tt in range(4):
                S.stt(yout[tt][:, nb * 512:(nb + 1) * 512], banks[tt][:], 0.5, xin[tt][:, nb * 512:(nb + 1) * 512],
                      ALU.mult, ALU.add, [banks[tt].b, xin[tt].b], [yout[tt].b])
        for tt in range(4):
            layer_norm_tile(yout[tt], E.lng, E.lnb, yout[tt])

    def linear_tm(xT, nk, w_ap, r0, c0, ncols, epilogue, bank0=0):
        slab = next_wsl()
        load_w(slab, w_ap, r0, nk, c0, ncols)
        flush_w()
        for tt in range(4):
            ps = psum[bank0 + tt % 2]
            for kc in range(nk):
                S.mm(ps[:, 0:ncols], xT[:, kc, tt * 128:(tt + 1) * 128], slab[:, kc, 0:ncols], kc == 0, kc == nk - 1,
                     [slab.b, xT.b], [ps.b])
            epilogue(tt, ps)

    def linear_fm(xT, nk, w_ap, r0, c0, nchunks, epilogue, bank0=2, acc=None):
        slab = next_wsl()
        load_w(slab, w_ap, r0, nk, c0, nchunks * 128)
        flush_w()
        for m in range(nchunks):
            ps = psum[bank0 + m % 2]
            for kc in range(nk):
                S.mm(ps[:], slab[:, kc, m * 128:(m + 1) * 128], xT[:, kc, :], kc == 0, kc == nk - 1, [slab.b, xT.b], [ps.b])
            epilogue(m, ps)

    rope_t = sb("rope_t", [128, NT, 48], F32)
    S.dma("sp", rope_t, rope_t[:], ropet.rearrange("(n p) c -> p n c", p=128), [], [rope_t.b])
    dtb_t = sb("dtb_t", [128, 4, 64], F32)
    S.dma("sp", dtb_t, dtb_t[:], ssmv.unsqueeze(0).broadcast_to([128, 4, 64]), [], [dtb_t.b])
    ropetmp = sb("ropetmp", [128, 4, 8 * 16], F32)

    def rope_epi(ps, nh, hd, half, ti, cofs, dst_tile):
        n = nh * hd
        S.copy("act", dst_tile[:, 0:n], ps[:, 0:n], [ps.b], [dst_tile.b])
        dv = dst_tile[:, 0:n].rearrange("p (h d) -> p h d", h=nh)
        cos = rope_t[:, ti, cofs:cofs + half].unsqueeze(1).broadcast_to([128, nh, half])
        sin = rope_t[:, ti, cofs + half:cofs + 2 * half].unsqueeze(1).broadcast_to([128, nh, half])
        x1, x2 = dv[:, :, 0:half], dv[:, :, half:2 * half]
        tmp = [ropetmp[:, j, 0:nh * half].rearrange("p (h c) -> p h c", h=nh) for j in range(4)]
        S.tt("dve", tmp[0], x1, cos, ALU.mult, [dst_tile.b, rope_t.b], [ropetmp.b])
        S.tt("dve", tmp[1], x2, sin, ALU.mult, [dst_tile.b, rope_t.b], [ropetmp.b])
        S.tt("dve", tmp[2], x2, cos, ALU.mult, [dst_tile.b, rope_t.b], [ropetmp.b])
        S.tt("dve", tmp[3], x1, sin, ALU.mult, [dst_tile.b, rope_t.b], [ropetmp.b])
        S.tt("dve", x1, tmp[0], tmp[1], ALU.subtract, [ropetmp.b], [dst_tile.b])
        S.tt("dve", x2, tmp[2], tmp[3], ALU.add, [ropetmp.b], [dst_tile.b])

    P1 = Env()

    def phase1():
        alloc_dense("p1")
        load_ln(0)
        P1.hal = sb("hal", [128, 48, 4], F32)
        P1.cwT = sb("cwT", [128, 48, 5], F32)
        P1.xcin = [sb("xcin%d" % i, [128, 515], F32) for i in range(3)]
        P1.acc = [sb("cacc%d" % i, [128, 512], F32) for i in range(3)]
        P1.so = [sb("cso%d" % i, [128, 512], F32) for i in range(3)]
        P1.k = 0
        P1.pending = None
        S.op("pool", lambda e: e.memset(P1.hal[:], 0.0), [], [P1.hal.b])
        S.dma("sp", P1.cwT, P1.cwT[:], convT, [], [P1.cwT.b])
        for blk in range(1 if "blk1" in debug else 4):
            t0 = blk * 512
            begin_block(blk)
            for tt in range(4):
                S.dma("sp", E.xt[tt], E.xt[tt][:], x_in[t0 + tt * 128:t0 + (tt + 1) * 128, :], [], [E.xt[tt].b])
            ffn_block(E.xt, w_gu1, w_dn1, E.yt)
            for tt in range(4):
                S.dma("sp", E.yt[tt], h1_s[t0 + tt * 128:t0 + (tt + 1) * 128, :], E.yt[tt][:], [E.yt[tt].b], [])
            if "nowin" in debug:
                continue
            transpose_to_xT(E.yt, E.xT)
            h1T = E.xT

            def rows(tt):
                return slice(t0 + tt * 128, t0 + (tt + 1) * 128)

            def on(name):
                segs = [d for d in debug if d.startswith("seg_")]
                return (not segs) or ("seg_" + name in segs)

            for sl in range(4 if on("q") else 0):
                def epi(tt, ps, sl=sl):
                    st = next_stg()
                    rope_epi(ps, 4, 128, 16, blk * 4 + tt, 0, st)
                    S.dma("sp", st, q_s[rows(tt), sl * 512:(sl + 1) * 512], st[:], [st.b], [])
                linear_tm(h1T, 16, w_in, 0, OFF_Q + sl * 512, 512, epi)

            def epi_k(tt, ps):
                st = next_stg()
                rope_epi(ps, 4, 128, 16, blk * 4 + tt, 0, st)
                S.dma("sp", st, kvg_in[rows(tt), 0:512], st[:], [st.b], [])
            if on("k"):
                linear_tm(h1T, 16, w_in, 0, OFF_K, 512, epi_k)

            def epi_v(tt, ps):
                st = next_stg()
                S.copy("act", st[:], ps[:], [ps.b], [st.b])
                S.dma("sp", st, kvg_in[rows(tt), 512:1024], st[:], [st.b], [])
            if on("v"):
                linear_tm(h1T, 16, w_in, 0, OFF_V, 512, epi_v)

            for sl in range(2 if on("qi") else 0):
                def epi(tt, ps, sl=sl):
                    st = next_stg()
                    rope_epi(ps, 8, 64, 8, blk * 4 + tt, 32, st)
                    S.dma("sp", st, qi_s[rows(tt), sl * 512:(sl + 1) * 512], st[:], [st.b], [])
                linear_tm(h1T, 16, w_in, 0, OFF_QI + sl * 512, 512, epi)

            def epi_kw(tt, ps):
                st = next_stg()
                rope_epi(ps, 1, 64, 8, blk * 4 + tt, 32, st)
                S.dma("sp", st, kvg_in[rows(tt), 1024:1088], st[:, 0:64], [st.b], [])
                sf = next_stg()
                S.op("act", lambda e: e.mul(sf[:, 0:16], ps[:, 64:80], 0.125 * 0.25), [ps.b], [sf.b])
                S.dma("sp", sf, wi_s[rows(tt), :], sf[:, 0:16], [sf.b], [])
            if on("kw"):
                linear_tm(h1T, 16, w_in, 0, OFF_KI, 80, epi_kw)

            for sl in range(8 if on("z") else 0):
                def epi(tt, ps, sl=sl):
                    st = next_stgb()
                    S.copy("act" if tt % 2 else "dve", st[:], ps[:], [ps.b], [st.b])
                    S.dma("sp", st, z_s[rows(tt), sl * 512:(sl + 1) * 512], st[:], [st.b], [])
                linear_tm(h1T, 16, w_in, 0, OFF_Z + sl * 512, 512, epi)

            for sl in range(12 if on("xbc") else 0):
                def epi(m, ps, sl=sl):
                    ch = sl * 4 + m
                    xin, acc, so = P1.xcin[P1.k % 3], P1.acc[P1.k % 3], P1.so[P1.k % 3]
                    P1.k += 1
                    S.copy("act", xin[:, 3:515], ps[:], [ps.b], [xin.b])
                    S.copy("act", xin[:, 0:3], P1.hal[:, ch, 0:3], [P1.hal.b], [xin.b])
                    S.copy("act", P1.hal[:, ch, 0:3], xin[:, 512:515], [xin.b], [P1.hal.b])
                    cols = slice(ch * 128, (ch + 1) * 128)
                    if blk == 0:
                        S.dma("sp", xin, xbc_s[3:6, cols].rearrange("t c -> c t"), xin[:, 3:6], [xin.b], [], slow=True)
                    if blk == 3:
                        S.dma("sp", xin, halo_in[0:3, cols].rearrange("t c -> c t"), xin[:, 512:515], [xin.b], [], slow=True)
                    cw = P1.cwT
                    S.ts("dve", acc[:], xin[:, 0:512], cw[:, ch, 0:1], cw[:, ch, 4:5], ALU.mult, ALU.add, [xin.b, cw.b], [acc.b])
                    for i in (1, 2, 3):
                        S.stt(acc[:], xin[:, i:i + 512], cw[:, ch, i:i + 1], acc[:], ALU.mult, ALU.add, [xin.b, cw.b, acc.b], [acc.b])
                    S.act(acc[:], acc[:], AF.Silu, [acc.b], [acc.b])

                    def tail(acc=acc, so=so, cols=cols, kk=P1.k):
                        pst = psum[kk % 2]
                        for j in range(4):
                            S.tr(pst[:, j * 128:(j + 1) * 128], acc[:, j * 128:(j + 1) * 128], ident_f, [acc.b, cst_t.b], [pst.b])
                        S.copy("dve" if kk % 2 else "act", so[:], pst[:], [pst.b], [so.b])
                        S.dma("sp", so, xc_s[t0:t0 + 512, cols].rearrange("(j p) c -> p j c", p=128),
                              so[:].rearrange("p (j c) -> p j c", j=4), [so.b], [])
                    if P1.pending is not None:
                        P1.pending()
                    P1.pending = tail
                linear_fm(h1T, 16, w_in, 0, OFF_XBC + sl * 512, 4, epi)
            if P1.pending is not None:
                P1.pending()
                P1.pending = None

            def epi_dt(tt, ps):
                sf = next_stg()
                S.tt("dve", sf[:, 0:64], ps[:, 0:64], dtb_t[:, 0, :], ALU.add, [ps.b, dtb_t.b], [sf.b])
                S.act(sf[:, 64:128], sf[:, 0:64], AF.Exp, [sf.b], [sf.b])
                S.act(sf[:, 128:192], sf[:, 64:128], AF.Ln, [sf.b], [sf.b], bias=1.0)
                S.dma("sp", sf, dt_s[rows(tt), :], sf[:, 128:192], [sf.b], [])
            if on("dt"):
                linear_tm(h1T, 16, w_in, 0, OFF_DT, 64, epi_dt)

            for sl in range(8 if on("g") else 0):
                def epi(m, ps, sl=sl):
                    st = next_stgb()
                    S.act(st[:], ps[:], AF.Sigmoid, [ps.b], [st.b])
                    r0 = (sl * 4 + m) * 128
                    S.dma("sp", st, gT_s[r0:r0 + 128, t0:t0 + 512], st[:], [st.b], [])
                linear_fm(h1T, 16, w_in, 0, OFF_G + sl * 512, 4, epi)

    with ExitStack() as es:
        ES[0] = es
        phase1()
        end_phase()
    ES[0] = None

    if "stop1" in debug:
        S.emit()
        return nc

    RG = [[2 * i, 2 * i + 1] for i in range(NCORES[0] // 2)]

    def collective(in_ap, out_ap):
        st = S.cc_stream()
        bb = Buf()
        S.op("pool", lambda e: e.collective_compute("AllGather", ALU.bypass, replica_groups=RG, ins=[in_ap], outs=[out_ap]),
             [], [bb], st)

    def phase2():
        for qq in range(8):
            collective(kvg_in[qq * 256:(qq + 1) * 256, :].rearrange("(p n) c -> p (n c)", p=128), kvg[qq])
        collective(halo_in, halo_g)
        S.barrier()
        hl = sb("hl", [3, 6144], F32)
        S.dma("sp", hl, hl[:], halo_g[0:3, :], [], [hl.b])
        S.ts("dve", hl[:], hl[:], flg_t[0:3, 0:1], None, ALU.mult, None, [hl.b, flg_t.b], [hl.b])
        S.dma("sp", hl, xbc_s[0:3, :], hl[:], [hl.b], [])
        S.barrier()
        CW = 1024
        wt = [sb("cw%d" % i, [3, 5, CW], F32) for i in range(2)]
        xs = [sb("cx%d" % i, [3, 4, CW], F32) for i in range(2)]
        ys = [sb("cy%d" % i, [3, CW], F32) for i in range(2)]
        for cb in range(6144 // CW):
            w, x4, y1 = wt[cb % 2], xs[cb % 2], ys[cb % 2]
            cs = slice(cb * CW, (cb + 1) * CW)
            S.dma("sp", w, w[:], conv_wb[:, cs].unsqueeze(0).broadcast_to([3, 5, CW]), [], [w.b])
            src = bass.AP(xbc_s.tensor, xbc_s[0:1, cs].offset, [[6144, 3], [6144, 4], [1, CW]])
            S.dma("sp", x4, x4[:], src, [], [x4.b])
            S.tt("dve", x4[:], x4[:], w[:, 0:4, :], ALU.mult, [x4.b, w.b], [x4.b])
            S.tt("dve", x4[:, 0:2, :], x4[:, 0:2, :], x4[:, 2:4, :], ALU.add, [x4.b], [x4.b])
            S.tt("dve", y1[:], x4[:, 0, :], x4[:, 1, :], ALU.add, [x4.b], [y1.b])
            S.tt("dve", y1[:], y1[:], w[:, 4, :], ALU.add, [y1.b, w.b], [y1.b])
            S.act(y1[:], y1[:], AF.Silu, [y1.b], [y1.b])
            S.dma("sp", y1, xc_s[0:3, cs], y1[:], [y1.b], [])

    with ExitStack() as es:
        ES[0] = es
        phase2()
        end_phase()
    ES[0] = None
    if "stop2" in debug:
        S.emit()
        return nc

    def bc3(ap2d, n_inner):
        return ap2d.unsqueeze(2).broadcast_to([128, ap2d.shape[1], n_inner])

    def phase3():
        xc = [sb("s_xc%d" % i, [128, 4096], F32) for i in range(2)]
        bc = [sb("s_bc%d" % i, [128, 2048], F32) for i in range(2)]
        dtt = [sb("s_dt%d" % i, [128, 64], F32) for i in range(2)]
        xdt = sb("s_xdt", [128, 4096], BF16)
        xw = sb("s_xw", [128, 4096], BF16)
        btm = sb("s_btm", [128, 1024], BF16)
        st_f = sb("s_stf", [128, 4096], F32)
        st_b = sb("s_stb", [128, 4096], BF16)
        BT = sb("s_BT", [128, 8, 128], BF16)
        CT = [sb("s_CT%d" % i, [128, 8, 128], BF16) for i in range(2)]
        sm = sb("s_sm", [128, 12, 64], F32)
        etot = sb("s_etot", [128, NT, 64], F32)
        cbm = [sb("s_cbm%d" % i, [128, 128], F32) for i in range(2)]
        Zw = [sb("s_Zw%d" % i, [128, 8, 128], F32) for i in range(2)]
        Eww = [sb("s_Ew%d" % i, [128, 1024], F32) for i in range(2)]
        Mww = [sb("s_Mw%d" % i, [128, 8, 128], BF16) for i in range(2)]
        yo = [sb("s_yo%d" % i, [128, 512], F32) for i in range(2)]
        dx = [sb("s_dx%d" % i, [128, 512], F32) for i in range(2)]
        A_, DTA, A_C, NEGA, EA, W_, DEC, AOFF, TMP, TMP2 = range(10)
        S.act(sm[:, A_, :], dtb_t[:, 1, :], AF.Exp, [dtb_t.b], [sm.b])
        S.ts("dve", sm[:, A_, :], sm[:, A_, :], -1.0, None, ALU.mult, None, [sm.b], [sm.b])
        S.op("pool", lambda e: e.memset(sm[:, AOFF, :], 0.0), [], [sm.b])
        S.op("pool", lambda e: e.memset(st_f[:], 0.0), [], [st_f.b])
        S.op("pool", lambda e: e.memset(st_b[:], 0.0), [], [st_b.b])
        hk = 0
        for c in range(NT):
            X, BC, DT = xc[c % 2], bc[c % 2], dtt[c % 2]
            r = slice(c * 128, (c + 1) * 128)
            precast(c * 7, (c + 1) * 7)
            S.dma("sp", X, X[:], xc_s[r, 0:4096], [], [X.b])
            S.dma("sp", BC, BC[:], xc_s[r, 4096:6144], [], [BC.b])
            S.dma("sp", DT, DT[:], dt_s[r, :], [], [DT.b])
            S.tt("dve", sm[:, DTA, :], DT[:], sm[:, A_, :], ALU.mult, [DT.b, sm.b], [sm.b])
            pa = psum[0]
            S.mm(pa[:, 0:64], U_f, sm[:, DTA, :], True, True, [cst_t.b, sm.b], [pa.b])
            S.mm(pa[:, 64:128], ones_f, sm[:, DTA, :], True, True, [cst_t.b, sm.b], [pa.b])
            S.copy("act", sm[:, A_C, :], pa[:, 0:64], [pa.b], [sm.b])
            S.ts("dve", sm[:, NEGA, :], pa[:, 0:64], -1.0, None, ALU.mult, None, [pa.b], [sm.b])
            S.act(sm[:, EA, :], pa[:, 0:64], AF.Exp, [pa.b], [sm.b])
            S.tt("dve", sm[:, TMP, :], pa[:, 64:128], sm[:, A_C, :], ALU.subtract, [pa.b, sm.b], [sm.b])
            S.act(sm[:, W_, :], sm[:, TMP, :], AF.Exp, [sm.b], [sm.b])
            S.act(sm[:, DEC, :], pa[:, 64:128], AF.Exp, [pa.b], [sm.b])
            S.tt("dve", sm[:, TMP2, :], sm[:, A_C, :], sm[:, AOFF, :], ALU.add, [sm.b], [sm.b])
            S.act(etot[:, c, :], sm[:, TMP2, :], AF.Exp, [sm.b], [etot.b])
            S.tt("dve", sm[:, AOFF, :], sm[:, AOFF, :], pa[:, 64:128], ALU.add, [sm.b, pa.b], [sm.b])
            x3 = X[:].rearrange("p (h d) -> p h d", h=64)
            S.tt("dve", xdt[:].rearrange("p (h d) -> p h d", h=64), x3, bc3(DT[:], 64), ALU.mult, [X.b, DT.b], [xdt.b])
            S.tt("dve", xw[:].rearrange("p (h d) -> p h d", h=64), xdt[:].rearrange("p (h d) -> p h d", h=64),
                 bc3(sm[:, W_, :], 64), ALU.mult, [xdt.b, sm.b], [xw.b])
            S.copy("act", btm[:], BC[:, 0:1024], [BC.b], [btm.b])
            ct = CT[c % 2]

            def front(g, X=X, BC=BC, ct=ct):
                pt = psum[1]
                S.tr(pt[:, 0:128], BC[:, g * 128:(g + 1) * 128], ident_f, [BC.b, cst_t.b], [pt.b])
                S.tr(pt[:, 128:256], BC[:, 1024 + g * 128:1024 + (g + 1) * 128], ident_f, [BC.b, cst_t.b], [pt.b])
                S.copy("act", BT[:, g, :], pt[:, 0:128], [pt.b], [BT.b])
                S.copy("act", ct[:, g, :], pt[:, 128:256], [pt.b], [ct.b])
                pcb = psum[1]
                S.mm(pcb[:, 256:384], BT[:, g, :], ct[:, g, :], True, True, [BT.b, ct.b], [pcb.b])
                cb_ = cbm[g % 2]
                S.tt("dve", cb_[:], pcb[:, 256:384], U_f, ALU.mult, [pcb.b, cst_t.b], [cb_.b])
                pyo = psum[3]
                S.mm(pyo[:], ct[:, g, :], st_b[:, g * 512:(g + 1) * 512], True, True, [ct.b, st_b.b], [pyo.b])
                y1 = yo[g % 2]
                S.copy("act", y1[:], pyo[:], [pyo.b], [y1.b])
                Zg, Ew, Mw = Zw[g % 2], Eww[g % 2], Mww[g % 2]
                S.tt("pool", Zg[:], U_f.unsqueeze(1).broadcast_to([128, 8, 128]), bc3(sm[:, DTA, g * 8:(g + 1) * 8], 128), ALU.mult,
                     [cst_t.b, sm.b], [Zg.b])
                pab0, pab1 = psum[4], psum[5]
                S.mm(pab0[:], ones_f, Zg[:, 0:4, :], True, True, [cst_t.b, Zg.b], [pab0.b])
                S.mm(pab1[:], ones_f, Zg[:, 4:8, :], True, True, [cst_t.b, Zg.b], [pab1.b])
                S.copy("act", Ew[:, 0:512], pab0[:], [pab0.b], [Ew.b])
                S.copy("act", Ew[:, 512:1024], pab1[:], [pab1.b], [Ew.b])
                E3 = Ew[:].rearrange("p (h l) -> p h l", h=8)
                S.tt("dve", E3, E3, bc3(sm[:, NEGA, g * 8:(g + 1) * 8], 128), ALU.add, [Ew.b, sm.b], [Ew.b])
                S.act(Ew[:], Ew[:], AF.Exp, [Ew.b], [Ew.b])
                S.stt(Mw[:], E3, 1.0, cb_[:].unsqueeze(1).broadcast_to([128, 8, 128]), ALU.min, ALU.mult, [Ew.b, cb_.b], [Mw.b])
                psu = psum[7] if g % 2 else psum[2]
                S.mm(psu[:], btm[:, g * 128:(g + 1) * 128], xw[:, g * 512:(g + 1) * 512], True, True, [btm.b, xw.b], [psu.b])

            def back(g, X=X, r=r):
                pyd = psum[6]
                Mw = Mww[g % 2]
                for hh in range(8):
                    h = g * 8 + hh
                    S.mm(pyd[:, hh * 64:(hh + 1) * 64], Mw[:, hh, :], xdt[:, h * 64:(h + 1) * 64], True, True, [Mw.b, xdt.b], [pyd.b])
                y1, d1 = yo[g % 2], dx[g % 2]
                gs = slice(g * 512, (g + 1) * 512)
                S.tt("dve", y1[:].rearrange("p (h d) -> p h d", h=8), y1[:].rearrange("p (h d) -> p h d", h=8),
                     bc3(sm[:, EA, g * 8:(g + 1) * 8], 64), ALU.mult, [y1.b, sm.b], [y1.b])
                S.tt("pool", d1[:].rearrange("p (h d) -> p h d", h=8), X[:, gs].rearrange("p (h d) -> p h d", h=8),
                     bc3(dtb_t[:, 2, g * 8:(g + 1) * 8], 64), ALU.mult, [X.b, dtb_t.b], [d1.b])
                S.tt("pool", d1[:], d1[:], y1[:], ALU.add, [d1.b, y1.b], [d1.b])
                S.tt("dve", d1[:], d1[:], pyd[:], ALU.add, [d1.b, pyd.b], [d1.b])
                S.dma("sp", d1, ypre_s[r, gs], d1[:], [d1.b], [])
                psu = psum[7] if g % 2 else psum[2]
                S.tt("dve", st_f[:, gs].rearrange("p (h d) -> p h d", h=8), st_f[:, gs].rearrange("p (h d) -> p h d", h=8),
                     bc3(sm[:, DEC, g * 8:(g + 1) * 8], 64), ALU.mult, [st_f.b, sm.b], [st_f.b])
                S.tt("dve", st_f[:, gs], st_f[:, gs], psu[:], ALU.add, [st_f.b, psu.b], [st_f.b])
                S.copy("act", st_b[:, gs], st_f[:, gs], [st_f.b], [st_b.b])

            front(0)
            for g in range(8):
                if g + 1 < 8:
                    front(g + 1)
                back(g)
            S.dma("sp", ct, ct_s[c * 1024:(c + 1) * 1024, :].rearrange("(g n) l -> n g l", g=8), ct[:], [ct.b], [])
        S.dma("sp", st_f, st_in, st_f[:], [st_f.b], [])
        S.barrier()
        collective(st_in, st_g)
        S.barrier()
        S.dma("sp", st_f, st_f[:], st_g[0:128, :], [], [st_f.b])
        S.ts("dve", st_b[:], st_f[:], flg_t[:, 0:1], None, ALU.mult, None, [st_f.b, flg_t.b], [st_b.b])
        nw = xc[0]
        S.dma("sp", nw, nw[:], normw.broadcast_to([128, 4096]), [], [nw.b])
        YP = xc[1]
        zt = [xdt, xw]
        ssm_f = sb("s_ssmf", [128, 4096], F32)
        ssb = [sb("s_ssb%d" % i, [128, 4, 128], BF16) for i in range(2)]
        yg = [sb("s_yg%d" % i, [128, 512], F32) for i in range(8)]
        dg = [sb("s_dg%d" % i, [128, 512], F32) for i in range(8)]
        sqs = sb("s_sqs", [128, 8, 4], F32)
        k = 0
        for c in range(NT):
            r = slice(c * 128, (c + 1) * 128)
            ct = CT[c % 2]
            Zc = zt[c % 2]
            S.dma("sp", ct, ct[:], ct_s[c * 1024:(c + 1) * 1024, :].rearrange("(g n) l -> n g l", g=8), [], [ct.b])
            S.dma("sp", YP, YP[:], ypre_s[r, :], [], [YP.b])
            S.dma("sp", Zc, Zc[:], z_s[r, :], [], [Zc.b])
            G = range(8)
            gsl = [slice(g * 512, (g + 1) * 512) for g in G]
            for g in G:
                S.mm(psum[g][:], ct[:, g, :], st_b[:, gsl[g]], True, True, [ct.b, st_b.b], [psum[g].b])
            for g in G:
                S.copy("act", yg[g][:], psum[g][:], [psum[g].b], [yg[g].b])
            for g in G:
                y1 = yg[g]
                S.tt("dve", y1[:].rearrange("p (h d) -> p h d", h=8), y1[:].rearrange("p (h d) -> p h d", h=8),
                     bc3(etot[:, c, g * 8:(g + 1) * 8], 64), ALU.mult, [y1.b, etot.b], [y1.b])
                S.tt("dve", y1[:], y1[:], YP[:, gsl[g]], ALU.add, [y1.b, YP.b], [y1.b])
            for g in G:
                S.act(dg[g][:], Zc[:, gsl[g]], AF.Silu, [Zc.b], [dg[g].b])
            for g in G:
                S.tt("pool", yg[g][:], yg[g][:], dg[g][:], ALU.mult, [yg[g].b, dg[g].b], [yg[g].b])
            for g in G:
                S.act(dg[g][:], yg[g][:], AF.Square, [yg[g].b], [dg[g].b, sqs.b], accum_out=sqs[:, g, 0:1])
            S.ts("dve", sqs[:, :, 1], sqs[:, :, 0], 1.0 / 512.0, 1e-5, ALU.mult, ALU.add, [sqs.b], [sqs.b])
            S.act(sqs[:, :, 2], sqs[:, :, 1], AF.Sqrt, [sqs.b], [sqs.b])
            S.op("dve", lambda e: e.reciprocal(sqs[:, :, 3], sqs[:, :, 2]), [sqs.b], [sqs.b])
            for g in G:
                S.stt(ssm_f[:, gsl[g]], yg[g][:], sqs[:, g, 3:4], nw[:, gsl[g]], ALU.mult, ALU.mult, [yg[g].b, sqs.b, nw.b], [ssm_f.b])
            for kc0 in range(0, 32, 4):
                ps = psum[2 + k % 2]
                sbt = ssb[k % 2]
                k += 1
                for j in range(4):
                    S.tr(ps[:, j * 128:(j + 1) * 128], ssm_f[:, (kc0 + j) * 128:(kc0 + j + 1) * 128], ident_f, [ssm_f.b, cst_t.b], [ps.b])
                S.copy("act" if k % 2 else "dve", sbt[:], ps[:].rearrange("p (j t) -> p j t", j=4), [ps.b], [sbt.b])
                S.dma("sp", sbt, ssmT_s[kc0 * 128:(kc0 + 4) * 128, r].rearrange("(j p) t -> p j t", p=128), sbt[:], [sbt.b], [])

    E_small = [sb("e_small%d" % i, [128, 8], F32) for i in range(2)]
    with ExitStack() as es:
        ES[0] = es
        phase3()
        end_phase()
    ES[0] = None
    if "stop3" in debug:
        S.emit()
        return nc

    def phase4():
        kT = sb("a_kT", [128, 4, 4096], BF16)
        V = sb("a_V", [128, 32, 512], BF16)
        kiT2 = sb("a_kiT", [128, 4096], BF16)
        kst = [sb("a_kst%d" % i, [128, 1088], F32) for i in range(2)]
        kid = [sb("a_kid%d" % i, [128, 128], F32) for i in range(2)]
        wi_t = sb("a_wi", [128, NT, 16], F32)
        S.dma("sp", wi_t, wi_t[:], wi_s.rearrange("(n p) h -> p n h", p=128), [], [wi_t.b])
        k = 0
        for b in range(32):
            ks, kd = kst[b % 2], kid[b % 2]
            src = (kvg[b // 2].rearrange("p (n c) -> (p n) c", n=2)[(b % 2) * 128:(b % 2) * 128 + 128, :] if b < 16
                   else kvg_in[(b - 16) * 128:(b - 15) * 128, :])
            S.dma("sp", ks, ks[:], src, [], [ks.b])
            S.copy("dve", kd[:, 0:64], ks[:, 1024:1088], [ks.b], [kd.b])
            S.copy("dve", kd[:, 64:128], ks[:, 1024:1088], [ks.b], [kd.b])
            S.copy("act", V[:, b, :], ks[:, 512:1024], [ks.b], [V.b])
            ps = psum[k % 2]
            k += 1
            for j in range(4):
                S.tr(ps[:, j * 128:(j + 1) * 128], ks[:, j * 128:(j + 1) * 128], ident_f, [ks.b, cst_t.b], [ps.b])
            S.copy("act", kT[:, :, b * 128:(b + 1) * 128], ps[:].rearrange("p (j t) -> p j t", j=4), [ps.b], [kT.b])
            ps2 = psum[2 + k % 2]
            S.tr(ps2[:, 0:128], kd[:], ident_f, [kd.b, cst_t.b], [ps2.b])
            S.copy("dve", kiT2[:, b * 128:(b + 1) * 128], ps2[:, 0:128], [ps2.b], [kiT2.b])

        qf = sb("a_qf", [128, 2048], F32)
        qif = sb("a_qif", [128, 1024], F32)
        qTs = [sb("a_qT%d" % i, [128, 16, 128], BF16) for i in range(2)]
        qiT = sb("a_qiT", [128, 8, 128], BF16)
        score = sb("a_score", [128, 4096], F32)
        maskf = sb("a_maskf", [128, 4096], F32)
        maskTs = [sb("a_maskT%d" % i, [128, 32, 128], BF16) for i in range(2)]
        rl = [sb("a_rl%d" % i, [128, 512], F32) for i in range(2)]
        pt = [sb("a_p%d" % i, [128, 512], BF16) for i in range(3)]
        pm = [sb("a_pm%d" % i, [128, 512], BF16) for i in range(3)]
        bs = sb("a_bs", [128, 8], F32)
        rinv = sb("a_rinv", [128, 512], F32)
        ost = [sb("a_ost%d" % i, [128, 512], BF16) for i in range(2)]
        LO, HI, MID, CNT, GC, W0 = range(6)
        SCALE = 128.0 ** -0.5
        kk = [k]

        def prep_index_bisect(i):
            r = slice(i * 128, (i + 1) * 128)
            qT = qTs[i % 2]
            S.dma("sp", qf, qf[:], q_s[r, :], [], [qf.b])
            S.dma("sp", qif, qif[:], qi_s[r, :], [], [qif.b])
            for kc0 in range(0, 16, 4):
                ps = psum[kk[0] % 2]
                kk[0] += 1
                for j in range(4):
                    S.tr(ps[:, j * 128:(j + 1) * 128], qf[:, (kc0 + j) * 128:(kc0 + j + 1) * 128], ident_f, [qf.b, cst_t.b], [ps.b])
                S.copy("act", qT[:, kc0:kc0 + 4, :], ps[:].rearrange("p (j t) -> p j t", j=4), [ps.b], [qT.b])
            for kc0 in range(0, 8, 4):
                ps = psum[kk[0] % 2]
                kk[0] += 1
                for j in range(4):
                    S.tr(ps[:, j * 128:(j + 1) * 128], qif[:, (kc0 + j) * 128:(kc0 + j + 1) * 128], ident_f, [qif.b, cst_t.b], [ps.b])
                S.copy("act", qiT[:, kc0:kc0 + 4, :], ps[:].rearrange("p (j t) -> p j t", j=4), [ps.b], [qiT.b])
            NB = 16 + i + 1
            ncols = NB * 128
            for c0 in range(0, ncols, 512):
                wd = min(512, ncols - c0)
                for h in range(16):
                    pl = psum[2 + h % 2]
                    p0 = (h % 2) * 64
                    S.mm(pl[:, 0:wd], qiT[p0:p0 + 64, h // 2, :], kiT2[p0:p0 + 64, c0:c0 + wd], True, True, [qiT.b, kiT2.b], [pl.b])
                    rr = rl[h % 2]
                    S.act(rr[:, 0:wd], pl[:, 0:wd], AF.Relu, [pl.b], [rr.b])
                    if h == 0:
                        S.ts("dve", score[:, c0:c0 + wd], rr[:, 0:wd], wi_t[:, i, 0:1], None, ALU.mult, None, [rr.b, wi_t.b], [score.b])
                    else:
                        S.stt(score[:, c0:c0 + wd], rr[:, 0:wd], wi_t[:, i, h:h + 1], score[:, c0:c0 + wd], ALU.mult, ALU.add,
                              [rr.b, wi_t.b, score.b], [score.b])
            S.op("dve", lambda e, n=ncols: e.tensor_reduce(out=bs[:, HI:HI + 1], in_=score[:, 0:n], axis=AX.X, op=ALU.max), [score.b], [bs.b])
            S.op("dve", lambda e, n=ncols: e.tensor_reduce(out=bs[:, LO:LO + 1], in_=score[:, 0:n], axis=AX.X, op=ALU.min), [score.b], [bs.b])
            S.stt(bs[:, W0:W0 + 1], bs[:, HI:HI + 1], 1.0, bs[:, LO:LO + 1], ALU.add, ALU.subtract, [bs.b], [bs.b])
            S.ts("dve", score[:, 0:2048], score[:, 0:2048], flg_t[:, 0:1], flg_t[:, 1:2], ALU.mult, ALU.add, [score.b, flg_t.b], [score.b])
            dc = 2048 + i * 128
            S.tt("dve", score[:, dc:dc + 128], score[:, dc:dc + 128], cmask_f, ALU.add, [score.b, cst_t.b], [score.b])
            for it in range(N_BISECT):
                if it in (0, 5, 10, 14):
                    yield
                ck = 2.0 ** -(it + 1)
                S.stt(bs[:, MID:MID + 1], bs[:, W0:W0 + 1], ck, bs[:, LO:LO + 1], ALU.mult, ALU.add, [bs.b], [bs.b])
                S.ts("dve", maskf[:, 0:ncols], score[:, 0:ncols], bs[:, MID:MID + 1], 0.0, ALU.is_ge, ALU.add, [score.b, bs.b], [maskf.b, bs.b],
                     accum_out=bs[:, CNT:CNT + 1])
                S.ts("dve", bs[:, GC:GC + 1], bs[:, CNT:CNT + 1], 256.0, ck, ALU.is_ge, ALU.mult, [bs.b], [bs.b])
                S.stt(bs[:, LO:LO + 1], bs[:, GC:GC + 1], bs[:, W0:W0 + 1], bs[:, LO:LO + 1], ALU.mult, ALU.add, [bs.b], [bs.b])
            S.ts("dve", maskf[:, 0:ncols], score[:, 0:ncols], bs[:, LO:LO + 1], None, ALU.is_ge, None, [score.b, bs.b], [maskf.b])

        def mask_transpose(i):
            NB = 16 + i + 1
            maskT = maskTs[i % 2]
            for b0 in range(0, NB, 4):
                nb4 = min(4, NB - b0)
                ps = psum[kk[0] % 2]
                kk[0] += 1
                for j in range(nb4):
                    S.tr(ps[:, j * 128:(j + 1) * 128], maskf[:, (b0 + j) * 128:(b0 + j + 1) * 128], ident_f, [maskf.b, cst_t.b], [ps.b])
                S.copy("act", maskT[:, b0:b0 + nb4, :], ps[:, 0:nb4 * 128].rearrange("p (j t) -> p j t", j=nb4), [ps.b], [maskT.b])

        def attend(i):
            r = slice(i * 128, (i + 1) * 128)
            NB = 16 + i + 1
            qT, maskT = qTs[i % 2], maskTs[i % 2]
            for kv in range(4):
                pot, prs = psum[4 + kv % 2], psum[6 + kv % 2]

                def qk(b, kv=kv):
                    psc = psum[2 + b % 2]
                    S.mm(psc[:], kT[:, kv, b * 128:(b + 1) * 128], qT[:, 4 * kv:4 * kv + 4, :], True, True, [kT.b, qT.b], [psc.b])
                    p_, pm_ = pt[b % 3], pm[b % 3]
                    S.act(p_[:], psc[:], AF.Exp, [psc.b], [p_.b], scale=SCALE)
                    S.tt("pool", pm_[:].rearrange("p (h t) -> p h t", h=4), p_[:].rearrange("p (h t) -> p h t", h=4),
                         maskT[:, b, :].unsqueeze(1).broadcast_to([128, 4, 128]), ALU.mult, [p_.b, maskT.b], [pm_.b])

                def pv(b, kv=kv, pot=pot, prs=prs):
                    pm_ = pm[b % 3]
                    S.mm(pot[:], V[:, b, kv * 128:(kv + 1) * 128], pm_[:], b == 0, b == NB - 1, [V.b, pm_.b], [pot.b])
                    S.mm(prs[:], onesb[:], pm_[:], b == 0, b == NB - 1, [onesb.b, pm_.b], [prs.b])

                LA = 2
                for b in range(min(LA, NB)):
                    qk(b)
                for b in range(NB):
                    if b + LA < NB:
                        qk(b + LA)
                    pv(b)
                S.op("dve", lambda e, prs=prs: e.reciprocal(rinv[:], prs[:]), [prs.b], [rinv.b])
                o_ = ost[kv % 2]
                S.tt("dve", o_[:], pot[:], rinv[:], ALU.mult, [pot.b, rinv.b], [o_.b])
                S.dma("sp", o_, attnT_s[kv * 512:(kv + 1) * 512, r].rearrange("(h d) t -> d h t", h=4),
                      o_[:].rearrange("p (h t) -> p h t", h=4), [o_.b], [])
                yield

        for _ in prep_index_bisect(0):
            pass
        mask_transpose(0)
        for i in range(NT):
            gen_b = prep_index_bisect(i + 1) if i + 1 < NT else iter(())
            gen_a = attend(i)
            next(gen_b, None)
            for _ in range(4):
                next(gen_b, None)
                next(gen_a, None)
            for _ in gen_b:
                pass
            for _ in gen_a:
                pass
            if i + 1 < NT:
                mask_transpose(i + 1)

    with ExitStack() as es:
        ES[0] = es
        phase4()
        end_phase()
    ES[0] = None
    if "stop4" in debug:
        S.emit()
        return nc

    def phase5():
        alloc_dense("p5")
        E.p5 = True
        mT = sb("mT", [128, 16, 512], BF16)
        gt = [sb("gt%d" % i, [128, 2, 512], BF16) for i in range(2)]
        for blk in range(4):
            t0 = blk * 512
            begin_block(blk)
            ts_ = slice(t0, t0 + 512)
            S.dma("sp", E.xT, E.xT[:], attnT_s[:, ts_].rearrange("(k p) t -> p k t", p=128), [], [E.xT.b])
            S.dma("sp", E.actT, E.actT[:, 0:32, :], ssmT_s[:, ts_].rearrange("(k p) t -> p k t", p=128), [], [E.actT.b])
            for tt in range(4):
                S.dma("sp", E.xt[tt], E.xt[tt][:], h1_s[t0 + tt * 128:t0 + (tt + 1) * 128, :], [], [E.xt[tt].b])
            for sl in range(4):
                sa, s1, s2 = next_wsl(), next_wsl(), next_wsl()
                load_w(sa, w_oa, 0, 16, sl * 512, 512)
                load_w(s1, w_os, 0, 16, sl * 512, 512)
                load_w(s2, w_os, 2048, 16, sl * 512, 512)
                flush_w()
                for m in range(4):
                    dch = sl * 4 + m
                    g_ = gt[dch % 2]
                    S.dma("sp", g_, g_[:, 0, :], gT_s[dch * 128:(dch + 1) * 128, ts_], [], [g_.b])
                    S.dma("sp", g_, g_[:, 1, :], gT_s[2048 + dch * 128:2048 + (dch + 1) * 128, ts_], [], [g_.b])
                    pa, pss = psum[2 + m % 2], psum[4 + m % 2]
                    ms = slice(m * 128, (m + 1) * 128)
                    for kc in range(16):
                        S.mm(pa[:], sa[:, kc, ms], E.xT[:, kc, :], kc == 0, kc == 15, [sa.b, E.xT.b], [pa.b])
                    for kc in range(32):
                        w_ = s1 if kc < 16 else s2
                        S.mm(pss[:], w_[:, kc % 16, ms], E.actT[:, kc, :], kc == 0, kc == 31, [w_.b, E.actT.b], [pss.b])
                    a1, a2 = E.sg[0], E.sg[1]
                    S.tt("dve", a1[:], pa[:], g_[:, 0, :], ALU.mult, [pa.b, g_.b], [a1.b])
                    S.tt("dve", a2[:], pss[:], g_[:, 1, :], ALU.mult, [pss.b, g_.b], [a2.b])
                    S.tt("pool", mT[:, dch, :], a1[:], a2[:], ALU.add, [a1.b, a2.b], [mT.b])
            for nb in range(4):
                def epi(tt, ps, nb=nb):
                    cs = slice(nb * 512, (nb + 1) * 512)
                    S.stt(E.xt[tt][:, cs], E.xt[tt][:, cs], ALPHA, ps[:], ALU.mult, ALU.add, [E.xt[tt].b, ps.b], [E.xt[tt].b])
                linear_tm(mT, 16, w_out, 0, nb * 512, 512, epi, bank0=6)
            load_ln(2)
            for tt in range(4):
                layer_norm_tile(E.xt[tt], E.lng, E.lnb, E.xt[tt])
            load_ln(4)
            ffn_block(E.xt, w_gu2, w_dn2, E.xt)
            for tt in range(4):
                S.dma("sp", E.xt[tt], out[t0 + tt * 128:t0 + (tt + 1) * 128, :], E.xt[tt][:], [E.xt[tt].b], [])

    with ExitStack() as es:
        ES[0] = es
        phase5()
        end_phase()
    ES[0] = None

    S.emit()
    return nc


def _rope_tables(pos):
    def tab(rot):
        inv = (1.0 / (np.float32(500000.0) ** (np.arange(0, rot, 2, dtype=np.float32) / np.float32(rot)))).astype(np.float32)
        ang = pos.astype(np.float32)[:, None] * inv[None, :]
        return np.cos(ang).astype(np.float32), np.sin(ang).astype(np.float32)
    c, s = tab(32)
    ci, si = tab(16)
    return np.concatenate([c, s, ci, si], axis=1).astype(np.float32)


def make_in_maps(inp):
    f = lambda a: np.ascontiguousarray(np.asarray(a, dtype=np.float32))
    lnp = np.stack([f(inp[k])[0] for k in ("ln1_g", "ln1_b", "ln2_g", "ln2_b", "ln3_g", "ln3_b")], 0)
    conv_wb = np.concatenate([f(inp["conv_w"])[0], f(inp["conv_b"])], 0)
    ssmv = np.concatenate([f(inp["dt_bias"]), f(inp["A_log"]), f(inp["D_skip"]), np.zeros((1, 64), np.float32)], 0)
    ii = np.arange(128)
    cst = np.zeros((128, 5, 128), np.float32)
    cst[:, 0, :] = np.eye(128)
    cst[:, 1, :] = (ii[:, None] <= ii[None, :])
    cst[:, 2, :] = 1.0
    cst[:, 3, :] = np.where(ii[None, :] <= ii[:, None], 0.0, NEG)
    shared = {
        "ffn1_w_gu": f(inp["ffn1_w_gu"])[0], "ffn1_w_down": f(inp["ffn1_w_down"])[0],
        "ffn2_w_gu": f(inp["ffn2_w_gu"])[0], "ffn2_w_down": f(inp["ffn2_w_down"])[0],
        "w_in": f(inp["w_in"])[0], "w_o_attn": f(inp["w_o_attn"])[0], "w_o_ssm": f(inp["w_o_ssm"])[0],
        "w_out": f(inp["w_out"])[0], "lnp": lnp, "conv_wb": conv_wb, "ssmv": ssmv,
        "normw": f(inp["ssm_norm_w"]), "cst": cst,
        "convT": np.ascontiguousarray(conv_wb.reshape(5, 48, 128).transpose(2, 1, 0)),
    }
    x = f(inp["x"])
    maps = []
    for c in range(8):
        b, j = c // 2, c % 2
        m = dict(shared)
        m["x"] = np.ascontiguousarray(x[b, j * T:(j + 1) * T, :])
        m["ropet"] = _rope_tables(np.arange(j * T, (j + 1) * T))
        fl = np.zeros((128, 2), np.float32)
        fl[:, 0] = float(j)
        fl[:, 1] = (float(j) - 1.0) * 1.0e30
        m["flg"] = fl
        maps.append(m)
    return maps


def kernel(**inputs):
    import os
    dbg = tuple(d for d in os.environ.get("KDEBUG", "").split(",") if d)
    nc = build_nc(debug=dbg)
    in_maps = make_in_maps(inputs)
    res = run_bass_kernel_spmd(nc, in_maps, core_ids=list(range(8)))
    outp = np.zeros((4, 4096, D), np.float32)
    for c in range(8):
        b, j = c // 2, c % 2
        outp[b, j * T:(j + 1) * T, :] = res.results[c]["out"]
    return outp
```

```python
from contextlib import ExitStack
import numpy as np
import concourse.bass as bass
import concourse.mybir as mybir
from concourse.bass_utils import run_bass_kernel_spmd

F32 = mybir.dt.float32
BF16 = mybir.dt.bfloat16
AF = mybir.ActivationFunctionType
ALU = mybir.AluOpType
AX = mybir.AxisListType

D = 2048
T = 2048
NT = 16
DFF = 5504
NF = 43
WIN = 18576
ALPHA = 2.0 ** 0.25
NEG = -1.0e30
OFF_Q, OFF_K, OFF_V, OFF_QI, OFF_KI, OFF_WI, OFF_Z, OFF_XBC, OFF_DT, OFF_G = 0, 2048, 2560, 3072, 4096, 4160, 4176, 8272, 14416, 14480
N_BISECT = 18
NCORES = [8]


class Stream:
    def __init__(self, name, sem, inc):
        self.name, self.sem, self.inc, self.n = name, sem, inc, 0


class Buf:
    __slots__ = ("w", "r")

    def __init__(self):
        self.w = None
        self.r = {}


class Sched:
    QUEUES = ("pe", "act", "dve", "pool", "sp")

    def __init__(self, nc):
        self.nc = nc
        self.q = {k: [] for k in self.QUEUES}
        self.streams = {}
        for k in ("pe", "act", "dve", "pool"):
            self.streams[k] = Stream(k, nc.alloc_semaphore("s_" + k), 1)
        self.free_dma = []
        self.ndma = 0
        self.known = {k: {} for k in self.QUEUES}

    def dma_stream(self, tile):
        if tile.ds is None:
            if self.free_dma:
                tile.ds = self.free_dma.pop()
            else:
                name = "d%d" % self.ndma
                self.ndma += 1
                tile.ds = Stream(name, self.nc.alloc_semaphore("s_" + name), 16)
                self.streams[name] = tile.ds
        return tile.ds

    def release(self, tiles):
        for t in tiles:
            if t.ds is not None:
                self.free_dma.append(t.ds)
                t.ds = None

    def cc_stream(self):
        name = "cc%d" % self.ndma
        self.ndma += 1
        st = Stream(name, self.nc.alloc_semaphore("s_" + name), 1)
        self.streams[name] = st
        return st

    def _wait(self, queue, st, n):
        if n <= 0 or self.known[queue].get(st.name, 0) >= n:
            return
        self.known[queue][st.name] = n
        val, sem = n * st.inc, st.sem
        self.q[queue].append(lambda eng: eng.wait_ge(sem, val))

    def op(self, queue, fn, reads=(), writes=(), stream=None):
        st = stream if isinstance(stream, Stream) else self.streams[stream or queue]
        deps = {}
        for b in reads:
            if b.w is not None and deps.get(b.w[0].name, (None, 0))[1] < b.w[1]:
                deps[b.w[0].name] = b.w
        for b in writes:
            if b.w is not None and deps.get(b.w[0].name, (None, 0))[1] < b.w[1]:
                deps[b.w[0].name] = b.w
            for d in b.r.values():
                if deps.get(d[0].name, (None, 0))[1] < d[1]:
                    deps[d[0].name] = d
        for s, n in deps.values():
            if queue == "pe" and s.name == "pe":
                continue
            self._wait(queue, s, n)
        st.n += 1
        me = (st, st.n)
        sem, inc = st.sem, st.inc
        self.q[queue].append(lambda eng: fn(eng).then_inc(sem, inc))
        for b in reads:
            b.r[st.name] = me
        for b in writes:
            b.w = me
            b.r = {}
        return me

    def barrier(self):
        for queue in self.QUEUES:
            for st in self.streams.values():
                if queue == "pe" and st.name == "pe":
                    continue
                self._wait(queue, st, st.n)

    def emit(self):
        with self.nc.Block() as block:
            @block.tensor
            def _(e):
                for f in self.q["pe"]:
                    f(e)

            @block.scalar
            def _(e):
                for f in self.q["act"]:
                    f(e)

            @block.vector
            def _(e):
                for f in self.q["dve"]:
                    f(e)

            @block.gpsimd
            def _(e):
                for f in self.q["pool"]:
                    f(e)

            @block.sync
            def _(e):
                for f in self.q["sp"]:
                    f(e)

    def mm(self, out, lhsT, rhs, start, stop, reads, writes):
        self.op("pe", lambda e: e.matmul(out, lhsT=lhsT, rhs=rhs, start=start, stop=stop), reads, writes)

    def tr(self, out, in_, ident, reads, writes):
        self.op("pe", lambda e: e.transpose(out, in_, ident), reads, writes)

    def act(self, out, in_, func, reads, writes, scale=None, bias=None, accum_out=None):
        kw = {}
        if scale is not None:
            kw["scale"] = scale
        if bias is not None:
            kw["bias"] = bias
        if accum_out is not None:
            kw["accum_out"] = accum_out
        self.op("act", lambda e: e.activation(out=out, in_=in_, func=func, **kw), reads, writes)

    def tt(self, q, out, in0, in1, op, reads, writes):
        self.op(q, lambda e: e.tensor_tensor(out=out, in0=in0, in1=in1, op=op), reads, writes)

    def ts(self, q, out, in0, s1, s2, op0, op1, reads, writes, accum_out=None):
        if op1 is None:
            self.op(q, lambda e: e.tensor_scalar(out=out, in0=in0, scalar1=s1, scalar2=None, op0=op0), reads, writes)
        elif accum_out is None:
            self.op(q, lambda e: e.tensor_scalar(out=out, in0=in0, scalar1=s1, scalar2=s2, op0=op0, op1=op1), reads, writes)
        else:
            self.op(q, lambda e: e.tensor_scalar(out=out, in0=in0, scalar1=s1, scalar2=s2, op0=op0, op1=op1, accum_out=accum_out), reads, writes)

    def stt(self, out, in0, scalar, in1, op0, op1, reads, writes):
        self.op("dve", lambda e: e.scalar_tensor_tensor(out=out, in0=in0, scalar=scalar, in1=in1, op0=op0, op1=op1), reads, writes)

    def copy(self, q, out, in_, reads, writes):
        if q == "act":
            self.op("act", lambda e: e.copy(out=out, in_=in_), reads, writes)
        else:
            self.op(q, lambda e: e.tensor_copy(out=out, in_=in_), reads, writes)

    def dma(self, q, tile, out, in_, reads, writes, slow=False):
        if slow:
            self.op(q, lambda e: e.dma_start(out=out, in_=in_, allow_slow_non_contiguous=True), reads, writes, self.dma_stream(tile))
        else:
            self.op(q, lambda e: e.dma_start(out=out, in_=in_), reads, writes, self.dma_stream(tile))


class Tile:
    def __init__(self, t):
        self.t = t
        self.b = Buf()
        self.ds = None

    def __getitem__(self, k):
        return self.t[k]


def build_nc(debug=()):
    nc = bass.Bass("TRN2", target_bir_lowering=False)

    def din(name, shape, dt=F32):
        return nc.dram_tensor(name, list(shape), dt, kind="ExternalInput").ap()

    x_in = din("x", [T, D])
    w_gu1, w_dn1 = din("ffn1_w_gu", [D, 2 * DFF]), din("ffn1_w_down", [DFF, D])
    w_gu2, w_dn2 = din("ffn2_w_gu", [D, 2 * DFF]), din("ffn2_w_down", [DFF, D])
    w_in = din("w_in", [D, WIN])
    w_oa, w_os, w_out = din("w_o_attn", [D, D]), din("w_o_ssm", [2 * D, D]), din("w_out", [D, D])
    lnp = din("lnp", [6, D])
    conv_wb = din("conv_wb", [5, 6144])
    convT = din("convT", [128, 48, 5])
    ssmv = din("ssmv", [4, 64])
    normw = din("normw", [1, 4096])
    ropet = din("ropet", [T, 48])
    cst = din("cst", [128, 5, 128])
    flg = din("flg", [128, 2])
    out = nc.dram_tensor("out", [T, D], F32, kind="ExternalOutput").ap()

    def dscr(name, shape, dt):
        if name in debug:
            return nc.dram_tensor(name, list(shape), dt, kind="ExternalOutput").ap()
        return nc.dram_tensor(name, list(shape), dt).ap()

    h1_s = dscr("h1_s", [T, D], F32)
    q_s = dscr("q_s", [T, D], F32)
    qi_s = dscr("qi_s", [T, 1024], F32)
    wi_s = dscr("wi_s", [T, 16], F32)
    kvg_in = dscr("kvg_in", [T, 1088], F32)
    kvg = [nc.dram_tensor("kvg%d" % i, [256, 2176], F32).ap() for i in range(8)]
    z_s = dscr("z_s", [T, 4096], BF16)
    xbc_s = dscr("xbc_s", [6, 6144], F32)
    halo_in = nc.dram_tensor("halo_in", [3, 6144], F32).ap()
    halo_g = nc.dram_tensor("halo_g", [6, 6144], F32).ap()
    dt_s = dscr("dt_s", [T, 64], F32)
    gT_s = dscr("gT_s", [4096, T], BF16)
    xc_s = dscr("xc_s", [T, 6144], F32)
    ypre_s = dscr("ypre_s", [T, 4096], F32)
    ct_s = nc.dram_tensor("ct_s", [NT * 8 * 128, 128], BF16).ap()
    st_in = nc.dram_tensor("st_in", [128, 4096], F32).ap()
    st_g = nc.dram_tensor("st_g", [256, 4096], F32).ap()
    attnT_s = dscr("attnT_s", [D, T], BF16)
    ssmT_s = dscr("ssmT_s", [4096, T], BF16)

    wscr = nc.dram_tensor("wscr", [128, 128, 16 * 512], BF16).ap()
    wscr2 = nc.dram_tensor("wscr2", [112, 128, 16 * 512], BF16).ap()

    S = Sched(nc)

    ES = [None]

    PH_TILES = []

    def sb(name, shape, dt):
        if ES[0] is None:
            return Tile(nc.alloc_sbuf_tensor(name, list(shape), dt))
        t = Tile(ES[0].enter_context(nc.sbuf_tensor(name, list(shape), dt)))
        PH_TILES.append(t)
        return t

    def end_phase():
        S.barrier()
        S.release(PH_TILES)
        del PH_TILES[:]

    cst_t = sb("cst_t", [128, 5, 128], F32)
    identb = sb("identb", [128, 128], BF16)
    onesb = sb("onesb", [128, 128], BF16)
    ub = sb("ub", [128, 128], BF16)
    flg_t = sb("flg_t", [128, 2], F32)
    S.dma("sp", cst_t, cst_t[:], cst, [], [cst_t.b])
    S.dma("sp", flg_t, flg_t[:], flg, [], [flg_t.b])
    S.copy("dve", identb[:], cst_t[:, 0, :], [cst_t.b], [identb.b])
    S.copy("dve", onesb[:], cst_t[:, 2, :], [cst_t.b], [onesb.b])
    S.copy("dve", ub[:], cst_t[:, 1, :], [cst_t.b], [ub.b])
    ident_f = cst_t[:, 0, :]
    U_f = cst_t[:, 1, :]
    ones_f = cst_t[:, 2, :]
    cmask_f = cst_t[:, 3, :]

    psum = [Tile(nc.alloc_psum_tensor("ps%d" % i, [128, 512], F32)) for i in range(8)]

    def bcast_rows(ap_row, n):
        return ap_row.broadcast(0, 128) if hasattr(ap_row, "broadcast") else ap_row


    class Env:
        pass

    E = Env()
    E.p5 = False

    def alloc_dense(tag):
        E.xT = sb("xT" + tag, [128, 16, 512], BF16)
        E.actT = sb("actT" + tag, [128, NF, 512], BF16)
        E.wsl = [sb("wsl%d" % i + tag, [128, 16, 512], BF16) for i in range(3)]
        E.xt = [sb("xt%d" % i + tag, [128, D], F32) for i in range(4)]
        E.yt = E.xt
        E.lng = sb("lng" + tag, [128, D], F32)
        E.lnb = sb("lnb" + tag, [128, D], F32)
        E.sg = [sb("sg%d" % i + tag, [128, 512], F32) for i in range(2)]
        E.stg = [sb("stg%d" % i + tag, [128, 512], F32) for i in range(3)]
        E.stgb = [sb("stgb%d" % i + tag, [128, 512], BF16) for i in range(3)]
        E.small = [sb("small%d" % i + tag, [128, 64], F32) for i in range(4)]
        E.stats = sb("stats" + tag, [128, 4, 6], F32)
        E.wslot = 0
        E.stgi = 0
        E.stgbi = 0

    def next_wsl():
        E.wslot = (E.wslot + 1) % 3
        return E.wsl[E.wslot]

    def next_stg():
        E.stgi = (E.stgi + 1) % 3
        return E.stg[E.stgi]

    def next_stgb():
        E.stgbi = (E.stgbi + 1) % 3
        return E.stgb[E.stgbi]

    def load_w(slab, w_ap, r0, nk, c0, ncols, col_off=0):
        idx = E.slab_idx
        E.slab_idx += 1
        dst = slab[:, 0:nk, col_off:col_off + ncols]
        if E.p5:
            assert P5LIST[idx] == (w_ap, r0, nk, c0, ncols, col_off), (idx, r0, nk, c0, ncols, col_off)
            scr = wscr2[idx].rearrange("p (k n) -> p k n", k=16)[:, 0:nk, col_off:col_off + ncols]
            S.dma("pool", slab, dst, scr, [], [slab.b])
            return
        scr = wscr[idx].rearrange("p (k n) -> p k n", k=16)[:, 0:nk, col_off:col_off + ncols]
        if E.blk == 0:
            src = w_ap[r0:r0 + nk * 128, c0:c0 + ncols].rearrange("(k p) n -> p k n", p=128)
            S.dma("pool", slab, dst, src, [], [slab.b])
            E.pending_stores.append((slab, scr, dst))
        else:
            S.dma("pool", slab, dst, scr, [], [slab.b])

    def p5_slabs():
        for sl in range(4):
            yield (w_oa, 0, 16, sl * 512, 512, 0)
            yield (w_os, 0, 16, sl * 512, 512, 0)
            yield (w_os, 2048, 16, sl * 512, 512, 0)
        for nb in range(4):
            yield (w_out, 0, 16, nb * 512, 512, 0)
        for s0 in range(0, NF, 2):
            nf = min(2, NF - s0)
            yield (w_gu2, 0, 16, s0 * 128, nf * 128, 0)
            yield (w_gu2, 0, 16, DFF + s0 * 128, nf * 128, 256)
        for nb in range(4):
            for s0 in range(0, NF, 4):
                yield (w_dn2, s0 * 128, min(4, NF - s0), nb * 512, 512, 0)

    P5LIST = list(p5_slabs())
    PRE = Tile(None)

    def precast(lo, hi):
        for idx in range(lo, min(hi, len(P5LIST))):
            w_ap, r0, nk, c0, ncols, col_off = P5LIST[idx]
            src = w_ap[r0:r0 + nk * 128, c0:c0 + ncols].rearrange("(k p) n -> p k n", p=128)
            scr = wscr2[idx].rearrange("p (k n) -> p k n", k=16)[:, 0:nk, col_off:col_off + ncols]
            S.dma("pool", PRE, scr, src, [], [])

    def flush_w():
        for slab, scr, dst in E.pending_stores:
            S.dma("sp", slab, scr, dst, [slab.b], [])
        del E.pending_stores[:]

    def begin_block(blk):
        E.blk = blk
        E.slab_idx = 0
        E.pending_stores = []

    def transpose_to_xT(tiles, dstT, ncol_chunks=16):
        k = 0
        for tt in range(4):
            for kc0 in range(0, ncol_chunks, 4):
                ps = psum[k % 2]
                k += 1
                for j in range(4):
                    S.tr(ps[:, j * 128:(j + 1) * 128], tiles[tt][:, (kc0 + j) * 128:(kc0 + j + 1) * 128], ident_f,
                         [tiles[tt].b, cst_t.b], [ps.b])
                q = "act" if k % 2 else "dve"
                S.copy(q, dstT[:, kc0:kc0 + 4, tt * 128:(tt + 1) * 128],
                       ps[:].rearrange("p (j t) -> p j t", j=4), [ps.b], [dstT.b])

    def layer_norm_tile(y, g_t, b_t, out_t):
        st = E.stats
        for c in range(4):
            S.op("dve", lambda e, c=c: e.bn_stats(st[:, c, :], y[:, c * 512:(c + 1) * 512]), [y.b], [st.b])
        mv = E.small[0]
        S.op("dve", lambda e: e.bn_aggr(mv[:, 0:2], st[:]), [st.b], [mv.b])
        S.ts("dve", mv[:, 2:3], mv[:, 1:2], 1e-5, None, ALU.add, None, [mv.b], [mv.b])
        S.act(mv[:, 3:4], mv[:, 2:3], AF.Sqrt, [mv.b], [mv.b])
        S.op("dve", lambda e: e.reciprocal(mv[:, 4:5], mv[:, 3:4]), [mv.b], [mv.b])
        S.ts("dve", out_t[:], y[:], mv[:, 0:1], mv[:, 4:5], ALU.subtract, ALU.mult, [y.b, mv.b], [out_t.b])
        S.tt("dve", out_t[:], out_t[:], g_t[:], ALU.mult, [out_t.b, g_t.b], [out_t.b])
        S.tt("dve", out_t[:], out_t[:], b_t[:], ALU.add, [out_t.b, b_t.b], [out_t.b])

    def load_ln(idx):
        S.dma("sp", E.lng, E.lng[:], lnp[idx:idx + 1, :].broadcast_to([128, D]), [], [E.lng.b])
        S.dma("sp", E.lnb, E.lnb[:], lnp[idx + 1:idx + 2, :].broadcast_to([128, D]), [], [E.lnb.b])

    def ffn_block(xin, w_gu, w_dn, yout):
        transpose_to_xT(xin, E.xT)
        for tt in range(4):
            S.op("act", lambda e, tt=tt: e.mul(xin[tt][:], xin[tt][:], ALPHA), [xin[tt].b], [xin[tt].b])
        for s0 in range(0, NF, 2):
            nf = min(2, NF - s0)
            slab = next_wsl()
            load_w(slab, w_gu, 0, 16, s0 * 128, nf * 128, 0)
            load_w(slab, w_gu, 0, 16, DFF + s0 * 128, nf * 128, 256)
            flush_w()
            for m in range(nf):
                f = s0 + m
                pg, pu = psum[2 + f % 2], psum[4 + f % 2]
                for kc in range(16):
                    S.mm(pg[:], slab[:, kc, m * 128:(m + 1) * 128], E.xT[:, kc, :], kc == 0, kc == 15, [slab.b, E.xT.b], [pg.b])
                for kc in range(16):
                    S.mm(pu[:], slab[:, kc, 256 + m * 128:256 + (m + 1) * 128], E.xT[:, kc, :], kc == 0, kc == 15, [slab.b, E.xT.b], [pu.b])
                sg = E.sg[f % 2]
                S.act(sg[:], pg[:], AF.Silu, [pg.b], [sg.b])
                S.tt("dve", E.actT[:, f, :], sg[:], pu[:], ALU.mult, [sg.b, pu.b], [E.actT.b])
        for nb in range(4):
            banks = [psum[(nb % 2) * 4 + tt] for tt in range(4)]
            for s0 in range(0, NF, 4):
                nf = min(4, NF - s0)
                slab = next_wsl()
                load_w(slab, w_dn, s0 * 128, nf, nb * 512, 512)
                flush_w()
                for m in range(nf):
                    f = s0 + m
                    for tt in range(4):
                        S.mm(banks[tt][:], E.actT[:, f, tt * 128:(tt + 1) * 128], slab[:, m, :], f == 0, f == NF - 1,
                             [slab.b, E.actT.b], [banks[tt].b])
            for tt in range(4):
                S.stt(yout[tt][:, nb * 512:(nb + 1) * 512], banks[tt][:], 0.5, xin[tt][:, nb * 512:(nb + 1) * 512],
                      ALU.mult, ALU.add, [banks[tt].b, xin[tt].b], [yout[tt].b])
        for tt in range(4):
            layer_norm_tile(yout[tt], E.lng, E.lnb, yout[tt])

    def linear_tm(xT, nk, w_ap, r0, c0, ncols, epilogue, bank0=0):
        slab = next_wsl()
        load_w(slab, w_ap, r0, nk, c0, ncols)
        flush_w()
        for tt in range(4):
            ps = psum[bank0 + tt % 2]
            for kc in range(nk):
                S.mm(ps[:, 0:ncols], xT[:, kc, tt * 128:(tt + 1) * 128], slab[:, kc, 0:ncols], kc == 0, kc == nk - 1,
                     [slab.b, xT.b], [ps.b])
            epilogue(tt, ps)

    def linear_fm(xT, nk, w_ap, r0, c0, nchunks, epilogue, bank0=2, acc=None):
        slab = next_wsl()
        load_w(slab, w_ap, r0, nk, c0, nchunks * 128)
        flush_w()
        for m in range(nchunks):
            ps = psum[bank0 + m % 2]
            for kc in range(nk):
                S.mm(ps[:], slab[:, kc, m * 128:(m + 1) * 128], xT[:, kc, :], kc == 0, kc == nk - 1, [slab.b, xT.b], [ps.b])
            epilogue(m, ps)

    rope_t = sb("rope_t", [128, NT, 48], F32)
    S.dma("sp", rope_t, rope_t[:], ropet.rearrange("(n p) c -> p n c", p=128), [], [rope_t.b])
    dtb_t = sb("dtb_t", [128, 4, 64], F32)
    S.dma("sp", dtb_t, dtb_t[:], ssmv.unsqueeze(0).broadcast_to([128, 4, 64]), [], [dtb_t.b])
    ropetmp = sb("ropetmp", [128, 4, 8 * 16], F32)

    def rope_epi(ps, nh, hd, half, ti, cofs, dst_tile):
        n = nh * hd
        S.copy("act", dst_tile[:, 0:n], ps[:, 0:n], [ps.b], [dst_tile.b])
        dv = dst_tile[:, 0:n].rearrange("p (h d) -> p h d", h=nh)
        cos = rope_t[:, ti, cofs:cofs + half].unsqueeze(1).broadcast_to([128, nh, half])
        sin = rope_t[:, ti, cofs + half:cofs + 2 * half].unsqueeze(1).broadcast_to([128, nh, half])
        x1, x2 = dv[:, :, 0:half], dv[:, :, half:2 * half]
        tmp = [ropetmp[:, j, 0:nh * half].rearrange("p (h c) -> p h c", h=nh) for j in range(4)]
        S.tt("dve", tmp[0], x1, cos, ALU.mult, [dst_tile.b, rope_t.b], [ropetmp.b])
        S.tt("dve", tmp[1], x2, sin, ALU.mult, [dst_tile.b, rope_t.b], [ropetmp.b])
        S.tt("dve", tmp[2], x2, cos, ALU.mult, [dst_tile.b, rope_t.b], [ropetmp.b])
        S.tt("dve", tmp[3], x1, sin, ALU.mult, [dst_tile.b, rope_t.b], [ropetmp.b])
        S.tt("dve", x1, tmp[0], tmp[1], ALU.subtract, [ropetmp.b], [dst_tile.b])
        S.tt("dve", x2, tmp[2], tmp[3], ALU.add, [ropetmp.b], [dst_tile.b])

    P1 = Env()

    def phase1():
        alloc_dense("p1")
        load_ln(0)
        P1.hal = sb("hal", [128, 48, 4], F32)
        P1.cwT = sb("cwT", [128, 48, 5], F32)
        P1.xcin = [sb("xcin%d" % i, [128, 515], F32) for i in range(3)]
        P1.acc = [sb("cacc%d" % i, [128, 512], F32) for i in range(3)]
        P1.so = [sb("cso%d" % i, [128, 512], F32) for i in range(3)]
        P1.k = 0
        P1.pending = None
        S.op("pool", lambda e: e.memset(P1.hal[:], 0.0), [], [P1.hal.b])
        S.dma("sp", P1.cwT, P1.cwT[:], convT, [], [P1.cwT.b])
        for blk in range(1 if "blk1" in debug else 4):
            t0 = blk * 512
            begin_block(blk)
            for tt in range(4):
                S.dma("sp", E.xt[tt], E.xt[tt][:], x_in[t0 + tt * 128:t0 + (tt + 1) * 128, :], [], [E.xt[tt].b])
            ffn_block(E.xt, w_gu1, w_dn1, E.yt)
            for tt in range(4):
                S.dma("sp", E.yt[tt], h1_s[t0 + tt * 128:t0 + (tt + 1) * 128, :], E.yt[tt][:], [E.yt[tt].b], [])
            if "nowin" in debug:
                continue
            transpose_to_xT(E.yt, E.xT)
            h1T = E.xT

            def rows(tt):
                return slice(t0 + tt * 128, t0 + (tt + 1) * 128)

            def on(name):
                segs = [d for d in debug if d.startswith("seg_")]
                return (not segs) or ("seg_" + name in segs)

            for sl in range(4 if on("q") else 0):
                def epi(tt, ps, sl=sl):
                    st = next_stg()
                    rope_epi(ps, 4, 128, 16, blk * 4 + tt, 0, st)
                    S.dma("sp", st, q_s[rows(tt), sl * 512:(sl + 1) * 512], st[:], [st.b], [])
                linear_tm(h1T, 16, w_in, 0, OFF_Q + sl * 512, 512, epi)

            def epi_k(tt, ps):
                st = next_stg()
                rope_epi(ps, 4, 128, 16, blk * 4 + tt, 0, st)
                S.dma("sp", st, kvg_in[rows(tt), 0:512], st[:], [st.b], [])
            if on("k"):
                linear_tm(h1T, 16, w_in, 0, OFF_K, 512, epi_k)

            def epi_v(tt, ps):
                st = next_stg()
                S.copy("act", st[:], ps[:], [ps.b], [st.b])
                S.dma("sp", st, kvg_in[rows(tt), 512:1024], st[:], [st.b], [])
            if on("v"):
                linear_tm(h1T, 16, w_in, 0, OFF_V, 512, epi_v)

            for sl in range(2 if on("qi") else 0):
                def epi(tt, ps, sl=sl):
                    st = next_stg()
                    rope_epi(ps, 8, 64, 8, blk * 4 + tt, 32, st)
                    S.dma("sp", st, qi_s[rows(tt), sl * 512:(sl + 1) * 512], st[:], [st.b], [])
                linear_tm(h1T, 16, w_in, 0, OFF_QI + sl * 512, 512, epi)

            def epi_kw(tt, ps):
                st = next_stg()
                rope_epi(ps, 1, 64, 8, blk * 4 + tt, 32, st)
                S.dma("sp", st, kvg_in[rows(tt), 1024:1088], st[:, 0:64], [st.b], [])
                sf = next_stg()
                S.op("act", lambda e: e.mul(sf[:, 0:16], ps[:, 64:80], 0.125 * 0.25), [ps.b], [sf.b])
                S.dma("sp", sf, wi_s[rows(tt), :], sf[:, 0:16], [sf.b], [])
            if on("kw"):
                linear_tm(h1T, 16, w_in, 0, OFF_KI, 80, epi_kw)

            for sl in range(8 if on("z") else 0):
                def epi(tt, ps, sl=sl):
                    st = next_stgb()
                    S.copy("act" if tt % 2 else "dve", st[:], ps[:], [ps.b], [st.b])
                    S.dma("sp", st, z_s[rows(tt), sl * 512:(sl + 1) * 512], st[:], [st.b], [])
                linear_tm(h1T, 16, w_in, 0, OFF_Z + sl * 512, 512, epi)

            for sl in range(12 if on("xbc") else 0):
                def epi(m, ps, sl=sl):
                    ch = sl * 4 + m
                    xin, acc, so = P1.xcin[P1.k % 3], P1.acc[P1.k % 3], P1.so[P1.k % 3]
                    P1.k += 1
                    S.copy("act", xin[:, 3:515], ps[:], [ps.b], [xin.b])
                    S.copy("act", xin[:, 0:3], P1.hal[:, ch, 0:3], [P1.hal.b], [xin.b])
                    S.copy("act", P1.hal[:, ch, 0:3], xin[:, 512:515], [xin.b], [P1.hal.b])
                    cols = slice(ch * 128, (ch + 1) * 128)
                    if blk == 0:
                        S.dma("sp", xin, xbc_s[3:6, cols].rearrange("t c -> c t"), xin[:, 3:6], [xin.b], [], slow=True)
                    if blk == 3:
                        S.dma("sp", xin, halo_in[0:3, cols].rearrange("t c -> c t"), xin[:, 512:515], [xin.b], [], slow=True)
                    cw = P1.cwT
                    S.ts("dve", acc[:], xin[:, 0:512], cw[:, ch, 0:1], cw[:, ch, 4:5], ALU.mult, ALU.add, [xin.b, cw.b], [acc.b])
                    for i in (1, 2, 3):
                        S.stt(acc[:], xin[:, i:i + 512], cw[:, ch, i:i + 1], acc[:], ALU.mult, ALU.add, [xin.b, cw.b, acc.b], [acc.b])
                    S.act(acc[:], acc[:], AF.Silu, [acc.b], [acc.b])

                    def tail(acc=acc, so=so, cols=cols, kk=P1.k):
                        pst = psum[kk % 2]
                        for j in range(4):
                            S.tr(pst[:, j * 128:(j + 1) * 128], acc[:, j * 128:(j + 1) * 128], ident_f, [acc.b, cst_t.b], [pst.b])
                        S.copy("dve" if kk % 2 else "act", so[:], pst[:], [pst.b], [so.b])
                        S.dma("sp", so, xc_s[t0:t0 + 512, cols].rearrange("(j p) c -> p j c", p=128),
                              so[:].rearrange("p (j c) -> p j c", j=4), [so.b], [])
                    if P1.pending is not None:
                        P1.pending()
                    P1.pending = tail
                linear_fm(h1T, 16, w_in, 0, OFF_XBC + sl * 512, 4, epi)
            if P1.pending is not None:
                P1.pending()
                P1.pending = None

            def epi_dt(tt, ps):
                sf = next_stg()
                S.tt("dve", sf[:, 0:64], ps[:, 0:64], dtb_t[:, 0, :], ALU.add, [ps.b, dtb_t.b], [sf.b])
                S.act(sf[:, 64:128], sf[:, 0:64], AF.Exp, [sf.b], [sf.b])
                S.act(sf[:, 128:192], sf[:, 64:128], AF.Ln, [sf.b], [sf.b], bias=1.0)
                S.dma("sp", sf, dt_s[rows(tt), :], sf[:, 128:192], [sf.b], [])
            if on("dt"):
                linear_tm(h1T, 16, w_in, 0, OFF_DT, 64, epi_dt)

            for sl in range(8 if on("g") else 0):
                def epi(m, ps, sl=sl):
                    st = next_stgb()
                    S.act(st[:], ps[:], AF.Sigmoid, [ps.b], [st.b])
                    r0 = (sl * 4 + m) * 128
                    S.dma("sp", st, gT_s[r0:r0 + 128, t0:t0 + 512], st[:], [st.b], [])
                linear_fm(h1T, 16, w_in, 0, OFF_G + sl * 512, 4, epi)

    with ExitStack() as es:
        ES[0] = es
        phase1()
        end_phase()
    ES[0] = None

    if "stop1" in debug:
        S.emit()
        return nc

    RG = [[2 * i, 2 * i + 1] for i in range(NCORES[0] // 2)]

    def collective(in_ap, out_ap):
        st = S.cc_stream()
        bb = Buf()
        S.op("pool", lambda e: e.collective_compute("AllGather", ALU.bypass, replica_groups=RG, ins=[in_ap], outs=[out_ap]),
             [], [bb], st)

    def phase2():
        for qq in range(8):
            collective(kvg_in[qq * 256:(qq + 1) * 256, :].rearrange("(p n) c -> p (n c)", p=128), kvg[qq])
        collective(halo_in, halo_g)
        S.barrier()
        hl = sb("hl", [3, 6144], F32)
        S.dma("sp", hl, hl[:], halo_g[0:3, :], [], [hl.b])
        S.ts("dve", hl[:], hl[:], flg_t[0:3, 0:1], None, ALU.mult, None, [hl.b, flg_t.b], [hl.b])
        S.dma("sp", hl, xbc_s[0:3, :], hl[:], [hl.b], [])
        S.barrier()
        CW = 1024
        wt = [sb("cw%d" % i, [3, 5, CW], F32) for i in range(2)]
        xs = [sb("cx%d" % i, [3, 4, CW], F32) for i in range(2)]
        ys = [sb("cy%d" % i, [3, CW], F32) for i in range(2)]
        for cb in range(6144 // CW):
            w, x4, y1 = wt[cb % 2], xs[cb % 2], ys[cb % 2]
            cs = slice(cb * CW, (cb + 1) * CW)
            S.dma("sp", w, w[:], conv_wb[:, cs].unsqueeze(0).broadcast_to([3, 5, CW]), [], [w.b])
            src = bass.AP(xbc_s.tensor, xbc_s[0:1, cs].offset, [[6144, 3], [6144, 4], [1, CW]])
            S.dma("sp", x4, x4[:], src, [], [x4.b])
            S.tt("dve", x4[:], x4[:], w[:, 0:4, :], ALU.mult, [x4.b, w.b], [x4.b])
            S.tt("dve", x4[:, 0:2, :], x4[:, 0:2, :], x4[:, 2:4, :], ALU.add, [x4.b], [x4.b])
            S.tt("dve", y1[:], x4[:, 0, :], x4[:, 1, :], ALU.add, [x4.b], [y1.b])
            S.tt("dve", y1[:], y1[:], w[:, 4, :], ALU.add, [y1.b, w.b], [y1.b])
            S.act(y1[:], y1[:], AF.Silu, [y1.b], [y1.b])
            S.dma("sp", y1, xc_s[0:3, cs], y1[:], [y1.b], [])

    with ExitStack() as es:
        ES[0] = es
        phase2()
        end_phase()
    ES[0] = None
    if "stop2" in debug:
        S.emit()
        return nc

    def bc3(ap2d, n_inner):
        return ap2d.unsqueeze(2).broadcast_to([128, ap2d.shape[1], n_inner])

    def phase3():
        xc = [sb("s_xc%d" % i, [128, 4096], F32) for i in range(2)]
        bc = [sb("s_bc%d" % i, [128, 2048], F32) for i in range(2)]
        dtt = [sb("s_dt%d" % i, [128, 64], F32) for i in range(2)]
        xdt = sb("s_xdt", [128, 4096], BF16)
        xw = sb("s_xw", [128, 4096], BF16)
        btm = sb("s_btm", [128, 1024], BF16)
        st_f = sb("s_stf", [128, 4096], F32)
        st_b = sb("s_stb", [128, 4096], BF16)
        BT = sb("s_BT", [128, 8, 128], BF16)
        CT = [sb("s_CT%d" % i, [128, 8, 128], BF16) for i in range(2)]
        sm = sb("s_sm", [128, 12, 64], F32)
        etot = sb("s_etot", [128, NT, 64], F32)
        cbm = [sb("s_cbm%d" % i, [128, 128], F32) for i in range(2)]
        Zw = [sb("s_Zw%d" % i, [128, 8, 128], F32) for i in range(2)]
        Eww = [sb("s_Ew%d" % i, [128, 1024], F32) for i in range(2)]
        Mww = [sb("s_Mw%d" % i, [128, 8, 128], BF16) for i in range(2)]
        yo = [sb("s_yo%d" % i, [128, 512], F32) for i in range(2)]
        dx = [sb("s_dx%d" % i, [128, 512], F32) for i in range(2)]
        A_, DTA, A_C, NEGA, EA, W_, DEC, AOFF, TMP, TMP2 = range(10)
        S.act(sm[:, A_, :], dtb_t[:, 1, :], AF.Exp, [dtb_t.b], [sm.b])
        S.ts("dve", sm[:, A_, :], sm[:, A_, :], -1.0, None, ALU.mult, None, [sm.b], [sm.b])
        S.op("pool", lambda e: e.memset(sm[:, AOFF, :], 0.0), [], [sm.b])
        S.op("pool", lambda e: e.memset(st_f[:], 0.0), [], [st_f.b])
        S.op("pool", lambda e: e.memset(st_b[:], 0.0), [], [st_b.b])
        hk = 0
        for c in range(NT):
            X, BC, DT = xc[c % 2], bc[c % 2], dtt[c % 2]
            r = slice(c * 128, (c + 1) * 128)
            precast(c * 7, (c + 1) * 7)
            S.dma("sp", X, X[:], xc_s[r, 0:4096], [], [X.b])
            S.dma("sp", BC, BC[:], xc_s[r, 4096:6144], [], [BC.b])
            S.dma("sp", DT, DT[:], dt_s[r, :], [], [DT.b])
            S.tt("dve", sm[:, DTA, :], DT[:], sm[:, A_, :], ALU.mult, [DT.b, sm.b], [sm.b])
            pa = psum[0]
            S.mm(pa[:, 0:64], U_f, sm[:, DTA, :], True, True, [cst_t.b, sm.b], [pa.b])
            S.mm(pa[:, 64:128], ones_f, sm[:, DTA, :], True, True, [cst_t.b, sm.b], [pa.b])
            S.copy("act", sm[:, A_C, :], pa[:, 0:64], [pa.b], [sm.b])
            S.ts("dve", sm[:, NEGA, :], pa[:, 0:64], -1.0, None, ALU.mult, None, [pa.b], [sm.b])
            S.act(sm[:, EA, :], pa[:, 0:64], AF.Exp, [pa.b], [sm.b])
            S.tt("dve", sm[:, TMP, :], pa[:, 64:128], sm[:, A_C, :], ALU.subtract, [pa.b, sm.b], [sm.b])
            S.act(sm[:, W_, :], sm[:, TMP, :], AF.Exp, [sm.b], [sm.b])
            S.act(sm[:, DEC, :], pa[:, 64:128], AF.Exp, [pa.b], [sm.b])
            S.tt("dve", sm[:, TMP2, :], sm[:, A_C, :], sm[:, AOFF, :], ALU.add, [sm.b], [sm.b])
            S.act(etot[:, c, :], sm[:, TMP2, :], AF.Exp, [sm.b], [etot.b])
            S.tt("dve", sm[:, AOFF, :], sm[:, AOFF, :], pa[:, 64:128], ALU.add, [sm.b, pa.b], [sm.b])
            x3 = X[:].rearrange("p (h d) -> p h d", h=64)
            S.tt("dve", xdt[:].rearrange("p (h d) -> p h d", h=64), x3, bc3(DT[:], 64), ALU.mult, [X.b, DT.b], [xdt.b])
            S.tt("dve", xw[:].rearrange("p (h d) -> p h d", h=64), xdt[:].rearrange("p (h d) -> p h d", h=64),
                 bc3(sm[:, W_, :], 64), ALU.mult, [xdt.b, sm.b], [xw.b])
            S.copy("act", btm[:], BC[:, 0:1024], [BC.b], [btm.b])
            ct = CT[c % 2]

            def front(g, X=X, BC=BC, ct=ct):
                pt = psum[1]
                S.tr(pt[:, 0:128], BC[:, g * 128:(g + 1) * 128], ident_f, [BC.b, cst_t.b], [pt.b])
                S.tr(pt[:, 128:256], BC[:, 1024 + g * 128:1024 + (g + 1) * 128], ident_f, [BC.b, cst_t.b], [pt.b])
                S.copy("act", BT[:, g, :], pt[:, 0:128], [pt.b], [BT.b])
                S.copy("act", ct[:, g, :], pt[:, 128:256], [pt.b], [ct.b])
                pcb = psum[1]
                S.mm(pcb[:, 256:384], BT[:, g, :], ct[:, g, :], True, True, [BT.b, ct.b], [pcb.b])
                cb_ = cbm[g % 2]
                S.tt("dve", cb_[:], pcb[:, 256:384], U_f, ALU.mult, [pcb.b, cst_t.b], [cb_.b])
                pyo = psum[3]
                S.mm(pyo[:], ct[:, g, :], st_b[:, g * 512:(g + 1) * 512], True, True, [ct.b, st_b.b], [pyo.b])
                y1 = yo[g % 2]
                S.copy("act", y1[:], pyo[:], [pyo.b], [y1.b])
                Zg, Ew, Mw = Zw[g % 2], Eww[g % 2], Mww[g % 2]
                S.tt("dve", Zg[:], U_f.unsqueeze(1).broadcast_to([128, 8, 128]), bc3(sm[:, DTA, g * 8:(g + 1) * 8], 128), ALU.mult,
                     [cst_t.b, sm.b], [Zg.b])
                pab0, pab1 = psum[4], psum[5]
                S.mm(pab0[:], ones_f, Zg[:, 0:4, :], True, True, [cst_t.b, Zg.b], [pab0.b])
                S.mm(pab1[:], ones_f, Zg[:, 4:8, :], True, True, [cst_t.b, Zg.b], [pab1.b])
                S.copy("act", Ew[:, 0:512], pab0[:], [pab0.b], [Ew.b])
                S.copy("act", Ew[:, 512:1024], pab1[:], [pab1.b], [Ew.b])
                E3 = Ew[:].rearrange("p (h l) -> p h l", h=8)
                S.tt("dve", E3, E3, bc3(sm[:, NEGA, g * 8:(g + 1) * 8], 128), ALU.add, [Ew.b, sm.b], [Ew.b])
                S.act(Ew[:], Ew[:], AF.Exp, [Ew.b], [Ew.b])
                S.stt(Mw[:], E3, 1.0, cb_[:].unsqueeze(1).broadcast_to([128, 8, 128]), ALU.min, ALU.mult, [Ew.b, cb_.b], [Mw.b])
                psu = psum[7] if g % 2 else psum[2]
                S.mm(psu[:], btm[:, g * 128:(g + 1) * 128], xw[:, g * 512:(g + 1) * 512], True, True, [btm.b, xw.b], [psu.b])

            def back(g, X=X, r=r):
                pyd = psum[6]
                Mw = Mww[g % 2]
                for hh in range(8):
                    h = g * 8 + hh
                    S.mm(pyd[:, hh * 64:(hh + 1) * 64], Mw[:, hh, :], xdt[:, h * 64:(h + 1) * 64], True, True, [Mw.b, xdt.b], [pyd.b])
                y1, d1 = yo[g % 2], dx[g % 2]
                gs = slice(g * 512, (g + 1) * 512)
                S.tt("dve", y1[:].rearrange("p (h d) -> p h d", h=8), y1[:].rearrange("p (h d) -> p h d", h=8),
                     bc3(sm[:, EA, g * 8:(g + 1) * 8], 64), ALU.mult, [y1.b, sm.b], [y1.b])
                S.tt("pool", d1[:].rearrange("p (h d) -> p h d", h=8), X[:, gs].rearrange("p (h d) -> p h d", h=8),
                     bc3(dtb_t[:, 2, g * 8:(g + 1) * 8], 64), ALU.mult, [X.b, dtb_t.b], [d1.b])
                S.tt("pool", d1[:], d1[:], y1[:], ALU.add, [d1.b, y1.b], [d1.b])
                S.tt("dve", d1[:], d1[:], pyd[:], ALU.add, [d1.b, pyd.b], [d1.b])
                S.dma("sp", d1, ypre_s[r, gs], d1[:], [d1.b], [])
                psu = psum[7] if g % 2 else psum[2]
                S.tt("dve", st_f[:, gs].rearrange("p (h d) -> p h d", h=8), st_f[:, gs].rearrange("p (h d) -> p h d", h=8),
                     bc3(sm[:, DEC, g * 8:(g + 1) * 8], 64), ALU.mult, [st_f.b, sm.b], [st_f.b])
                S.tt("dve", st_f[:, gs], st_f[:, gs], psu[:], ALU.add, [st_f.b, psu.b], [st_f.b])
                S.copy("act", st_b[:, gs], st_f[:, gs], [st_f.b], [st_b.b])

            front(0)
            for g in range(8):
                if g + 1 < 8:
                    front(g + 1)
                back(g)
            S.dma("sp", ct, ct_s[c * 1024:(c + 1) * 1024, :].rearrange("(g n) l -> n g l", g=8), ct[:], [ct.b], [])
        S.dma("sp", st_f, st_in, st_f[:], [st_f.b], [])
        S.barrier()
        collective(st_in, st_g)
        S.barrier()
        S.dma("sp", st_f, st_f[:], st_g[0:128, :], [], [st_f.b])
        S.ts("dve", st_b[:], st_f[:], flg_t[:, 0:1], None, ALU.mult, None, [st_f.b, flg_t.b], [st_b.b])
        nw = xc[0]
        S.dma("sp", nw, nw[:], normw.broadcast_to([128, 4096]), [], [nw.b])
        YP = xc[1]
        zt = [xdt, xw]
        ssm_f = sb("s_ssmf", [128, 4096], F32)
        ssb = [sb("s_ssb%d" % i, [128, 4, 128], BF16) for i in range(2)]
        yg = [sb("s_yg%d" % i, [128, 512], F32) for i in range(8)]
        dg = [sb("s_dg%d" % i, [128, 512], F32) for i in range(8)]
        sqs = sb("s_sqs", [128, 8, 4], F32)
        k = 0
        for c in range(NT):
            r = slice(c * 128, (c + 1) * 128)
            ct = CT[c % 2]
            Zc = zt[c % 2]
            S.dma("sp", ct, ct[:], ct_s[c * 1024:(c + 1) * 1024, :].rearrange("(g n) l -> n g l", g=8), [], [ct.b])
            S.dma("sp", YP, YP[:], ypre_s[r, :], [], [YP.b])
            S.dma("sp", Zc, Zc[:], z_s[r, :], [], [Zc.b])
            G = range(8)
            gsl = [slice(g * 512, (g + 1) * 512) for g in G]
            for g in G:
                S.mm(psum[g][:], ct[:, g, :], st_b[:, gsl[g]], True, True, [ct.b, st_b.b], [psum[g].b])
            for g in G:
                S.copy("act", yg[g][:], psum[g][:], [psum[g].b], [yg[g].b])
            for g in G:
                y1 = yg[g]
                S.tt("dve", y1[:].rearrange("p (h d) -> p h d", h=8), y1[:].rearrange("p (h d) -> p h d", h=8),
                     bc3(etot[:, c, g * 8:(g + 1) * 8], 64), ALU.mult, [y1.b, etot.b], [y1.b])
                S.tt("dve", y1[:], y1[:], YP[:, gsl[g]], ALU.add, [y1.b, YP.b], [y1.b])
            for g in G:
                S.act(dg[g][:], Zc[:, gsl[g]], AF.Silu, [Zc.b], [dg[g].b])
            for g in G:
                S.tt("pool", yg[g][:], yg[g][:], dg[g][:], ALU.mult, [yg[g].b, dg[g].b], [yg[g].b])
            for g in G:
                S.act(dg[g][:], yg[g][:], AF.Square, [yg[g].b], [dg[g].b, sqs.b], accum_out=sqs[:, g, 0:1])
            S.ts("dve", sqs[:, :, 1], sqs[:, :, 0], 1.0 / 512.0, 1e-5, ALU.mult, ALU.add, [sqs.b], [sqs.b])
            S.act(sqs[:, :, 2], sqs[:, :, 1], AF.Sqrt, [sqs.b], [sqs.b])
            S.op("dve", lambda e: e.reciprocal(sqs[:, :, 3], sqs[:, :, 2]), [sqs.b], [sqs.b])
            for g in G:
                S.stt(ssm_f[:, gsl[g]], yg[g][:], sqs[:, g, 3:4], nw[:, gsl[g]], ALU.mult, ALU.mult, [yg[g].b, sqs.b, nw.b], [ssm_f.b])
            for kc0 in range(0, 32, 4):
                ps = psum[2 + k % 2]
                sbt = ssb[k % 2]
                k += 1
                for j in range(4):
                    S.tr(ps[:, j * 128:(j + 1) * 128], ssm_f[:, (kc0 + j) * 128:(kc0 + j + 1) * 128], ident_f, [ssm_f.b, cst_t.b], [ps.b])
                S.copy("act" if k % 2 else "dve", sbt[:], ps[:].rearrange("p (j t) -> p j t", j=4), [ps.b], [sbt.b])
                S.dma("sp", sbt, ssmT_s[kc0 * 128:(kc0 + 4) * 128, r].rearrange("(j p) t -> p j t", p=128), sbt[:], [sbt.b], [])

    E_small = [sb("e_small%d" % i, [128, 8], F32) for i in range(2)]
    with ExitStack() as es:
        ES[0] = es
        phase3()
        end_phase()
    ES[0] = None
    if "stop3" in debug:
        S.emit()
        return nc

    def phase4():
        kT = sb("a_kT", [128, 4, 4096], BF16)
        V = sb("a_V", [128, 32, 512], BF16)
        kiT2 = sb("a_kiT", [128, 4096], BF16)
        kst = [sb("a_kst%d" % i, [128, 1088], F32) for i in range(2)]
        kid = [sb("a_kid%d" % i, [128, 128], F32) for i in range(2)]
        wi_t = sb("a_wi", [128, NT, 16], F32)
        S.dma("sp", wi_t, wi_t[:], wi_s.rearrange("(n p) h -> p n h", p=128), [], [wi_t.b])
        k = 0
        for b in range(32):
            ks, kd = kst[b % 2], kid[b % 2]
            src = (kvg[b // 2].rearrange("p (n c) -> (p n) c", n=2)[(b % 2) * 128:(b % 2) * 128 + 128, :] if b < 16
                   else kvg_in[(b - 16) * 128:(b - 15) * 128, :])
            S.dma("sp", ks, ks[:], src, [], [ks.b])
            S.copy("dve", kd[:, 0:64], ks[:, 1024:1088], [ks.b], [kd.b])
            S.copy("dve", kd[:, 64:128], ks[:, 1024:1088], [ks.b], [kd.b])
            S.copy("act", V[:, b, :], ks[:, 512:1024], [ks.b], [V.b])
            ps = psum[k % 2]
            k += 1
            for j in range(4):
                S.tr(ps[:, j * 128:(j + 1) * 128], ks[:, j * 128:(j + 1) * 128], ident_f, [ks.b, cst_t.b], [ps.b])
            S.copy("act", kT[:, :, b * 128:(b + 1) * 128], ps[:].rearrange("p (j t) -> p j t", j=4), [ps.b], [kT.b])
            ps2 = psum[2 + k % 2]
            S.tr(ps2[:, 0:128], kd[:], ident_f, [kd.b, cst_t.b], [ps2.b])
            S.copy("dve", kiT2[:, b * 128:(b + 1) * 128], ps2[:, 0:128], [ps2.b], [kiT2.b])

        qf = sb("a_qf", [128, 2048], F32)
        qif = sb("a_qif", [128, 1024], F32)
        qTs = [sb("a_qT%d" % i, [128, 16, 128], BF16) for i in range(2)]
        qiT = sb("a_qiT", [128, 8, 128], BF16)
        score = sb("a_score", [128, 4096], F32)
        maskf = sb("a_maskf", [128, 4096], F32)
        maskTs = [sb("a_maskT%d" % i, [128, 32, 128], BF16) for i in range(2)]
        rl = [sb("a_rl%d" % i, [128, 512], F32) for i in range(2)]
        pt = [sb("a_p%d" % i, [128, 512], BF16) for i in range(3)]
        pm = [sb("a_pm%d" % i, [128, 512], BF16) for i in range(3)]
        bs = sb("a_bs", [128, 8], F32)
        rinv = sb("a_rinv", [128, 512], F32)
        ost = [sb("a_ost%d" % i, [128, 512], BF16) for i in range(2)]
        LO, HI, MID, CNT, GC, W0 = range(6)
        SCALE = 128.0 ** -0.5
        kk = [k]

        def prep_index_bisect(i):
            r = slice(i * 128, (i + 1) * 128)
            qT = qTs[i % 2]
            S.dma("sp", qf, qf[:], q_s[r, :], [], [qf.b])
            S.dma("sp", qif, qif[:], qi_s[r, :], [], [qif.b])
            for kc0 in range(0, 16, 4):
                ps = psum[kk[0] % 2]
                kk[0] += 1
                for j in range(4):
                    S.tr(ps[:, j * 128:(j + 1) * 128], qf[:, (kc0 + j) * 128:(kc0 + j + 1) * 128], ident_f, [qf.b, cst_t.b], [ps.b])
                S.copy("act", qT[:, kc0:kc0 + 4, :], ps[:].rearrange("p (j t) -> p j t", j=4), [ps.b], [qT.b])
            for kc0 in range(0, 8, 4):
                ps = psum[kk[0] % 2]
                kk[0] += 1
                for j in range(4):
                    S.tr(ps[:, j * 128:(j + 1) * 128], qif[:, (kc0 + j) * 128:(kc0 + j + 1) * 128], ident_f, [qif.b, cst_t.b], [ps.b])
                S.copy("act", qiT[:, kc0:kc0 + 4, :], ps[:].rearrange("p (j t) -> p j t", j=4), [ps.b], [qiT.b])
            NB = 16 + i + 1
            ncols = NB * 128
            for c0 in range(0, ncols, 512):
                wd = min(512, ncols - c0)
                for h in range(16):
                    pl = psum[2 + h % 2]
                    p0 = (h % 2) * 64
                    S.mm(pl[:, 0:wd], qiT[p0:p0 + 64, h // 2, :], kiT2[p0:p0 + 64, c0:c0 + wd], True, True, [qiT.b, kiT2.b], [pl.b])
                    rr = rl[h % 2]
                    S.act(rr[:, 0:wd], pl[:, 0:wd], AF.Relu, [pl.b], [rr.b])
                    if h == 0:
                        S.ts("dve", score[:, c0:c0 + wd], rr[:, 0:wd], wi_t[:, i, 0:1], None, ALU.mult, None, [rr.b, wi_t.b], [score.b])
                    else:
                        S.stt(score[:, c0:c0 + wd], rr[:, 0:wd], wi_t[:, i, h:h + 1], score[:, c0:c0 + wd], ALU.mult, ALU.add,
                              [rr.b, wi_t.b, score.b], [score.b])
            S.op("dve", lambda e, n=ncols: e.tensor_reduce(out=bs[:, HI:HI + 1], in_=score[:, 0:n], axis=AX.X, op=ALU.max), [score.b], [bs.b])
            S.op("dve", lambda e, n=ncols: e.tensor_reduce(out=bs[:, LO:LO + 1], in_=score[:, 0:n], axis=AX.X, op=ALU.min), [score.b], [bs.b])
            S.stt(bs[:, W0:W0 + 1], bs[:, HI:HI + 1], 1.0, bs[:, LO:LO + 1], ALU.add, ALU.subtract, [bs.b], [bs.b])
            S.ts("dve", score[:, 0:2048], score[:, 0:2048], flg_t[:, 0:1], flg_t[:, 1:2], ALU.mult, ALU.add, [score.b, flg_t.b], [score.b])
            dc = 2048 + i * 128
            S.tt("dve", score[:, dc:dc + 128], score[:, dc:dc + 128], cmask_f, ALU.add, [score.b, cst_t.b], [score.b])
            for it in range(N_BISECT):
                if it in (0, 5, 10, 14):
                    yield
                ck = 2.0 ** -(it + 1)
                S.stt(bs[:, MID:MID + 1], bs[:, W0:W0 + 1], ck, bs[:, LO:LO + 1], ALU.mult, ALU.add, [bs.b], [bs.b])
                S.ts("dve", maskf[:, 0:ncols], score[:, 0:ncols], bs[:, MID:MID + 1], 0.0, ALU.is_ge, ALU.add, [score.b, bs.b], [maskf.b, bs.b],
                     accum_out=bs[:, CNT:CNT + 1])
                S.ts("dve", bs[:, GC:GC + 1], bs[:, CNT:CNT + 1], 256.0, ck, ALU.is_ge, ALU.mult, [bs.b], [bs.b])
                S.stt(bs[:, LO:LO + 1], bs[:, GC:GC + 1], bs[:, W0:W0 + 1], bs[:, LO:LO + 1], ALU.mult, ALU.add, [bs.b], [bs.b])
            S.ts("dve", maskf[:, 0:ncols], score[:, 0:ncols], bs[:, LO:LO + 1], None, ALU.is_ge, None, [score.b, bs.b], [maskf.b])

        def mask_transpose(i):
            NB = 16 + i + 1
            maskT = maskTs[i % 2]
            for b0 in range(0, NB, 4):
                nb4 = min(4, NB - b0)
                ps = psum[kk[0] % 2]
                kk[0] += 1
                for j in range(nb4):
                    S.tr(ps[:, j * 128:(j + 1) * 128], maskf[:, (b0 + j) * 128:(b0 + j + 1) * 128], ident_f, [maskf.b, cst_t.b], [ps.b])
                S.copy("act", maskT[:, b0:b0 + nb4, :], ps[:, 0:nb4 * 128].rearrange("p (j t) -> p j t", j=nb4), [ps.b], [maskT.b])

        def attend(i):
            r = slice(i * 128, (i + 1) * 128)
            NB = 16 + i + 1
            qT, maskT = qTs[i % 2], maskTs[i % 2]
            for kv in range(4):
                pot, prs = psum[4 + kv % 2], psum[6 + kv % 2]

                def qk(b, kv=kv):
                    psc = psum[2 + b % 2]
                    S.mm(psc[:], kT[:, kv, b * 128:(b + 1) * 128], qT[:, 4 * kv:4 * kv + 4, :], True, True, [kT.b, qT.b], [psc.b])
                    p_, pm_ = pt[b % 3], pm[b % 3]
                    S.act(p_[:], psc[:], AF.Exp, [psc.b], [p_.b], scale=SCALE)
                    S.tt("pool", pm_[:].rearrange("p (h t) -> p h t", h=4), p_[:].rearrange("p (h t) -> p h t", h=4),
                         maskT[:, b, :].unsqueeze(1).broadcast_to([128, 4, 128]), ALU.mult, [p_.b, maskT.b], [pm_.b])

                def pv(b, kv=kv, pot=pot, prs=prs):
                    pm_ = pm[b % 3]
                    S.mm(pot[:], V[:, b, kv * 128:(kv + 1) * 128], pm_[:], b == 0, b == NB - 1, [V.b, pm_.b], [pot.b])
                    S.mm(prs[:], onesb[:], pm_[:], b == 0, b == NB - 1, [onesb.b, pm_.b], [prs.b])

                LA = 2
                for b in range(min(LA, NB)):
                    qk(b)
                for b in range(NB):
                    if b + LA < NB:
                        qk(b + LA)
                    pv(b)
                S.op("dve", lambda e, prs=prs: e.reciprocal(rinv[:], prs[:]), [prs.b], [rinv.b])
                o_ = ost[kv % 2]
                S.tt("dve", o_[:], pot[:], rinv[:], ALU.mult, [pot.b, rinv.b], [o_.b])
                S.dma("sp", o_, attnT_s[kv * 512:(kv + 1) * 512, r].rearrange("(h d) t -> d h t", h=4),
                      o_[:].rearrange("p (h t) -> p h t", h=4), [o_.b], [])
                yield

        for _ in prep_index_bisect(0):
            pass
        mask_transpose(0)
        for i in range(NT):
            gen_b = prep_index_bisect(i + 1) if i + 1 < NT else iter(())
            gen_a = attend(i)
            next(gen_b, None)
            for _ in range(4):
                next(gen_b, None)
                next(gen_a, None)
            for _ in gen_b:
                pass
            for _ in gen_a:
                pass
            if i + 1 < NT:
                mask_transpose(i + 1)

    with ExitStack() as es:
        ES[0] = es
        phase4()
        end_phase()
    ES[0] = None
    if "stop4" in debug:
        S.emit()
        return nc

    def phase5():
        alloc_dense("p5")
        E.p5 = True
        mT = sb("mT", [128, 16, 512], BF16)
        gt = [sb("gt%d" % i, [128, 2, 512], BF16) for i in range(2)]
        for blk in range(4):
            t0 = blk * 512
            begin_block(blk)
            ts_ = slice(t0, t0 + 512)
            S.dma("sp", E.xT, E.xT[:], attnT_s[:, ts_].rearrange("(k p) t -> p k t", p=128), [], [E.xT.b])
            S.dma("sp", E.actT, E.actT[:, 0:32, :], ssmT_s[:, ts_].rearrange("(k p) t -> p k t", p=128), [], [E.actT.b])
            for tt in range(4):
                S.dma("sp", E.xt[tt], E.xt[tt][:], h1_s[t0 + tt * 128:t0 + (tt + 1) * 128, :], [], [E.xt[tt].b])
            for sl in range(4):
                sa, s1, s2 = next_wsl(), next_wsl(), next_wsl()
                load_w(sa, w_oa, 0, 16, sl * 512, 512)
                load_w(s1, w_os, 0, 16, sl * 512, 512)
                load_w(s2, w_os, 2048, 16, sl * 512, 512)
                flush_w()
                for m in range(4):
                    dch = sl * 4 + m
                    g_ = gt[dch % 2]
                    S.dma("sp", g_, g_[:, 0, :], gT_s[dch * 128:(dch + 1) * 128, ts_], [], [g_.b])
                    S.dma("sp", g_, g_[:, 1, :], gT_s[2048 + dch * 128:2048 + (dch + 1) * 128, ts_], [], [g_.b])
                    pa, pss = psum[2 + m % 2], psum[4 + m % 2]
                    ms = slice(m * 128, (m + 1) * 128)
                    for kc in range(16):
                        S.mm(pa[:], sa[:, kc, ms], E.xT[:, kc, :], kc == 0, kc == 15, [sa.b, E.xT.b], [pa.b])
                    for kc in range(32):
                        w_ = s1 if kc < 16 else s2
                        S.mm(pss[:], w_[:, kc % 16, ms], E.actT[:, kc, :], kc == 0, kc == 31, [w_.b, E.actT.b], [pss.b])
                    a1, a2 = E.sg[0], E.sg[1]
                    S.tt("dve", a1[:], pa[:], g_[:, 0, :], ALU.mult, [pa.b, g_.b], [a1.b])
                    S.tt("dve", a2[:], pss[:], g_[:, 1, :], ALU.mult, [pss.b, g_.b], [a2.b])
                    S.tt("pool", mT[:, dch, :], a1[:], a2[:], ALU.add, [a1.b, a2.b], [mT.b])
            for nb in range(4):
                def epi(tt, ps, nb=nb):
                    cs = slice(nb * 512, (nb + 1) * 512)
                    S.stt(E.xt[tt][:, cs], E.xt[tt][:, cs], ALPHA, ps[:], ALU.mult, ALU.add, [E.xt[tt].b, ps.b], [E.xt[tt].b])
                linear_tm(mT, 16, w_out, 0, nb * 512, 512, epi, bank0=6)
            load_ln(2)
            for tt in range(4):
                layer_norm_tile(E.xt[tt], E.lng, E.lnb, E.xt[tt])
            load_ln(4)
            ffn_block(E.xt, w_gu2, w_dn2, E.xt)
            for tt in range(4):
                S.dma("sp", E.xt[tt], out[t0 + tt * 128:t0 + (tt + 1) * 128, :], E.xt[tt][:], [E.xt[tt].b], [])

    with ExitStack() as es:
        ES[0] = es
        phase5()
        end_phase()
    ES[0] = None

    S.emit()
    return nc


def _rope_tables(pos):
    def tab(rot):
        inv = (1.0 / (np.float32(500000.0) ** (np.arange(0, rot, 2, dtype=np.float32) / np.float32(rot)))).astype(np.float32)
        ang = pos.astype(np.float32)[:, None] * inv[None, :]
        return np.cos(ang).astype(np.float32), np.sin(ang).astype(np.float32)
    c, s = tab(32)
    ci, si = tab(16)
    return np.concatenate([c, s, ci, si], axis=1).astype(np.float32)


def make_in_maps(inp):
    f = lambda a: np.ascontiguousarray(np.asarray(a, dtype=np.float32))
    lnp = np.stack([f(inp[k])[0] for k in ("ln1_g", "ln1_b", "ln2_g", "ln2_b", "ln3_g", "ln3_b")], 0)
    conv_wb = np.concatenate([f(inp["conv_w"])[0], f(inp["conv_b"])], 0)
    ssmv = np.concatenate([f(inp["dt_bias"]), f(inp["A_log"]), f(inp["D_skip"]), np.zeros((1, 64), np.float32)], 0)
    ii = np.arange(128)
    cst = np.zeros((128, 5, 128), np.float32)
    cst[:, 0, :] = np.eye(128)
    cst[:, 1, :] = (ii[:, None] <= ii[None, :])
    cst[:, 2, :] = 1.0
    cst[:, 3, :] = np.where(ii[None, :] <= ii[:, None], 0.0, NEG)
    shared = {
        "ffn1_w_gu": f(inp["ffn1_w_gu"])[0], "ffn1_w_down": f(inp["ffn1_w_down"])[0],
        "ffn2_w_gu": f(inp["ffn2_w_gu"])[0], "ffn2_w_down": f(inp["ffn2_w_down"])[0],
        "w_in": f(inp["w_in"])[0], "w_o_attn": f(inp["w_o_attn"])[0], "w_o_ssm": f(inp["w_o_ssm"])[0],
        "w_out": f(inp["w_out"])[0], "lnp": lnp, "conv_wb": conv_wb, "ssmv": ssmv,
        "normw": f(inp["ssm_norm_w"]), "cst": cst,
        "convT": np.ascontiguousarray(conv_wb.reshape(5, 48, 128).transpose(2, 1, 0)),
    }
    x = f(inp["x"])
    maps = []
    for c in range(8):
        b, j = c // 2, c % 2
        m = dict(shared)
        m["x"] = np.ascontiguousarray(x[b, j * T:(j + 1) * T, :])
        m["ropet"] = _rope_tables(np.arange(j * T, (j + 1) * T))
        fl = np.zeros((128, 2), np.float32)
        fl[:, 0] = float(j)
        fl[:, 1] = (float(j) - 1.0) * 1.0e30
        m["flg"] = fl
        maps.append(m)
    return maps


def kernel(**inputs):
    import os
    dbg = tuple(d for d in os.environ.get("KDEBUG", "").split(",") if d)
    nc = build_nc(debug=dbg)
    in_maps = make_in_maps(inputs)
    res = run_bass_kernel_spmd(nc, in_maps, core_ids=list(range(8)))
    outp = np.zeros((4, 4096, D), np.float32)
    for c in range(8):
        b, j = c // 2, c % 2
        outp[b, j * T:(j + 1) * T, :] = res.results[c]["out"]
    return outp
```
